# Optimizing a Trainium2 kernel written in Bass

```python
import math
import jax, jax.numpy as jnp
from jax import lax
import numpy as np

D_MODEL = 2048
BATCH = 4
SEQ = 2048
DEPTH = 2

GRID_W = 64
CTX_LEN = 256
N_MIXERS = 2

RET_HEADS = 8
RET_DK = D_MODEL // RET_HEADS
RET_DV = 2 * D_MODEL // RET_HEADS
RET_CHUNK = 128
RET_IN = 2 * RET_HEADS * RET_DK + 2 * RET_HEADS * RET_DV
ROPE_BASE = 10000.0

GDN_DK = 128
GDN_DV = 128
GDN_QK_HEADS = D_MODEL // GDN_DK
GDN_V_HEADS = 2 * GDN_QK_HEADS
GDN_CONV = 5
GDN_CHUNK = 64
GDN_QKV = 2 * GDN_QK_HEADS * GDN_DK + GDN_V_HEADS * GDN_DV
GDN_IN = GDN_QKV + GDN_V_HEADS * GDN_DV + 4 * GDN_V_HEADS

D_FF = ((8 * D_MODEL // 3 + 255) // 256) * 256

DEEPNORM_ALPHA = (2 * DEPTH) ** 0.25
DEEPNORM_BETA = (8 * DEPTH) ** -0.25
LN_EPS = 1e-5
N_RET_LAYERS = (DEPTH + 1) // 2
N_GDN_LAYERS = DEPTH // 2

kernel_name = 'hybrid_retention_gdn_dit'


def _split_cols(t, widths):
    cuts = [int(s) for s in np.cumsum(widths)[:-1]]
    return jnp.split(t, cuts, axis=-1)


def _layer_norm(x, g, b):
    xf = x.astype(jnp.float32)
    mu = xf.mean(-1, keepdims=True)
    var = jnp.square(xf - mu).mean(-1, keepdims=True)
    return ((xf - mu) * lax.rsqrt(var + LN_EPS)).astype(x.dtype) * g + b


def _seg_flip(t, n_ctx):
    return jnp.concatenate([jnp.flip(t[:, :n_ctx], 1), jnp.flip(t[:, n_ctx:], 1)], axis=1)


def _axial_rope(rows, dim):
    row = jnp.repeat(jnp.arange(rows, dtype=jnp.float32), GRID_W)
    col = jnp.tile(jnp.arange(GRID_W, dtype=jnp.float32), rows)
    n_freq = dim // 4
    inv_freq = ROPE_BASE ** (-jnp.arange(n_freq, dtype=jnp.float32) / n_freq)
    ang = jnp.concatenate([row[:, None] * inv_freq, col[:, None] * inv_freq], axis=-1)
    return jnp.cos(ang), jnp.sin(ang)


def _apply_rope(x, cos, sin):
    half = x.shape[-1] // 2
    x1, x2 = x[..., :half], x[..., half:]
    cs = cos[None, :, None, :].astype(x.dtype)
    sn = sin[None, :, None, :].astype(x.dtype)
    return jnp.concatenate([x1 * cs - x2 * sn, x1 * sn + x2 * cs], axis=-1)


def _to_chunks(t, chunk):
    b, l, h, d = t.shape
    return t.reshape(b, l // chunk, chunk, h, d).transpose(1, 0, 3, 2, 4)


def _from_chunks(t):
    n, b, h, c, d = t.shape
    return t.transpose(1, 0, 3, 2, 4).reshape(b, n * c, h, d)


def _retention_scan(q, k, v, log_gamma, strict):
    bsz, _, h, dk = q.shape
    dv = v.shape[-1]
    C = RET_CHUNK
    qc = _to_chunks(q.astype(jnp.float32), C)
    kc = _to_chunks(k.astype(jnp.float32), C)
    vc = _to_chunks(v.astype(jnp.float32), C)
    idx = jnp.arange(C, dtype=jnp.float32)
    diff = idx[:, None] - idx[None, :]
    mask = diff > 0 if strict else diff >= 0
    lg = log_gamma[:, None, None]
    intra = jnp.where(mask, jnp.exp(lg * jnp.where(mask, diff, 0.0)), 0.0)
    q_decay = jnp.exp(log_gamma[:, None] * (idx + 1.0))[..., None]
    k_decay = jnp.exp(log_gamma[:, None] * (C - 1.0 - idx))[..., None]
    chunk_decay = jnp.exp(log_gamma * C)[:, None, None]

    def step(state, inp):
        qi, ki, vi = inp
        s = jnp.einsum('bhid,bhjd->bhij', qi, ki) * intra
        o = jnp.einsum('bhij,bhjv->bhiv', s, vi) + jnp.einsum('bhid,bhdv->bhiv', qi * q_decay, state)
        state = chunk_decay * state + jnp.einsum('bhjd,bhjv->bhdv', ki * k_decay, vi)
        return state, o

    state0 = jnp.zeros((bsz, h, dk, dv), jnp.float32)
    _, o = lax.scan(step, state0, (qc, kc, vc))
    return _from_chunks(o)


def _gdn_scan(q, k, v, beta, log_alpha):
    bsz, l, h, dk = q.shape
    dv = v.shape[-1]
    C = GDN_CHUNK
    n = l // C
    qc, kc, vc = _to_chunks(q, C), _to_chunks(k, C), _to_chunks(v, C)
    bc = beta.reshape(bsz, n, C, h).transpose(1, 0, 3, 2)
    gc = jnp.cumsum(log_alpha.reshape(bsz, n, C, h).transpose(1, 0, 3, 2), axis=-1)
    idx = jnp.arange(C)
    incl = idx[:, None] >= idx[None, :]
    strict = idx[:, None] > idx[None, :]
    decay = jnp.exp(jnp.where(incl, gc[..., :, None] - gc[..., None, :], -jnp.inf))
    kk = jnp.einsum('nbhid,nbhjd->nbhij', kc, kc)
    a_mat = jnp.where(strict, bc[..., :, None] * kk * decay, 0.0)
    eye = jnp.eye(C, dtype=jnp.float32)
    rhs = jnp.concatenate([vc * bc[..., None], kc * (bc * jnp.exp(gc))[..., None]], axis=-1)
    sol = lax.linalg.triangular_solve(eye + a_mat, rhs, left_side=True, lower=True, unit_diagonal=True)
    u0, w = sol[..., :dv], sol[..., dv:]
    qk = jnp.einsum('nbhid,nbhjd->nbhij', qc, kc) * decay
    q_dec = qc * jnp.exp(gc)[..., None]
    k_dec = kc * jnp.exp(gc[..., -1:] - gc)[..., None]
    chunk_dec = jnp.exp(gc[..., -1])[..., None, None]

    def step(m, inp):
        u0_i, w_i, qk_i, qd_i, kd_i, cd_i = inp
        u = u0_i - jnp.einsum('bhcd,bhdv->bhcv', w_i, m)
        o = jnp.einsum('bhcd,bhdv->bhcv', qd_i, m) + jnp.einsum('bhij,bhjv->bhiv', qk_i, u)
        m = cd_i * m + jnp.einsum('bhcd,bhcv->bhdv', kd_i, u)
        return m, o

    m0 = jnp.zeros((bsz, h, dk, dv), jnp.float32)
    _, o = lax.scan(step, m0, (u0, w, qk, q_dec, k_dec, chunk_dec))
    return _from_chunks(o)


def _centred_conv_silu(x, w):
    pad = GDN_CONV // 2
    y = lax.conv_general_dilated(x, w[:, None, :].astype(x.dtype), window_strides=(1,), padding=[(pad, pad)],
                                 dimension_numbers=('NWC', 'WIO', 'NWC'), feature_group_count=x.shape[-1])
    return jax.nn.silu(y)


def _l2norm(t):
    tf = t.astype(jnp.float32)
    return tf * lax.rsqrt(jnp.sum(tf * tf, axis=-1, keepdims=True) + 1e-6)


def retention_mixer(u_ctx, u_lat, w_in, decay_raw, w_out, cos, sin, need_ctx):
    n_ctx = u_ctx.shape[1]
    u = jnp.concatenate([u_ctx, u_lat], axis=1)
    bsz, n_tot, _ = u.shape
    q, k, v, g = _split_cols(u @ w_in, [RET_HEADS * RET_DK, RET_HEADS * RET_DK,
                                        RET_HEADS * RET_DV, RET_HEADS * RET_DV])
    q = q.reshape(bsz, n_tot, RET_HEADS, RET_DK)
    k = k.reshape(bsz, n_tot, RET_HEADS, RET_DK)
    v = v.reshape(bsz, n_tot, RET_HEADS, RET_DV)
    q = jnp.concatenate([q[:, :n_ctx], _apply_rope(q[:, n_ctx:], cos, sin)], axis=1) * RET_DK ** -0.5
    k = jnp.concatenate([k[:, :n_ctx], _apply_rope(k[:, n_ctx:], cos, sin)], axis=1)
    log_gamma = -jnp.exp(decay_raw.astype(jnp.float32))
    o_f = _retention_scan(q, k, v, log_gamma[0], strict=False)
    o_b = _seg_flip(_retention_scan(_seg_flip(q, n_ctx), _seg_flip(k, n_ctx), _seg_flip(v, n_ctx),
                                    log_gamma[1], strict=True), n_ctx)
    o = o_f + o_b
    if not need_ctx:
        o, g = o[:, n_ctx:], g[:, n_ctx:]
    mu = o.mean(-1, keepdims=True)
    var = jnp.square(o - mu).mean(-1, keepdims=True)
    o = ((o - mu) * lax.rsqrt(var + LN_EPS)).astype(u.dtype)
    o = o.reshape(bsz, -1, RET_HEADS * RET_DV) * jax.nn.silu(g)
    y = o @ w_out
    if need_ctx:
        return y[:, :n_ctx], y[:, n_ctx:]
    return None, y


def gated_deltanet_mixer(u_ctx, u_lat, w_in, conv_w, a_log, dt_bias, norm_w, w_out, need_ctx):
    n_ctx = u_ctx.shape[1]
    u = jnp.concatenate([u_ctx, u_lat], axis=1)
    bsz, n_tot, _ = u.shape
    qkv, z, b, a = _split_cols(u @ w_in, [GDN_QKV, GDN_V_HEADS * GDN_DV, 2 * GDN_V_HEADS, 2 * GDN_V_HEADS])
    qkv = jnp.concatenate([_centred_conv_silu(qkv[:, :n_ctx], conv_w),
                           _centred_conv_silu(qkv[:, n_ctx:], conv_w)], axis=1)
    q, k, v = _split_cols(qkv, [GDN_QK_HEADS * GDN_DK, GDN_QK_HEADS * GDN_DK, GDN_V_HEADS * GDN_DV])
    rep = GDN_V_HEADS // GDN_QK_HEADS
    q = jnp.repeat(_l2norm(q.reshape(bsz, n_tot, GDN_QK_HEADS, GDN_DK)), rep, axis=2) * GDN_DK ** -0.5
    k = jnp.repeat(_l2norm(k.reshape(bsz, n_tot, GDN_QK_HEADS, GDN_DK)), rep, axis=2)
    v = v.reshape(bsz, n_tot, GDN_V_HEADS, GDN_DV).astype(jnp.float32)
    beta = jax.nn.sigmoid(b.astype(jnp.float32)).reshape(bsz, n_tot, 2, GDN_V_HEADS)
    log_alpha = -jnp.exp(a_log.astype(jnp.float32)) * jax.nn.softplus(
        a.astype(jnp.float32).reshape(bsz, n_tot, 2, GDN_V_HEADS) + dt_bias.astype(jnp.float32))
    o_f = _gdn_scan(q, k, v, beta[:, :, 0], log_alpha[:, :, 0])
    o_b = _seg_flip(_gdn_scan(_seg_flip(q, n_ctx), _seg_flip(k, n_ctx), _seg_flip(v, n_ctx),
                              _seg_flip(beta[:, :, 1], n_ctx), _seg_flip(log_alpha[:, :, 1], n_ctx)), n_ctx)
    o = o_f + o_b
    if not need_ctx:
        o, z = o[:, n_ctx:], z[:, n_ctx:]
    zf = z.reshape(bsz, -1, GDN_V_HEADS, GDN_DV).astype(jnp.float32)
    o = o * lax.rsqrt(jnp.mean(o * o, axis=-1, keepdims=True) + 1e-6) * norm_w.astype(jnp.float32) * jax.nn.silu(zf)
    y = o.reshape(bsz, -1, GDN_V_HEADS * GDN_DV).astype(u.dtype) @ w_out
    if need_ctx:
        return y[:, :n_ctx], y[:, n_ctx:]
    return None, y


def _swiglu(u, w_in, w_out):
    gate, up = jnp.split(u @ w_in, 2, axis=-1)
    return (jax.nn.silu(gate) * up) @ w_out


def setup_inputs(seed: int = 0) -> dict:
    key = jax.random.key(seed)
    ks = jax.random.split(key, 20)
    f32 = jnp.float32
    D = D_MODEL

    def nrm(k, shape, scale):
        return jax.random.normal(k, shape, f32) * scale

    x = nrm(ks[0], (BATCH, SEQ, D), 1.0)
    c = nrm(ks[1], (BATCH, D), 1.0)
    ctx = nrm(ks[2], (BATCH, CTX_LEN, D), 1.0)
    c_ctx = nrm(ks[3], (D,), 1.0)
    ada_w = nrm(ks[4], (DEPTH, D, 6 * D), 0.5 * D ** -0.5)
    ada_b = nrm(ks[5], (DEPTH, 6 * D), 0.02)
    ln_g = 1.0 + nrm(ks[6], (DEPTH, 2, D), 0.02)
    ln_b = nrm(ks[7], (DEPTH, 2, D), 0.02)
    ret_w_in = nrm(ks[8], (N_RET_LAYERS, D, RET_IN), D ** -0.5)
    ret_base = -(5.0 + jnp.arange(RET_HEADS, dtype=f32)) * math.log(2.0)
    ret_decay = ret_base + nrm(ks[9], (N_RET_LAYERS, 2, RET_HEADS), 0.1)
    ret_w_out = nrm(ks[10], (N_RET_LAYERS, RET_HEADS * RET_DV, D), (RET_HEADS * RET_DV) ** -0.5 * DEEPNORM_BETA)
    gdn_w_in = nrm(ks[11], (N_GDN_LAYERS, D, GDN_IN), D ** -0.5)
    gdn_conv = nrm(ks[12], (N_GDN_LAYERS, GDN_CONV, GDN_QKV), GDN_CONV ** -0.5)
    gdn_a_log = jnp.log(jax.random.uniform(ks[13], (N_GDN_LAYERS, 2, GDN_V_HEADS), f32, 1.0, 16.0))
    dt = jnp.exp(jax.random.uniform(ks[14], (N_GDN_LAYERS, 2, GDN_V_HEADS), f32, math.log(1e-3), math.log(1e-1)))
    gdn_dt_bias = dt + jnp.log(-jnp.expm1(-dt))
    gdn_norm = 1.0 + nrm(ks[15], (N_GDN_LAYERS, GDN_DV), 0.02)
    gdn_w_out = nrm(ks[16], (N_GDN_LAYERS, GDN_V_HEADS * GDN_DV, D), (GDN_V_HEADS * GDN_DV) ** -0.5 * DEEPNORM_BETA)
    ffn_w_in = nrm(ks[17], (DEPTH, D, 2 * D_FF), D ** -0.5)
    ffn_w_out = nrm(ks[18], (DEPTH, D_FF, D), D_FF ** -0.5 * DEEPNORM_BETA)
    return {'x': x, 'c': c, 'ctx': ctx, 'c_ctx': c_ctx, 'ada_w': ada_w, 'ada_b': ada_b,
            'ln_g': ln_g, 'ln_b': ln_b, 'ret_w_in': ret_w_in, 'ret_decay': ret_decay, 'ret_w_out': ret_w_out,
            'gdn_w_in': gdn_w_in, 'gdn_conv': gdn_conv, 'gdn_a_log': gdn_a_log, 'gdn_dt_bias': gdn_dt_bias,
            'gdn_norm': gdn_norm, 'gdn_w_out': gdn_w_out, 'ffn_w_in': ffn_w_in, 'ffn_w_out': ffn_w_out}


def reference(x, c, ctx, c_ctx, ada_w, ada_b, ln_g, ln_b, ret_w_in, ret_decay, ret_w_out,
              gdn_w_in, gdn_conv, gdn_a_log, gdn_dt_bias, gdn_norm, gdn_w_out, ffn_w_in, ffn_w_out):
    n_lat = x.shape[1]
    rows = n_lat // GRID_W
    cos, sin = _axial_rope(rows, RET_DK)
    h_lat, h_ctx = x, ctx
    for i in range(DEPTH):
        need_ctx = i < DEPTH - 1
        mod_l = (jax.nn.silu(c) @ ada_w[i] + ada_b[i]).reshape(-1, 6, 1, D_MODEL)
        mod_c = (jax.nn.silu(c_ctx) @ ada_w[i] + ada_b[i]).reshape(6, D_MODEL)
        u_l = h_lat * (1.0 + mod_l[:, 1]) + mod_l[:, 0]
        u_c = h_ctx * (1.0 + mod_c[1]) + mod_c[0]
        j = i // N_MIXERS
        if i % N_MIXERS == 0:
            y_c, y_l = retention_mixer(u_c, u_l, ret_w_in[j], ret_decay[j], ret_w_out[j], cos, sin, need_ctx)
        else:
            y_c, y_l = gated_deltanet_mixer(u_c, u_l, gdn_w_in[j], gdn_conv[j], gdn_a_log[j], gdn_dt_bias[j],
                                            gdn_norm[j], gdn_w_out[j], need_ctx)
        h_lat = _layer_norm(DEEPNORM_ALPHA * h_lat + mod_l[:, 2] * y_l, ln_g[i, 0], ln_b[i, 0])
        u_l = h_lat * (1.0 + mod_l[:, 4]) + mod_l[:, 3]
        h_lat = _layer_norm(DEEPNORM_ALPHA * h_lat + mod_l[:, 5] * _swiglu(u_l, ffn_w_in[i], ffn_w_out[i]),
                            ln_g[i, 1], ln_b[i, 1])
        if need_ctx:
            h_ctx = _layer_norm(DEEPNORM_ALPHA * h_ctx + mod_c[2] * y_c, ln_g[i, 0], ln_b[i, 0])
            u_c = h_ctx * (1.0 + mod_c[4]) + mod_c[3]
            h_ctx = _layer_norm(DEEPNORM_ALPHA * h_ctx + mod_c[5] * _swiglu(u_c, ffn_w_in[i], ffn_w_out[i]),
                                ln_g[i, 1], ln_b[i, 1])
    return h_lat
```

```python
import contextlib
import numpy as np
import concourse.bass as bass
import concourse.mybir as mybir
from concourse.bass_utils import run_bass_kernel_spmd

F32 = mybir.dt.float32
BF16 = mybir.dt.bfloat16
AF = mybir.ActivationFunctionType
ALU = mybir.AluOpType
AX = mybir.AxisListType

D = 2048
NCTX = 256
NLAT = 2048
NTOT = NCTX + NLAT
DFF = 5632
ALPHA = 4.0 ** 0.25
LN_EPS = 1e-5


class Res:
    __slots__ = ("w", "rd", "name", "excl")

    def __init__(self, name="", excl=False):
        self.w = []
        self.rd = {}
        self.name = name
        self.excl = excl


class Prog:
    ENG = ("pe", "act", "dve", "pool", "sp")

    def __init__(self, nc, n_slots=32):
        self.nc = nc
        self.stack = contextlib.ExitStack()
        self.sem = {}
        self.cnt = {}
        self.pending = {e: False for e in self.ENG}
        self.seen = {e: {} for e in self.ENG}
        self.streams = {e: [] for e in self.ENG}
        for e in self.ENG:
            if e == "sp":
                continue
            self.sem[e] = self.stack.enter_context(nc.semaphore("s_" + e))
            self.cnt[e] = 0
        self.n_slots = n_slots
        for i in range(n_slots):
            n = "d%d" % i
            self.sem[n] = self.stack.enter_context(nc.semaphore("s_" + n))
            self.cnt[n] = 0
        self.slot_i = 0
        self.slot_p = 0
        self.n_inst = {e: 0 for e in self.ENG}
        self._uid = 0

    def sbuf(self, shape, dtype, name=None):
        self._uid += 1
        return self.stack.enter_context(self.nc.sbuf_tensor(name or ("t%d" % self._uid), list(shape), dtype))

    def psum(self, shape, dtype, name=None):
        self._uid += 1
        return self.stack.enter_context(self.nc.psum_tensor(name or ("p%d" % self._uid), list(shape), dtype))

    def res(self, name="", excl=False):
        return Res(name, excl)

    def _need(self, eng, reads, writes):
        need = {}
        for r in reads:
            for (s, v) in r.w:
                if need.get(s, 0) < v:
                    need[s] = v
        for r in writes:
            for (s, v) in r.w:
                if need.get(s, 0) < v:
                    need[s] = v
            for s, v in r.rd.items():
                if need.get(s, 0) < v:
                    need[s] = v
        seen = self.seen[eng]
        for s, v in need.items():
            if s == "pe" and eng == "pe":
                continue
            if seen.get(s, 0) >= v:
                continue
            seen[s] = v
            sem = self.sem[s]
            self.streams[eng].append(lambda e, sem=sem, v=v: e.wait_ge(sem, v))

    def op(self, eng, fn, reads=(), writes=(), inc=True):
        if eng != "pe":
            ex = [r for r in reads if r.excl]
            if ex:
                reads = [r for r in reads if not r.excl]
                writes = list(writes) + ex
        self._need(eng, reads, writes)
        self.n_inst[eng] += 1
        if inc:
            self.cnt[eng] += 1
            val = self.cnt[eng]
            sem = self.sem[eng]
            self.streams[eng].append(lambda e, fn=fn, sem=sem: fn(e).then_inc(sem, 1))
            self.pending[eng] = False
        else:
            val = self.cnt[eng] + 1
            self.streams[eng].append(lambda e, fn=fn: fn(e))
            self.pending[eng] = True
        for r in reads:
            if r.rd.get(eng, 0) < val:
                r.rd[eng] = val
        for r in writes:
            r.w = [(eng, val)]
            r.rd = {}
        return val

    def dma(self, q, out, in_, reads=(), writes=(), **kw):
        half = self.n_slots // 2
        if q == "pool":
            s = "d%d" % (half + self.slot_p)
            self.slot_p = (self.slot_p + 1) % half
        else:
            s = "d%d" % self.slot_i
            self.slot_i = (self.slot_i + 1) % half
        self._need(q, reads, writes)
        seen = self.seen[q]
        if seen.get(s, 0) < self.cnt[s]:
            seen[s] = self.cnt[s]
            self.streams[q].append(lambda e, sem=self.sem[s], v=self.cnt[s]: e.wait_ge(sem, v))
        self.cnt[s] += 16
        val = self.cnt[s]
        sem = self.sem[s]
        self.streams[q].append(lambda e, sem=sem, out=out, in_=in_, kw=kw: e.dma_start(out=out, in_=in_, **kw).then_inc(sem, 16))
        self.n_inst[q] += 1
        for r in reads:
            if r.rd.get(s, 0) < val:
                r.rd[s] = val
        for r in writes:
            r.w = [(s, val)]
            r.rd = {}
        return (s, val)

    def finish(self):
        for e in self.ENG:
            assert not self.pending[e], "engine %s has pending un-signalled ops" % e
        for i in range(self.n_slots):
            s = "d%d" % i
            if self.cnt[s] > 0:
                self.streams["sp"].append(lambda e, sem=self.sem[s], v=self.cnt[s]: e.wait_ge(sem, v))
        for en in ("pe", "act", "dve", "pool"):
            if self.cnt[en] > 0:
                self.streams["sp"].append(lambda e, sem=self.sem[en], v=self.cnt[en]: e.wait_ge(sem, v))
        nc = self.nc
        streams = self.streams
        with nc.Block() as block:
            @block.sync
            def _(e):
                for f in streams["sp"]:
                    f(e)

            @block.tensor
            def _(e):
                for f in streams["pe"]:
                    f(e)

            @block.scalar
            def _(e):
                for f in streams["act"]:
                    f(e)

            @block.vector
            def _(e):
                for f in streams["dve"]:
                    f(e)

            @block.gpsimd
            def _(e):
                for f in streams["pool"]:
                    f(e)
        self.stack.close()


def new_nc():
    return bass.Bass("TRN2", target_bir_lowering=False)


def build_ada():
    nc = new_nc()
    cc = nc.dram_tensor("cc", [5, 2048], F32, kind="ExternalInput").ap()
    adaw = nc.dram_tensor("adaw", [2, 2048, 1536], F32, kind="ExternalInput").ap()
    adab = nc.dram_tensor("adab", [2, 1, 1536], F32, kind="ExternalInput").ap()
    ident_d = nc.dram_tensor("ident", [128, 128], F32, kind="ExternalInput").ap()
    mod = nc.dram_tensor("mod", [2, 5, 1536], F32, kind="ExternalOutput").ap()
    P = Prog(nc)
    cc_t = P.sbuf([5, 2048], F32); r_cc = P.res()
    sc = P.sbuf([5, 2048], F32); r_sc = P.res()
    ident = P.sbuf([128, 128], F32); r_id = P.res()
    ones = P.sbuf([1, 8], F32); r_ones = P.res()
    scT = P.sbuf([128, 16, 5], F32); r_scT = P.res()
    bias = P.sbuf([1, 2, 1536], F32); r_bias = P.res()
    P.dma("sp", cc_t[:], cc, writes=[r_cc])
    P.dma("sp", ident[:], ident_d, writes=[r_id])
    P.dma("sp", bias[:], adab.rearrange("l o n -> o l n"), writes=[r_bias])
    P.op("dve", lambda e: e.memset(ones[:], 1.0), writes=[r_ones])
    P.op("act", lambda e: e.activation(out=sc[:], in_=cc_t[:], func=AF.Silu), reads=[r_cc], writes=[r_sc])
    ps_t = P.psum([128, 16, 5], F32); r_pst = P.res()
    for kc in range(16):
        P.op("pe", lambda e, kc=kc: e.transpose(ps_t[:, kc, :], sc[:, kc * 128:(kc + 1) * 128], ident[:5, :5]),
             reads=[r_sc, r_id], writes=[r_pst], inc=(kc == 15))
    P.op("dve", lambda e: e.tensor_copy(out=scT[:], in_=ps_t[:]), reads=[r_pst], writes=[r_scT])
    wt = [P.sbuf([128, 16, 512], F32) for _ in range(3)]; r_wt = [P.res() for _ in range(3)]
    ps = [P.psum([5, 512], F32) for _ in range(2)]; r_ps = [P.res() for _ in range(2)]
    ot = [P.sbuf([5, 512], F32) for _ in range(2)]; r_ot = [P.res() for _ in range(2)]
    it = 0
    for l in range(2):
        for g in range(3):
            b = it % 3; pb = it % 2
            q = ["sp", "pool", "act"][it % 3]
            P.dma(q, wt[b][:], adaw[l, :, g * 512:(g + 1) * 512].rearrange("(kc p) n -> p kc n", p=128), writes=[r_wt[b]])
            for kc in range(16):
                P.op("pe", lambda e, kc=kc, b=b, pb=pb: e.matmul(ps[pb][:], lhsT=scT[:, kc, :], rhs=wt[b][:, kc, :], start=(kc == 0), stop=False),
                     reads=[r_scT, r_wt[b]], writes=[r_ps[pb]], inc=False)
            P.op("pe", lambda e, l=l, g=g, pb=pb: e.matmul(ps[pb][:], lhsT=ones[:, :5], rhs=bias[:, l, g * 512:(g + 1) * 512], start=False, stop=True),
                 reads=[r_ones, r_bias], writes=[r_ps[pb]], inc=True)
            P.op("dve", lambda e, pb=pb: e.tensor_copy(out=ot[pb][:], in_=ps[pb][:]), reads=[r_ps[pb]], writes=[r_ot[pb]])
            P.dma("sp", mod[l, :, g * 512:(g + 1) * 512], ot[pb][:], reads=[r_ot[pb]])
            it += 1
    P.finish()
    return nc


def emit_uT(P, hT, msc_d, uT, r_uT, T, segs):
    msc = P.sbuf([128, 16, 4], F32); r_msc = P.res()
    P.dma("sp", msc[:], msc_d, writes=[r_msc])
    P.op("dve", lambda e: e.tensor_scalar_add(out=msc[:, :, 0:1], in0=msc[:, :, 0:1], scalar1=1.0), reads=[r_msc], writes=[r_msc])
    P.op("dve", lambda e: e.tensor_scalar_add(out=msc[:, :, 2:3], in0=msc[:, :, 2:3], scalar1=1.0), reads=[r_msc], writes=[r_msc])
    stg = [P.sbuf([128, T], F32) for _ in range(2)]; r_stg = [P.res() for _ in range(2)]
    for kc in range(16):
        b = kc % 2
        P.dma(["sp", "act"][kc % 2], stg[b][:], hT[kc * 128:(kc + 1) * 128, :], writes=[r_stg[b]])
        for (t0, t1, w) in segs:
            P.op("act", lambda e, kc=kc, b=b, t0=t0, t1=t1, w=w: e.activation(
                out=uT[:, kc, t0:t1], in_=stg[b][:, t0:t1], func=AF.Identity,
                scale=msc[:, kc, 2 * w:2 * w + 1], bias=msc[:, kc, 2 * w + 1:2 * w + 2]),
                reads=[r_stg[b], r_msc], writes=[r_uT])


def build_ret():
    nc = new_nc()
    T = NTOT
    hT = nc.dram_tensor("hT", [D, T], F32, kind="ExternalInput").ap()
    msc_d = nc.dram_tensor("msc", [128, 16, 4], F32, kind="ExternalInput").ap()
    wq = nc.dram_tensor("wq", [D, 2048], F32, kind="ExternalInput").ap()
    wk = nc.dram_tensor("wk", [D, 2048], F32, kind="ExternalInput").ap()
    wv = nc.dram_tensor("wv", [D, 4096], F32, kind="ExternalInput").ap()
    cos_d = nc.dram_tensor("cosT", [128, NLAT], F32, kind="ExternalInput").ap()
    sin_d = nc.dram_tensor("sinT", [128, NLAT], F32, kind="ExternalInput").ap()
    mask_d = nc.dram_tensor("mask01", [128, 128], F32, kind="ExternalInput").ap()
    expo_d = nc.dram_tensor("expo", [128, 128], F32, kind="ExternalInput").ap()
    idx_d = nc.dram_tensor("idxs", [128, 4], F32, kind="ExternalInput").ap()
    rdec_d = nc.dram_tensor("rdec", [1, 8], F32, kind="ExternalInput").ap()
    identb_d = nc.dram_tensor("identb", [128, 128], F32, kind="ExternalInput").ap()
    o_d = nc.dram_tensor("o", [T, 8, 512], F32, kind="ExternalOutput").ap()
    P = Prog(nc)
    uT = P.sbuf([128, 16, T], BF16); r_uT = P.res()
    emit_uT(P, hT, msc_d, uT, r_uT, T, [(0, NCTX, 0), (NCTX, T, 1)])
    cosT = P.sbuf([128, NLAT], F32); r_cos = P.res()
    sinT = P.sbuf([128, NLAT], F32); r_sin = P.res()
    mask01 = P.sbuf([128, 128], F32); expo = P.sbuf([128, 128], F32); idxs = P.sbuf([128, 4], F32)
    rdec = P.sbuf([128, 8], F32); identf = P.sbuf([128, 128], F32); identb = P.sbuf([128, 128], BF16)
    r_c = P.res()
    P.dma("sp", cosT[:], cos_d, writes=[r_cos])
    P.dma("sp", sinT[:], sin_d, writes=[r_sin])
    P.dma("sp", mask01[:], mask_d, writes=[r_c])
    r_c2 = P.res(); r_c3 = P.res(); r_c4 = P.res(); r_c5 = P.res()
    P.dma("sp", expo[:], expo_d, writes=[r_c2])
    P.dma("sp", idxs[:], idx_d, writes=[r_c3])
    P.dma("sp", rdec[:], rdec_d.partition_broadcast(128), writes=[r_c4])
    P.dma("sp", identf[:], identb_d, writes=[r_c5])
    r_idb = P.res()
    P.op("dve", lambda e: e.tensor_copy(out=identb[:], in_=identf[:]), reads=[r_c5], writes=[r_idb])
    lg = P.sbuf([128, 8], F32); r_lg = P.res()
    P.op("act", lambda e: e.activation(out=lg[:], in_=rdec[:], func=AF.Exp), reads=[r_c4], writes=[r_lg])
    P.op("dve", lambda e: e.tensor_scalar_mul(out=lg[:], in0=lg[:], scalar1=-1.0), reads=[r_lg], writes=[r_lg])
    maskT = P.sbuf([128, 8, 128], F32); r_mask = P.res()
    decs = P.sbuf([128, 8, 4], F32); r_decs = P.res()
    for h in range(8):
        P.op("act", lambda e, h=h: e.activation(out=maskT[:, h, :], in_=expo[:], func=AF.Exp, scale=lg[:, h:h + 1]),
             reads=[r_c2, r_lg], writes=[r_mask])
        P.op("dve", lambda e, h=h: e.scalar_tensor_tensor(out=maskT[:, h, :], in0=maskT[:, h, :], scalar=1.0 / 16.0, in1=mask01[:],
                                                          op0=ALU.mult, op1=ALU.mult), reads=[r_mask, r_c], writes=[r_mask])
        P.op("act", lambda e, h=h: e.activation(out=decs[:, h, 0:3], in_=idxs[:, 0:3], func=AF.Exp, scale=lg[:, h:h + 1]),
             reads=[r_c3, r_lg], writes=[r_decs])
        P.op("dve", lambda e, h=h: e.tensor_scalar_mul(out=decs[:, h, 0:1], in0=decs[:, h, 0:1], scalar1=1.0 / 16.0),
             reads=[r_decs], writes=[r_decs])
    qT = P.sbuf([128, 2, T], BF16); r_qT = P.res()
    kT = P.sbuf([128, 2, T], BF16); r_kT = P.res()
    vt = P.sbuf([128, 18, 512], BF16); r_v = P.res()
    wq_t = [P.sbuf([128, 16, 256], BF16)] * 2; r_wq = [P.res()] * 2
    wk_t = [P.sbuf([128, 16, 256], BF16)] * 2; r_wk = [P.res()] * 2
    wv_t = [P.sbuf([128, 16, 512], BF16)] * 2; r_wv = [P.res()] * 2
    ps_a = [P.psum([128, 512], F32) for _ in range(2)]; r_psa = [P.res() for _ in range(2)]
    ps_S = P.psum([128, 512], F32); r_psS = P.res()
    ps_kt = P.psum([128, 1024], BF16); r_pskt = P.res()
    ps_o = P.psum([128, 512], F32); r_pso = P.res()
    ps_i = P.psum([128, 512], F32); r_psi = P.res()
    ps_s = [P.psum([128, 512], F32) for _ in range(2)]; r_pss = [P.res() for _ in range(2)]
    tmp = [P.sbuf([128, 512], F32) for _ in range(4)]; r_tmp = [P.res() for _ in range(4)]
    AT = P.sbuf([128, 128], BF16); r_AT = P.res()
    kd = P.sbuf([128, 256], BF16); r_kd = P.res()
    o_sb = [P.sbuf([128, 512], F32) for _ in range(2)]; r_osb = [P.res() for _ in range(2)]
    st = P.sbuf([128, 2, 512], F32); r_st = P.res()
    st_bf = P.sbuf([128, 2, 512], BF16); r_stbf = P.res()
    groups = [(0, 256, 0), (256, 768, 1), (768, 1280, 1), (1280, 1792, 1), (1792, 2304, 1)]

    def load_w(h):
        b = h % 2
        P.dma("pool", wq_t[b][:], wq[:, h * 256:(h + 1) * 256].rearrange("(kc p) n -> p kc n", p=128), writes=[r_wq[b]])
        P.dma("pool", wk_t[b][:], wk[:, h * 256:(h + 1) * 256].rearrange("(kc p) n -> p kc n", p=128), writes=[r_wk[b]])
        P.dma("pool", wv_t[b][:], wv[:, h * 512:(h + 1) * 512].rearrange("(kc p) n -> p kc n", p=128), writes=[r_wv[b]])

    for h in range(8):
        b = h % 2
        load_w(h)
        for (w_t, r_w, dst, r_dst) in ((wq_t[b], r_wq[b], qT, r_qT), (wk_t[b], r_wk[b], kT, r_kT)):
            for (t0, t1, lat) in groups:
                n = t1 - t0
                for dc in range(2):
                    for kc in range(16):
                        P.op("pe", lambda e, dc=dc, kc=kc, w_t=w_t, t0=t0, t1=t1, n=n: e.matmul(
                            ps_a[dc][:, :n], lhsT=w_t[:, kc, dc * 128:(dc + 1) * 128], rhs=uT[:, kc, t0:t1],
                            start=(kc == 0), stop=(kc == 15)), reads=[r_w, r_uT], writes=[r_psa[dc]], inc=(kc == 15))
                if not lat:
                    for dc in range(2):
                        P.op("act", lambda e, dc=dc, dst=dst, t0=t0, t1=t1, n=n: e.activation(out=dst[:, dc, t0:t1], in_=ps_a[dc][:, :n], func=AF.Copy),
                             reads=[r_psa[dc]], writes=[r_dst])
                else:
                    l0 = t0 - NCTX; l1 = t1 - NCTX
                    P.op("dve", lambda e, l0=l0, l1=l1: e.tensor_tensor(out=tmp[0][:], in0=ps_a[0][:], in1=cosT[:, l0:l1], op=ALU.mult),
                         reads=[r_psa[0], r_cos], writes=[r_tmp[0]])
                    P.op("dve", lambda e, l0=l0, l1=l1: e.tensor_tensor(out=tmp[1][:], in0=ps_a[1][:], in1=sinT[:, l0:l1], op=ALU.mult),
                         reads=[r_psa[1], r_sin], writes=[r_tmp[1]])
                    P.op("dve", lambda e, l0=l0, l1=l1: e.tensor_tensor(out=tmp[2][:], in0=ps_a[0][:], in1=sinT[:, l0:l1], op=ALU.mult),
                         reads=[r_psa[0], r_sin], writes=[r_tmp[2]])
                    P.op("dve", lambda e, l0=l0, l1=l1: e.tensor_tensor(out=tmp[3][:], in0=ps_a[1][:], in1=cosT[:, l0:l1], op=ALU.mult),
                         reads=[r_psa[1], r_cos], writes=[r_tmp[3]])
                    P.op("pool", lambda e, dst=dst, t0=t0, t1=t1: e.tensor_tensor(out=dst[:, 0, t0:t1], in0=tmp[0][:], in1=tmp[1][:], op=ALU.subtract),
                         reads=[r_tmp[0], r_tmp[1]], writes=[r_dst])
                    P.op("pool", lambda e, dst=dst, t0=t0, t1=t1: e.tensor_tensor(out=dst[:, 1, t0:t1], in0=tmp[2][:], in1=tmp[3][:], op=ALU.add),
                         reads=[r_tmp[2], r_tmp[3]], writes=[r_dst])
        for c in range(18):
            pb = c % 2
            for kc in range(16):
                P.op("pe", lambda e, c=c, kc=kc, pb=pb, b=b: e.matmul(ps_a[pb][:], lhsT=uT[:, kc, c * 128:(c + 1) * 128], rhs=wv_t[b][:, kc, :],
                                                                 start=(kc == 0), stop=(kc == 15)), reads=[r_wv[b], r_uT], writes=[r_psa[pb]], inc=(kc == 15))
            P.op("act", lambda e, c=c, pb=pb: e.activation(out=vt[:, c, :], in_=ps_a[pb][:], func=AF.Copy), reads=[r_psa[pb]], writes=[r_v])
        for c in range(18):
            cs = slice(c * 128, (c + 1) * 128)
            ob = c % 2
            for dc in range(2):
                P.op("pe", lambda e, dc=dc, cs=cs: e.matmul(ps_S[:, :128], lhsT=kT[:, dc, cs], rhs=qT[:, dc, cs], start=(dc == 0), stop=(dc == 1)),
                     reads=[r_kT, r_qT], writes=[r_psS], inc=(dc == 1))
            P.op("dve", lambda e, h=h: e.tensor_tensor(out=AT[:], in0=ps_S[:, :128], in1=maskT[:, h, :], op=ALU.mult),
                 reads=[r_psS, r_mask], writes=[r_AT])
            P.op("pe", lambda e, c=c: e.matmul(ps_o[:], lhsT=AT[:], rhs=vt[:, c, :], start=True, stop=True), reads=[r_AT, r_v], writes=[r_pso])
            P.op("act", lambda e, ob=ob: e.activation(out=o_sb[ob][:], in_=ps_o[:], func=AF.Copy), reads=[r_pso], writes=[r_osb[ob]])
            if c > 0:
                for dc in range(2):
                    P.op("pe", lambda e, dc=dc, cs=cs: e.matmul(ps_i[:], lhsT=qT[:, dc, cs], rhs=st_bf[:, dc, :], start=(dc == 0), stop=(dc == 1)),
                         reads=[r_qT, r_stbf], writes=[r_psi], inc=(dc == 1))
                P.op("dve", lambda e, ob=ob, h=h: e.scalar_tensor_tensor(out=o_sb[ob][:], in0=ps_i[:], scalar=decs[:, h, 0:1], in1=o_sb[ob][:],
                                                                         op0=ALU.mult, op1=ALU.add), reads=[r_psi, r_decs, r_osb[ob]], writes=[r_osb[ob]])
            P.dma("sp", o_d[c * 128:(c + 1) * 128, h, :], o_sb[ob][:], reads=[r_osb[ob]])
            if c < 17:
                for dc in range(2):
                    P.op("pe", lambda e, dc=dc, cs=cs: e.transpose(ps_kt[:, dc * 128:(dc + 1) * 128], kT[:, dc, cs], identb[:]),
                         reads=[r_kT, r_idb], writes=[r_pskt], inc=(dc == 1))
                P.op("act", lambda e, h=h: e.activation(out=kd[:], in_=ps_kt[:, :256], func=AF.Identity, scale=decs[:, h, 1:2]),
                     reads=[r_pskt, r_decs], writes=[r_kd])
                for dc in range(2):
                    P.op("pe", lambda e, dc=dc, c=c: e.matmul(ps_s[dc][:], lhsT=kd[:, dc * 128:(dc + 1) * 128], rhs=vt[:, c, :], start=True, stop=True),
                         reads=[r_kd, r_v], writes=[r_pss[dc]])
                    if c == 0:
                        P.op("dve", lambda e, dc=dc: e.tensor_copy(out=st[:, dc, :], in_=ps_s[dc][:]), reads=[r_pss[dc]], writes=[r_st])
                    else:
                        P.op("dve", lambda e, dc=dc, h=h: e.scalar_tensor_tensor(out=st[:, dc, :], in0=st[:, dc, :], scalar=decs[:, h, 2:3], in1=ps_s[dc][:],
                                                                                 op0=ALU.mult, op1=ALU.add), reads=[r_st, r_decs, r_pss[dc]], writes=[r_st])
                P.op("pool", lambda e: e.tensor_copy(out=st_bf[:], in_=st[:]), reads=[r_st], writes=[r_stbf])
    P.finish()
    return nc


def _seg_flip_np(t):
    return np.concatenate([t[:NCTX][::-1], t[NCTX:][::-1]], axis=0)


def _rope_tables():
    rows = NLAT // 64
    row = np.repeat(np.arange(rows, dtype=np.float32), 64)
    col = np.tile(np.arange(64, dtype=np.float32), rows)
    n_freq = 64
    inv_freq = (np.float32(10000.0) ** (-np.arange(n_freq, dtype=np.float32) / np.float32(n_freq))).astype(np.float32)
    ang = np.concatenate([row[:, None] * inv_freq, col[:, None] * inv_freq], axis=-1).astype(np.float32)
    return np.cos(ang).astype(np.float32), np.sin(ang).astype(np.float32)


def _msc(mod, layer, b, j_scale, j_shift):
    m = mod[layer].reshape(5, 6, D)
    cols = [m[4, j_scale], m[4, j_shift], m[b, j_scale], m[b, j_shift]]
    a = np.stack(cols, axis=-1)
    return np.ascontiguousarray(a.reshape(16, 128, 4).transpose(1, 0, 2))


def run_ada(inputs):
    nc = build_ada()
    cc = np.concatenate([inputs["c"], inputs["c_ctx"][None]], 0).astype(np.float32)
    in_maps = []
    for i in range(8):
        in_maps.append({"cc": cc, "adaw": np.ascontiguousarray(inputs["ada_w"][:, :, i * 1536:(i + 1) * 1536]),
                        "adab": np.ascontiguousarray(inputs["ada_b"][:, None, i * 1536:(i + 1) * 1536]),
                        "ident": np.eye(128, dtype=np.float32)})
    res = run_bass_kernel_spmd(nc, in_maps, core_ids=list(range(8)))
    return np.concatenate([r["mod"] for r in res.results], axis=2)


def run_ret(inputs, mod, hcat):
    nc = build_ret()
    cos, sin = _rope_tables()
    w = inputs["ret_w_in"][0]
    jj = np.arange(128)[:, None]; ii = np.arange(128)[None, :]
    expo = np.maximum(ii - jj, 0).astype(np.float32)
    idxs = np.stack([np.arange(128) + 1.0, 127.0 - np.arange(128), np.full(128, 128.0), np.zeros(128)], -1).astype(np.float32)
    in_maps = []
    for core in range(8):
        b, d = core // 2, core % 2
        h = hcat[b]
        if d:
            h = _seg_flip_np(h)
        cs, sn = (cos[::-1], sin[::-1]) if d else (cos, sin)
        mask01 = (ii > jj) if d else (ii >= jj)
        in_maps.append({
            "hT": np.ascontiguousarray(h.T), "msc": _msc(mod, 0, b, 1, 0),
            "wq": np.ascontiguousarray(w[:, 0:2048]), "wk": np.ascontiguousarray(w[:, 2048:4096]),
            "wv": np.ascontiguousarray(w[:, 4096:8192]),
            "cosT": np.ascontiguousarray(cs.T), "sinT": np.ascontiguousarray(sn.T),
            "mask01": mask01.astype(np.float32), "expo": expo, "idxs": idxs,
            "rdec": np.ascontiguousarray(inputs["ret_decay"][0, d][None, :]),
            "identb": np.eye(128, dtype=np.float32)})
    res = run_bass_kernel_spmd(nc, in_maps, core_ids=list(range(8)))
    out = np.zeros((4, 2, NTOT, 8, 512), np.float32)
    for core in range(8):
        b, d = core // 2, core % 2
        o = res.results[core]["o"]
        out[b, d] = _seg_flip_np(o) if d else o
    return out


def emit_rsqrt(P, out_ap, in_ap, scale, eps_ap, r_s):
    P.op("act", lambda e: e.activation(out=out_ap, in_=in_ap, func=AF.Sqrt, bias=eps_ap, scale=scale), reads=[r_s], writes=[r_s])
    P.op("dve", lambda e: e.reciprocal(out=out_ap, in_=out_ap), reads=[r_s], writes=[r_s])


def emit_ln(P, hh, r_hh, t, lnrow_d, gi, bi, scr):
    stats, mv, rstd, gbc, bbc, r_s, r_g, r_b, eps5 = scr
    for j in range(4):
        P.op("dve", lambda e, j=j: e.bn_stats(out=stats[:, j, :], in_=hh[:, t, j * 512:(j + 1) * 512]), reads=[r_hh[t]], writes=[r_s])
    P.op("dve", lambda e: e.bn_aggr(out=mv[:], in_=stats[:].rearrange("p a b -> p (a b)")), reads=[r_s], writes=[r_s])
    emit_rsqrt(P, rstd[:, 0:1], mv[:, 1:2], 1.0, eps5[:, 0:1], r_s)
    P.op("dve", lambda e: e.tensor_scalar(out=hh[:, t, :], in0=hh[:, t, :], scalar1=mv[:, 0:1], scalar2=rstd[:, 0:1], op0=ALU.subtract, op1=ALU.mult),
         reads=[r_s, r_hh[t]], writes=[r_hh[t]])
    for j in range(4):
        cs = slice(j * 512, (j + 1) * 512)
        P.dma("sp", gbc[:], lnrow_d[gi:gi + 1, cs].partition_broadcast(128), writes=[r_g])
        P.dma("sp", bbc[:], lnrow_d[bi:bi + 1, cs].partition_broadcast(128), writes=[r_b])
        P.op("pool", lambda e, cs=cs: e.tensor_tensor(out=hh[:, t, cs], in0=hh[:, t, cs], in1=gbc[:], op=ALU.mult), reads=[r_g, r_hh[t]], writes=[r_hh[t]])
        P.op("pool", lambda e, cs=cs: e.tensor_tensor(out=hh[:, t, cs], in0=hh[:, t, cs], in1=bbc[:], op=ALU.add), reads=[r_b, r_hh[t]], writes=[r_hh[t]])


def build_blk(layer):
    nc = new_nc()
    T = 1152 if layer == 0 else 1024
    nt = T // 128
    tgs = [(0, 384), (384, 768), (768, 1152)] if layer == 0 else [(0, 512), (512, 1024)]
    of_d = nc.dram_tensor("of", [T, 4096], F32, kind="ExternalInput").ap()
    ob_d = nc.dram_tensor("ob", [T, 4096], F32, kind="ExternalInput").ap()
    h_d = nc.dram_tensor("h", [T, D], F32, kind="ExternalInput").ap()
    hT_d = nc.dram_tensor("hT", [D, T], F32, kind="ExternalInput").ap()
    msc1_d = nc.dram_tensor("msc1", [128, 16, nt, 2], F32, kind="ExternalInput").ap()
    msc2_d = nc.dram_tensor("msc2", [128, 16, nt, 2], F32, kind="ExternalInput").ap()
    g1_d = nc.dram_tensor("g1", [nt, D], F32, kind="ExternalInput").ap()
    g2_d = nc.dram_tensor("g2", [nt, D], F32, kind="ExternalInput").ap()
    ln_d = nc.dram_tensor("lnrows", [4, D], F32, kind="ExternalInput").ap()
    wg_d = nc.dram_tensor("wg", [D, 4096], F32, kind="ExternalInput").ap()
    wo_d = nc.dram_tensor("wo", [4096, D], F32, kind="ExternalInput").ap()
    w1_d = nc.dram_tensor("w1", [D, 2 * DFF], F32, kind="ExternalInput").ap()
    w2_d = nc.dram_tensor("w2", [DFF, D], F32, kind="ExternalInput").ap()
    nw_d = nc.dram_tensor("nw", [1, 128], F32, kind="ExternalInput").ap()
    id_d = nc.dram_tensor("identf", [128, 128], F32, kind="ExternalInput").ap()
    out_d = nc.dram_tensor("out", [T, D], F32, kind="ExternalOutput").ap()
    P = Prog(nc)
    uT = P.sbuf([128, 16, T], BF16); r_uT = P.res()
    big = P.sbuf([128, 11, T], BF16); r_big = P.res()
    hh = P.sbuf([128, nt, D], F32); r_hh = [P.res() for _ in range(nt)]
    W = [P.sbuf([128, 16, 512], BF16) for _ in range(2)]; r_W = [P.res() for _ in range(2)]
    Wgu = [P.sbuf([128, 16, 256], BF16) for _ in range(2)]; r_Wg = [P.res() for _ in range(2)]; r_Wu = [P.res() for _ in range(2)]
    msc1 = P.sbuf([128, 16, nt, 2], F32); r_m1 = P.res()
    msc2 = P.sbuf([128, 16, nt, 2], F32); r_m2 = P.res()
    identf = P.sbuf([128, 128], F32); r_idf = P.res()
    identb = P.sbuf([128, 128], BF16); r_idb = P.res()
    nw = P.sbuf([128, 128], F32); r_nw = P.res()
    P.dma("sp", msc1[:], msc1_d, writes=[r_m1])
    P.dma("sp", msc2[:], msc2_d, writes=[r_m2])
    P.dma("sp", identf[:], id_d, writes=[r_idf])
    P.dma("sp", nw[:], nw_d.partition_broadcast(128), writes=[r_nw])
    P.op("dve", lambda e: e.tensor_copy(out=identb[:], in_=identf[:]), reads=[r_idf], writes=[r_idb])
    P.op("dve", lambda e: e.tensor_scalar_add(out=msc1[:, :, :, 0:1], in0=msc1[:, :, :, 0:1], scalar1=1.0), reads=[r_m1], writes=[r_m1])
    P.op("dve", lambda e: e.tensor_scalar_add(out=msc2[:, :, :, 0:1], in0=msc2[:, :, :, 0:1], scalar1=1.0), reads=[r_m2], writes=[r_m2])
    for t in range(nt):
        P.dma(["sp", "act"][t % 2], hh[:, t, :], h_d[t * 128:(t + 1) * 128, :], writes=[r_hh[t]])
    stg = P.sbuf([128, T], F32); r_stg = P.res()
    for kc in range(16):
        P.dma("sp", stg[:], hT_d[kc * 128:(kc + 1) * 128, :], writes=[r_stg])
        for t in range(nt):
            P.op("act", lambda e, kc=kc, t=t: e.activation(out=uT[:, kc, t * 128:(t + 1) * 128], in_=stg[:, t * 128:(t + 1) * 128], func=AF.Identity,
                                                           scale=msc1[:, kc, t, 0:1], bias=msc1[:, kc, t, 1:2]), reads=[r_stg, r_m1], writes=[r_uT])
    ps_m = [P.psum([128, 512], F32) for _ in range(2)]; r_psm = [P.res() for _ in range(2)]
    ps_u = [P.psum([128, 512], F32) for _ in range(2)]; r_psu = [P.res() for _ in range(2)]
    ps_tr = P.psum([128, 1024], BF16); r_pstr = P.res()
    ps_tf = [P.psum([128, 512], F32) for _ in range(2)]; r_pstf = [P.res() for _ in range(2)]
    sg = P.sbuf([128, 512], F32); r_sg = P.res()
    oft = P.sbuf([128, 512], F32); r_of = P.res()
    obt = P.sbuf([128, 512], F32); r_ob = P.res()
    xn = P.sbuf([128, 512], F32); r_xn = P.res()
    og = P.sbuf([128, 512], BF16); r_og = P.res()
    gbc = P.sbuf([128, 512], F32); r_gbc = P.res()
    tmp = P.sbuf([128, 512], F32); r_tmp = P.res()
    stats = P.sbuf([128, 4, 6], F32); mv = P.sbuf([128, 2], F32); rstd = P.sbuf([128, 4], F32); r_s = P.res()
    lg_t = P.sbuf([128, 512], F32); lb_t = P.sbuf([128, 512], F32); r_lg = P.res(); r_lb = P.res()
    eps5 = P.sbuf([128, 2], F32)
    P.op("dve", lambda e: e.memset(eps5[:, 0:1], LN_EPS), writes=[r_s])
    P.op("dve", lambda e: e.memset(eps5[:, 1:2], 1e-6), writes=[r_s])
    ln_scr = (stats, mv, rstd, lg_t, lb_t, r_s, r_lg, r_lb, eps5)
    mmi = [0]

    def accum_h(t, cs, ps, r_ps, g_d, first):
        P.dma("sp", gbc[:], g_d[t:t + 1, cs].partition_broadcast(128), writes=[r_gbc])
        P.op("dve", lambda e: e.tensor_tensor(out=tmp[:], in0=ps[:], in1=gbc[:], op=ALU.mult), reads=[r_ps, r_gbc], writes=[r_tmp])
        P.op("dve", lambda e: e.scalar_tensor_tensor(out=hh[:, t, cs], in0=hh[:, t, cs], scalar=(ALPHA if first else 1.0), in1=tmp[:],
                                                     op0=ALU.mult, op1=ALU.add), reads=[r_tmp, r_hh[t]], writes=[r_hh[t]])

    for kq in range(4):
        for cgi in range(2):
            cg = kq * 2 + cgi
            cols = slice(cg * 512, (cg + 1) * 512)
            wb = mmi[0] % 2; mmi[0] += 1
            P.dma("pool", W[wb][:], wg_d[:, cols].rearrange("(kc p) n -> p kc n", p=128), writes=[r_W[wb]])
            for t in range(nt):
                pb = t % 2
                ts_ = slice(t * 128, (t + 1) * 128)
                for kc in range(16):
                    P.op("pe", lambda e, kc=kc, pb=pb, wb=wb, ts_=ts_: e.matmul(ps_m[pb][:], lhsT=uT[:, kc, ts_], rhs=W[wb][:, kc, :],
                                                                          start=(kc == 0), stop=(kc == 15)), reads=[r_uT, r_W[wb]], writes=[r_psm[pb]], inc=(kc == 15))
                P.op("act", lambda e, pb=pb: e.activation(out=sg[:], in_=ps_m[pb][:], func=AF.Silu), reads=[r_psm[pb]], writes=[r_sg])
                P.dma("sp", oft[:], of_d[ts_, cols], writes=[r_of])
                P.dma("act", obt[:], ob_d[ts_, cols], writes=[r_ob])
                P.op("pool", lambda e: e.tensor_tensor(out=oft[:], in0=oft[:], in1=obt[:], op=ALU.add), reads=[r_of, r_ob], writes=[r_of])
                if layer == 0:
                    P.op("dve", lambda e: e.bn_stats(out=stats[:, 0, :], in_=oft[:]), reads=[r_of], writes=[r_s])
                    P.op("dve", lambda e: e.bn_aggr(out=mv[:], in_=stats[:, 0, :]), reads=[r_s], writes=[r_s])
                    emit_rsqrt(P, rstd[:, 0:1], mv[:, 1:2], 1.0, eps5[:, 0:1], r_s)
                    P.op("dve", lambda e: e.tensor_scalar(out=xn[:], in0=oft[:], scalar1=mv[:, 0:1], scalar2=rstd[:, 0:1], op0=ALU.subtract, op1=ALU.mult),
                         reads=[r_s, r_of], writes=[r_xn])
                else:
                    P.op("pool", lambda e: e.tensor_tensor(out=xn[:], in0=oft[:], in1=oft[:], op=ALU.mult), reads=[r_of], writes=[r_xn])
                    P.op("dve", lambda e: e.tensor_reduce(out=rstd[:], in_=xn[:].rearrange("p (a b) -> p a b", a=4), axis=AX.X, op=ALU.add),
                         reads=[r_xn], writes=[r_s])
                    emit_rsqrt(P, rstd[:], rstd[:], 1.0 / 128.0, eps5[:, 1:2], r_s)
                    for j in range(4):
                        P.op("dve", lambda e, j=j: e.scalar_tensor_tensor(out=xn[:, j * 128:(j + 1) * 128], in0=oft[:, j * 128:(j + 1) * 128],
                                                                          scalar=rstd[:, j:j + 1], in1=nw[:], op0=ALU.mult, op1=ALU.mult),
                             reads=[r_s, r_of, r_nw, r_xn], writes=[r_xn])
                P.op("dve", lambda e: e.tensor_tensor(out=og[:], in0=xn[:], in1=sg[:], op=ALU.mult), reads=[r_xn, r_sg], writes=[r_og])
                for j in range(4):
                    P.op("pe", lambda e, j=j: e.transpose(ps_tr[:, j * 128:(j + 1) * 128], og[:, j * 128:(j + 1) * 128], identb[:]),
                         reads=[r_og, r_idb], writes=[r_pstr], inc=(j == 3))
                P.op("act", lambda e, cgi=cgi, ts_=ts_: e.activation(out=big[:, cgi * 4:(cgi + 1) * 4, ts_],
                                                                    in_=ps_tr[:, :512].rearrange("p (a b) -> p a b", a=4), func=AF.Copy),
                     reads=[r_pstr], writes=[r_big])
        for cg2 in range(4):
            cs = slice(cg2 * 512, (cg2 + 1) * 512)
            wb = mmi[0] % 2; mmi[0] += 1
            P.dma("pool", W[wb][:, 0:8, :], wo_d[kq * 1024:(kq + 1) * 1024, cs].rearrange("(kc p) n -> p kc n", p=128), writes=[r_W[wb]])
            for t in range(nt):
                pb = t % 2
                ts_ = slice(t * 128, (t + 1) * 128)
                for kc in range(8):
                    P.op("pe", lambda e, kc=kc, pb=pb, wb=wb, ts_=ts_: e.matmul(ps_u[pb][:], lhsT=big[:, kc, ts_], rhs=W[wb][:, kc, :],
                                                                          start=(kc == 0), stop=(kc == 7)), reads=[r_big, r_W[wb]], writes=[r_psu[pb]], inc=(kc == 7))
                accum_h(t, cs, ps_u[pb], r_psu[pb], g1_d, kq == 0)
    for t in range(nt):
        emit_ln(P, hh, r_hh, t, ln_d, 0, 1, ln_scr)
        ts_ = slice(t * 128, (t + 1) * 128)
        for kg in range(4):
            pb = kg % 2
            for j in range(4):
                kc = kg * 4 + j
                P.op("pe", lambda e, kc=kc, j=j, pb=pb, t=t: e.transpose(ps_tf[pb][:, j * 128:(j + 1) * 128], hh[:, t, kc * 128:(kc + 1) * 128], identf[:]),
                     reads=[r_hh[t], r_idf], writes=[r_pstf[pb]], inc=(j == 3))
            for j in range(4):
                kc = kg * 4 + j
                P.op("act", lambda e, kc=kc, j=j, pb=pb, t=t, ts_=ts_: e.activation(out=uT[:, kc, ts_], in_=ps_tf[pb][:, j * 128:(j + 1) * 128], func=AF.Identity,
                                                                               scale=msc2[:, kc, t, 0:1], bias=msc2[:, kc, t, 1:2]),
                     reads=[r_pstf[pb], r_m2], writes=[r_uT])
    for fq in range(4):
        for fi in range(11):
            f = fq * 11 + fi
            wb = f % 2
            P.dma("pool", Wgu[wb][:, :, 0:128], w1_d[:, f * 128:(f + 1) * 128].rearrange("(kc p) n -> p kc n", p=128), writes=[r_Wg[wb]])
            P.dma("pool", Wgu[wb][:, :, 128:256], w1_d[:, DFF + f * 128:DFF + (f + 1) * 128].rearrange("(kc p) n -> p kc n", p=128), writes=[r_Wu[wb]])
            for gi_, (t0, t1) in enumerate(tgs):
                n = t1 - t0
                pb = gi_ % 2
                for kc in range(16):
                    P.op("pe", lambda e, kc=kc, pb=pb, wb=wb, t0=t0, t1=t1, n=n: e.matmul(ps_m[pb][:, :n], lhsT=Wgu[wb][:, kc, 0:128], rhs=uT[:, kc, t0:t1],
                                                                                    start=(kc == 0), stop=(kc == 15)), reads=[r_Wg[wb], r_uT], writes=[r_psm[pb]], inc=(kc == 15))
                for kc in range(16):
                    P.op("pe", lambda e, kc=kc, pb=pb, wb=wb, t0=t0, t1=t1, n=n: e.matmul(ps_u[pb][:, :n], lhsT=Wgu[wb][:, kc, 128:256], rhs=uT[:, kc, t0:t1],
                                                                                    start=(kc == 0), stop=(kc == 15)), reads=[r_Wu[wb], r_uT], writes=[r_psu[pb]], inc=(kc == 15))
                P.op("act", lambda e, pb=pb, n=n: e.activation(out=sg[:, :n], in_=ps_m[pb][:, :n], func=AF.Silu), reads=[r_psm[pb]], writes=[r_sg])
                P.op("dve", lambda e, pb=pb, n=n, fi=fi, t0=t0, t1=t1: e.tensor_tensor(out=big[:, fi, t0:t1], in0=ps_u[pb][:, :n], in1=sg[:, :n], op=ALU.mult),
                     reads=[r_psu[pb], r_sg], writes=[r_big])
        for cg2 in range(4):
            cs = slice(cg2 * 512, (cg2 + 1) * 512)
            wb = mmi[0] % 2; mmi[0] += 1
            P.dma("pool", W[wb][:, 0:11, :], w2_d[fq * 1408:(fq + 1) * 1408, cs].rearrange("(kc p) n -> p kc n", p=128), writes=[r_W[wb]])
            for t in range(nt):
                pb = t % 2
                ts_ = slice(t * 128, (t + 1) * 128)
                for kc in range(11):
                    P.op("pe", lambda e, kc=kc, pb=pb, wb=wb, ts_=ts_: e.matmul(ps_tf[pb][:], lhsT=big[:, kc, ts_], rhs=W[wb][:, kc, :],
                                                                          start=(kc == 0), stop=(kc == 10)), reads=[r_big, r_W[wb]], writes=[r_pstf[pb]], inc=(kc == 10))
                accum_h(t, cs, ps_tf[pb], r_pstf[pb], g2_d, fq == 0)
    for t in range(nt):
        emit_ln(P, hh, r_hh, t, ln_d, 2, 3, ln_scr)
        P.dma("sp", out_d[t * 128:(t + 1) * 128, :], hh[:, t, :], reads=[r_hh[t]])
    P.finish()
    return nc


def run_blk(inputs, mod, layer, o, hcat):
    nc = build_blk(layer)
    T = 1152 if layer == 0 else 1024
    nt = T // 128
    m = mod[layer].reshape(5, 6, D)
    if layer == 0:
        wg = np.ascontiguousarray(inputs["ret_w_in"][0][:, 8192:12288]); wo = inputs["ret_w_out"][0]
        nw = np.ones((1, 128), np.float32)
    else:
        wg = np.ascontiguousarray(inputs["gdn_w_in"][0][:, 8192:12288]); wo = inputs["gdn_w_out"][0]
        nw = np.ascontiguousarray(inputs["gdn_norm"][0][None, :])
    lnrows = np.stack([inputs["ln_g"][layer, 0], inputs["ln_b"][layer, 0], inputs["ln_g"][layer, 1], inputs["ln_b"][layer, 1]], 0)
    in_maps = []
    for core in range(8):
        b, s = core // 2, core % 2
        t0 = s * T + (0 if layer == 0 else NCTX)
        rows = [(4 if (t0 + t * 128) < NCTX else b) for t in range(nt)]

        def msc(js, jh):
            a = np.stack([np.stack([m[r, js], m[r, jh]], -1) for r in rows], 1)
            return np.ascontiguousarray(a.reshape(16, 128, nt, 2).transpose(1, 0, 2, 3))
        h = hcat[b, t0:t0 + T]
        in_maps.append({
            "of": np.ascontiguousarray(o[b, 0, t0:t0 + T].reshape(T, 4096)), "ob": np.ascontiguousarray(o[b, 1, t0:t0 + T].reshape(T, 4096)),
            "h": np.ascontiguousarray(h), "hT": np.ascontiguousarray(h.T),
            "msc1": msc(1, 0), "msc2": msc(4, 3),
            "g1": np.ascontiguousarray(np.stack([m[r, 2] for r in rows], 0)), "g2": np.ascontiguousarray(np.stack([m[r, 5] for r in rows], 0)),
            "lnrows": np.ascontiguousarray(lnrows), "wg": wg, "wo": wo, "w1": inputs["ffn_w_in"][layer], "w2": inputs["ffn_w_out"][layer],
            "nw": nw, "identf": np.eye(128, dtype=np.float32)})
    res = run_bass_kernel_spmd(nc, in_maps, core_ids=list(range(8)))
    if layer == 0:
        out = np.zeros((4, NTOT, D), np.float32)
        for core in range(8):
            b, s = core // 2, core % 2
            out[b, s * T:(s + 1) * T] = res.results[core]["out"]
    else:
        out = np.zeros((4, NLAT, D), np.float32)
        for core in range(8):
            b, s = core // 2, core % 2
            out[b, s * T:(s + 1) * T] = res.results[core]["out"]
    return out


def build_gdn(n_heads=16):
    nc = new_nc()
    T = NTOT
    NCH = T // 128
    hT = nc.dram_tensor("hT", [D, T], F32, kind="ExternalInput").ap()
    msc_d = nc.dram_tensor("msc", [128, 16, 4], F32, kind="ExternalInput").ap()
    wqkv = nc.dram_tensor("wqkv", [D, 8192], F32, kind="ExternalInput").ap()
    wba = nc.dram_tensor("wba", [D, 64], F32, kind="ExternalInput").ap()
    convw_d = nc.dram_tensor("convw", [128, 64, 5], F32, kind="ExternalInput").ap()
    alog_d = nc.dram_tensor("alog", [1, 32], F32, kind="ExternalInput").ap()
    dtb_d = nc.dram_tensor("dtb", [1, 32], F32, kind="ExternalInput").ap()
    id_d = nc.dram_tensor("identf", [128, 128], F32, kind="ExternalInput").ap()
    U_d = nc.dram_tensor("U", [128, 128], F32, kind="ExternalInput").ap()
    msl_d = nc.dram_tensor("msl", [128, 128], F32, kind="ExternalInput").ap()
    o_d = nc.dram_tensor("o", [T, 32, 128], F32, kind="ExternalOutput").ap()
    P = Prog(nc)
    uT = P.sbuf([128, 16, T], BF16); r_uT = P.res()
    emit_uT(P, hT, msc_d, uT, r_uT, T, [(0, NCTX, 0), (NCTX, T, 1)])
    identf = P.sbuf([128, 128], F32); r_idf = P.res()
    identb = P.sbuf([128, 128], BF16); r_idb = P.res()
    U = P.sbuf([128, 128], F32); r_U = P.res()
    msl = P.sbuf([128, 128], F32); r_msl = P.res()
    onesf = P.sbuf([128, 128], F32); r_ones = P.res()
    cst = P.sbuf([128, 2], F32); r_cst = P.res()
    convw = P.sbuf([128, 64, 5], F32); r_cw = P.res()
    alog = P.sbuf([128, 32], F32); r_al = P.res()
    dtb = P.sbuf([128, 32], F32); r_dtb = P.res()
    P.dma("sp", identf[:], id_d, writes=[r_idf])
    P.dma("sp", U[:], U_d, writes=[r_U])
    P.dma("sp", msl[:], msl_d, writes=[r_msl])
    P.dma("sp", convw[:], convw_d, writes=[r_cw])
    P.dma("sp", alog[:], alog_d.partition_broadcast(128), writes=[r_al])
    P.dma("sp", dtb[:], dtb_d.partition_broadcast(128), writes=[r_dtb])
    P.op("dve", lambda e: e.tensor_copy(out=identb[:], in_=identf[:]), reads=[r_idf], writes=[r_idb])
    P.op("dve", lambda e: e.memset(onesf[:], 1.0), writes=[r_ones])
    P.op("dve", lambda e: e.memset(cst[:, 0:1], 1e-6), writes=[r_cst])
    P.op("dve", lambda e: e.memset(cst[:, 1:2], 1.0), writes=[r_cst])
    P.op("act", lambda e: e.activation(out=alog[:], in_=alog[:], func=AF.Exp), reads=[r_al], writes=[r_al])
    ps_p = [P.psum([128, 512], F32) for _ in range(2)]; r_pp = [P.res(excl=True) for _ in range(2)]
    bank = [P.psum([128, 512], F32) for _ in range(6)]; r_bk = [P.res(excl=True) for _ in range(6)]
    bank_bf = bank[3]
    beta = P.sbuf([128, NCH, 32], F32); r_beta = P.res()
    negb = P.sbuf([128, NCH, 32], F32); r_negb = P.res()
    la = P.sbuf([128, NCH, 32], F32); r_la = P.res()
    wba_t = P.sbuf([128, 16, 64], BF16); r_wba = P.res()
    t32 = P.sbuf([128, 32], F32); r_t32 = P.res()
    P.dma("pool", wba_t[:], wba.rearrange("(kc p) n -> p kc n", p=128), writes=[r_wba])
    for c in range(NCH):
        pb = c % 2
        for kc in range(16):
            P.op("pe", lambda e, c=c, kc=kc, pb=pb: e.matmul(ps_p[pb][:, :64], lhsT=uT[:, kc, c * 128:(c + 1) * 128], rhs=wba_t[:, kc, :],
                                                       start=(kc == 0), stop=(kc == 15)), reads=[r_uT, r_wba], writes=[r_pp[pb]], inc=(kc == 15))
        P.op("act", lambda e, c=c, pb=pb: e.activation(out=beta[:, c, :], in_=ps_p[pb][:, 0:32], func=AF.Sigmoid), reads=[r_pp[pb]], writes=[r_beta])
        P.op("dve", lambda e, pb=pb: e.tensor_tensor(out=t32[:], in0=ps_p[pb][:, 32:64], in1=dtb[:], op=ALU.add), reads=[r_pp[pb], r_dtb], writes=[r_t32])
        P.op("act", lambda e: e.activation(out=t32[:], in_=t32[:], func=AF.Exp), reads=[r_t32], writes=[r_t32])
        P.op("act", lambda e: e.activation(out=t32[:], in_=t32[:], func=AF.Ln, bias=cst[:, 1:2], scale=1.0), reads=[r_t32, r_cst], writes=[r_t32])
        P.op("dve", lambda e, c=c: e.scalar_tensor_tensor(out=la[:, c, :], in0=t32[:], scalar=-1.0, in1=alog[:], op0=ALU.mult, op1=ALU.mult),
             reads=[r_t32, r_al], writes=[r_la])
    P.op("dve", lambda e: e.tensor_scalar_mul(out=negb[:], in0=beta[:], scalar1=-1.0), reads=[r_beta], writes=[r_negb])
    LP = 2 + NCTX + 2 + 2 + NLAT + 2
    xp = P.sbuf([128, LP], F32); r_xp = P.res()
    acc = P.sbuf([128, T], F32); r_acc = P.res()
    ys = P.sbuf([128, T], F32); r_ys = P.res()
    sq = P.sbuf([128, 512], F32); r_sq = P.res()
    rn = P.sbuf([128, 512], F32); r_rn = P.res()
    qT = P.sbuf([128, T], BF16); r_qT = P.res()
    kT = P.sbuf([128, T], BF16); r_kT = P.res()
    kTok = P.sbuf([128, NCH, 128], BF16); r_kTok = P.res()
    vb = [P.sbuf([128, NCH, 128], BF16) for _ in range(2)]; r_vb = [P.res() for _ in range(2)]
    Wp = [P.sbuf([128, 16, 128], BF16) for _ in range(2)]; r_Wp = [P.res() for _ in range(2)]
    P.op("pool", lambda e: e.memset(xp[:], 0.0), writes=[r_xp])
    ps_trb = P.psum([128, 1024], BF16) if False else None
    def f32t():
        return P.sbuf([128, 128], F32), P.res()
    LAU, r_LAU = f32t(); Zabs, r_Z = f32t(); decA, r_decA = f32t(); egR, r_egR = f32t(); t1, r_t1 = f32t(); t2, r_t2 = f32t()
    Xs = [f32t() for _ in range(2)]; Ys = [f32t() for _ in range(2)]; Ts = [f32t() for _ in range(2)]
    Ttb = P.sbuf([128, 128], BF16); r_Ttb = P.res()
    cols = P.sbuf([128, 8], F32); r_cols = P.res()
    qdT = P.sbuf([128, 128], BF16); r_qdT = P.res()
    kbg = P.sbuf([128, 128], BF16); r_kbg = P.res()
    kd = P.sbuf([128, 128], BF16); r_kd = P.res()
    nwT = P.sbuf([128, 128], BF16); r_nwT = P.res()
    u_sb = P.sbuf([128, 128], BF16); r_u = P.res()
    QKT = P.sbuf([128, 128], BF16); r_QKT = P.res()
    o_sb = [P.sbuf([128, 128], F32) for _ in range(2)]; r_osb = [P.res() for _ in range(2)]
    m = P.sbuf([128, 128], F32); r_m = P.res()
    m_bf = P.sbuf([128, 128], BF16); r_mbf = P.res()
    groups = [(0, 256), (256, 768), (768, 1280), (1280, 1792), (1792, 2304)]
    segs = [(2, 0, NCTX), (2 + NCTX + 4, NCTX, NLAT)]
    B = bank; RB = r_bk
    wi = [0]

    def project_conv_silu(g):
        wb = wi[0] % 2; wi[0] += 1
        P.dma("pool", Wp[wb][:], wqkv[:, g * 128:(g + 1) * 128].rearrange("(kc p) n -> p kc n", p=128), writes=[r_Wp[wb]])
        for gi, (t0, t1) in enumerate(groups):
            n = t1 - t0; pb = gi % 2
            for kc in range(16):
                P.op("pe", lambda e, kc=kc, pb=pb, wb=wb, t0=t0, t1=t1, n=n: e.matmul(ps_p[pb][:, :n], lhsT=Wp[wb][:, kc, :], rhs=uT[:, kc, t0:t1],
                                                                                start=(kc == 0), stop=(kc == 15)), reads=[r_Wp[wb], r_uT], writes=[r_pp[pb]], inc=(kc == 15))
            off = 2 + t0 if t0 < NCTX else 2 + NCTX + 4 + (t0 - NCTX)
            P.op("act", lambda e, pb=pb, n=n, off=off: e.activation(out=xp[:, off:off + n], in_=ps_p[pb][:, :n], func=AF.Copy), reads=[r_pp[pb]], writes=[r_xp])
        for (xo, to, L) in segs:
            P.op("dve", lambda e, xo=xo, to=to, L=L, g=g: e.tensor_scalar_mul(out=acc[:, to:to + L], in0=xp[:, xo - 2:xo - 2 + L], scalar1=convw[:, g, 0:1]),
                 reads=[r_xp, r_cw], writes=[r_acc])
            for j in range(1, 5):
                P.op("dve", lambda e, xo=xo, to=to, L=L, g=g, j=j: e.scalar_tensor_tensor(out=acc[:, to:to + L], in0=xp[:, xo - 2 + j:xo - 2 + j + L], scalar=convw[:, g, j:j + 1],
                                                                                        in1=acc[:, to:to + L], op0=ALU.mult, op1=ALU.add), reads=[r_xp, r_cw, r_acc], writes=[r_acc])
        P.op("act", lambda e: e.activation(out=ys[:], in_=acc[:], func=AF.Silu), reads=[r_acc], writes=[r_ys])

    def l2norm_to(dst, r_dst):
        for gi, (t0, t1) in enumerate(groups):
            n = t1 - t0; pb = gi % 2
            P.op("pool", lambda e, t0=t0, t1=t1, n=n: e.tensor_tensor(out=sq[:, :n], in0=ys[:, t0:t1], in1=ys[:, t0:t1], op=ALU.mult), reads=[r_ys], writes=[r_sq])
            P.op("pe", lambda e, pb=pb, n=n: e.matmul(ps_p[pb][:, :n], lhsT=onesf[:], rhs=sq[:, :n], start=True, stop=True), reads=[r_ones, r_sq], writes=[r_pp[pb]])
            P.op("act", lambda e, pb=pb, n=n: e.activation(out=rn[:, :n], in_=ps_p[pb][:, :n], func=AF.Sqrt, bias=cst[:, 0:1], scale=1.0), reads=[r_pp[pb], r_cst], writes=[r_rn])
            P.op("dve", lambda e, n=n: e.reciprocal(out=rn[:, :n], in_=rn[:, :n]), reads=[r_rn], writes=[r_rn])
            P.op("dve", lambda e, t0=t0, t1=t1, n=n: e.tensor_tensor(out=dst[:, t0:t1], in0=ys[:, t0:t1], in1=rn[:, :n], op=ALU.mult), reads=[r_ys, r_rn], writes=[r_dst])

    for hq in range(n_heads):
        project_conv_silu(hq); l2norm_to(qT, r_qT)
        project_conv_silu(16 + hq); l2norm_to(kT, r_kT)
        for c in range(NCH):
            P.op("pe", lambda e, c=c: e.matmul(B[0][:, :128], lhsT=kT[:, c * 128:(c + 1) * 128], rhs=identb[:], start=True, stop=True), reads=[r_kT, r_idb], writes=[RB[0]])
            P.op("act", lambda e, c=c: e.activation(out=kTok[:, c, :], in_=B[0][:, :128], func=AF.Copy), reads=[RB[0]], writes=[r_kTok])
        for e_ in range(2):
            hv = 2 * hq + e_
            project_conv_silu(32 + hv)
            for c in range(NCH):
                pb = c % 2
                P.op("pe", lambda e, c=c, pb=pb: e.matmul(B[pb][:, :128], lhsT=ys[:, c * 128:(c + 1) * 128], rhs=identf[:], start=True, stop=True), reads=[r_ys, r_idf], writes=[RB[pb]])
                P.op("act", lambda e, c=c, pb=pb, e_=e_, hv=hv: e.activation(out=vb[e_][:, c, :], in_=B[pb][:, :128], func=AF.Identity, scale=beta[:, c, hv:hv + 1]),
                     reads=[RB[pb], r_beta], writes=[r_vb[e_]])
        for e_ in range(2):
            hv = 2 * hq + e_
            for c in range(NCH):
                cs = slice(c * 128, (c + 1) * 128)
                ob = c % 2
                P.op("dve", lambda e, c=c, hv=hv: e.tensor_scalar_mul(out=LAU[:], in0=U[:], scalar1=la[:, c, hv:hv + 1]), reads=[r_U, r_la], writes=[r_LAU])
                P.op("pe", lambda e: e.matmul(B[0][:, :128], lhsT=onesf[:], rhs=LAU[:], start=True, stop=True), reads=[r_ones, r_LAU], writes=[RB[0]])
                P.op("pe", lambda e: e.matmul(B[1][:, :128], lhsT=LAU[:], rhs=onesf[:], start=True, stop=True), reads=[r_ones, r_LAU], writes=[RB[1]])
                P.op("act", lambda e: e.activation(out=cols[:, 0:1], in_=B[1][:, 0:1], func=AF.Copy), reads=[RB[1]], writes=[r_cols])
                P.op("act", lambda e: e.activation(out=cols[:, 5:6], in_=B[1][:, 0:1], func=AF.Identity, scale=-1.0), reads=[RB[1], r_cols], writes=[r_cols])
                P.op("act", lambda e: e.activation(out=Zabs[:], in_=B[0][:, :128], func=AF.Abs, bias=cols[:, 5:6], scale=1.0),
                     reads=[RB[0], r_cols], writes=[r_Z])
                P.op("act", lambda e: e.activation(out=decA[:], in_=Zabs[:], func=AF.Exp, scale=-1.0), reads=[r_Z], writes=[r_decA])
                P.op("dve", lambda e: e.tensor_scalar(out=cols[:, 3:4], in0=B[0][:, 127:128], scalar1=cols[:, 0:1], scalar2=None, op0=ALU.subtract),
                     reads=[RB[0], r_cols], writes=[r_cols])
                P.op("act", lambda e: e.activation(out=cols[:, 1:2], in_=cols[:, 0:1], func=AF.Exp), reads=[r_cols], writes=[r_cols])
                P.op("act", lambda e: e.activation(out=cols[:, 2:3], in_=B[0][:, 127:128], func=AF.Exp), reads=[RB[0], r_cols], writes=[r_cols])
                P.op("act", lambda e: e.activation(out=cols[:, 3:4], in_=cols[:, 3:4], func=AF.Exp), reads=[r_cols], writes=[r_cols])
                P.op("act", lambda e: e.activation(out=egR[:], in_=B[0][:, :128], func=AF.Exp), reads=[RB[0]], writes=[r_egR])
                P.op("dve", lambda e, cs=cs: e.tensor_tensor(out=qdT[:], in0=qT[:, cs], in1=egR[:], op=ALU.mult), reads=[r_qT, r_egR], writes=[r_qdT])
                P.op("dve", lambda e, c=c, hv=hv: e.tensor_tensor(out=cols[:, 4:5], in0=cols[:, 1:2], in1=beta[:, c, hv:hv + 1], op=ALU.mult), reads=[r_cols, r_beta], writes=[r_cols])
                P.op("pool", lambda e: e.tensor_tensor(out=t1[:], in0=decA[:], in1=msl[:], op=ALU.mult), reads=[r_decA, r_msl], writes=[r_t1])
                P.op("pool", lambda e: e.tensor_tensor(out=t2[:], in0=decA[:], in1=U[:], op=ALU.mult), reads=[r_decA, r_U], writes=[r_t2])
                X0, rX0 = Xs[0]; Y0, rY0 = Ys[0]; T0, rT0 = Ts[0]
                P.op("pe", lambda e, cs=cs: e.matmul(B[2][:, :128], lhsT=kT[:, cs], rhs=kT[:, cs], start=True, stop=True), reads=[r_kT], writes=[RB[2]])
                P.op("dve", lambda e, c=c, hv=hv: e.scalar_tensor_tensor(out=X0[:], in0=B[2][:, :128], scalar=negb[:, c, hv:hv + 1], in1=t1[:], op0=ALU.mult, op1=ALU.mult),
                     reads=[RB[2], r_negb, r_t1], writes=[rX0])
                P.op("pe", lambda e: e.matmul(B[3][:, :128], lhsT=X0[:], rhs=identf[:], start=True, stop=True), reads=[rX0, r_idf], writes=[RB[3]])
                P.op("act", lambda e: e.activation(out=Y0[:], in_=B[3][:, :128], func=AF.Copy), reads=[RB[3]], writes=[rY0])
                P.op("dve", lambda e: e.tensor_tensor(out=T0[:], in0=B[3][:, :128], in1=identf[:], op=ALU.add), reads=[RB[3], r_idf], writes=[rT0])
                cur = 0
                for lvl in range(1, 7):
                    Xc, rXc = Xs[cur]; Yc, rYc = Ys[cur]; Tc, rTc = Ts[cur]
                    Xn, rXn = Xs[1 - cur]; Yn, rYn = Ys[1 - cur]; Tn, rTn = Ts[1 - cur]
                    last = (lvl == 6)
                    P.op("pe", lambda e, Xc=Xc, Yc=Yc: e.matmul(B[4][:, :128], lhsT=Yc[:], rhs=Xc[:], start=True, stop=True), reads=[rXc, rYc], writes=[RB[4]])
                    P.op("act", lambda e, Xn=Xn: e.activation(out=Xn[:], in_=B[4][:, :128], func=AF.Copy), reads=[RB[4]], writes=[rXn])
                    if not last:
                        P.op("pe", lambda e, Xc=Xc, Yc=Yc: e.matmul(B[5][:, :128], lhsT=Xc[:], rhs=Yc[:], start=True, stop=True), reads=[rXc, rYc], writes=[RB[5]])
                        P.op("dve", lambda e, Yn=Yn: e.tensor_copy(out=Yn[:], in_=B[5][:, :128]), reads=[RB[5]], writes=[rYn])
                    P.op("pe", lambda e, Xn=Xn, Tc=Tc: e.matmul(B[3][:, :128], lhsT=Xn[:], rhs=Tc[:], start=True, stop=False), reads=[rXn, rTc], writes=[RB[3]], inc=False)
                    P.op("pe", lambda e, Tc=Tc: e.matmul(B[3][:, :128], lhsT=identf[:], rhs=Tc[:], start=False, stop=True), reads=[r_idf, rTc], writes=[RB[3]])
                    if not last:
                        P.op("dve", lambda e, Tn=Tn: e.tensor_copy(out=Tn[:], in_=B[3][:, :128]), reads=[RB[3]], writes=[rTn])
                    else:
                        P.op("dve", lambda e: e.tensor_copy(out=Ttb[:], in_=B[3][:, :128]), reads=[RB[3]], writes=[r_Ttb])
                    cur = 1 - cur
                P.op("act", lambda e, c=c: e.activation(out=kbg[:], in_=kTok[:, c, :], func=AF.Identity, scale=cols[:, 4:5]), reads=[r_kTok, r_cols], writes=[r_kbg])
                P.op("act", lambda e, c=c: e.activation(out=kd[:], in_=kTok[:, c, :], func=AF.Identity, scale=cols[:, 3:4]), reads=[r_kTok, r_cols], writes=[r_kd])
                P.op("pe", lambda e: e.matmul(B[4][:, :128], lhsT=kbg[:], rhs=Ttb[:], start=True, stop=True), reads=[r_kbg, r_Ttb], writes=[RB[4]])
                P.op("act", lambda e: e.activation(out=nwT[:], in_=B[4][:, :128], func=AF.Identity, scale=-1.0), reads=[RB[4]], writes=[r_nwT])
                P.op("pe", lambda e, c=c, e_=e_: e.matmul(B[5][:, :128], lhsT=Ttb[:], rhs=vb[e_][:, c, :], start=True, stop=(c == 0)), reads=[r_Ttb, r_vb[e_]], writes=[RB[5]], inc=(c == 0))
                if c > 0:
                    P.op("pe", lambda e: e.matmul(B[5][:, :128], lhsT=nwT[:], rhs=m_bf[:], start=False, stop=True), reads=[r_nwT, r_mbf], writes=[RB[5]])
                P.op("act", lambda e: e.activation(out=u_sb[:], in_=B[5][:, :128], func=AF.Copy), reads=[RB[5]], writes=[r_u])
                P.op("pe", lambda e, cs=cs: e.matmul(B[2][:, :128], lhsT=kT[:, cs], rhs=qT[:, cs], start=True, stop=True), reads=[r_kT, r_qT], writes=[RB[2]])
                P.op("dve", lambda e: e.tensor_tensor(out=QKT[:], in0=B[2][:, :128], in1=t2[:], op=ALU.mult), reads=[RB[2], r_t2], writes=[r_QKT])
                if c > 0:
                    P.op("pe", lambda e: e.matmul(B[0][:, :128], lhsT=qdT[:], rhs=m_bf[:], start=True, stop=False), reads=[r_qdT, r_mbf], writes=[RB[0]], inc=False)
                P.op("pe", lambda e, c=c: e.matmul(B[0][:, :128], lhsT=QKT[:], rhs=u_sb[:], start=(c == 0), stop=True), reads=[r_QKT, r_u], writes=[RB[0]])
                P.op("act", lambda e, ob=ob: e.activation(out=o_sb[ob][:], in_=B[0][:, :128], func=AF.Identity, scale=128.0 ** -0.5), reads=[RB[0]], writes=[r_osb[ob]])
                P.dma("sp", o_d[c * 128:(c + 1) * 128, hv, :], o_sb[ob][:], reads=[r_osb[ob]])
                if c < NCH - 1:
                    P.op("pe", lambda e: e.matmul(B[1][:, :128], lhsT=kd[:], rhs=u_sb[:], start=True, stop=True), reads=[r_kd, r_u], writes=[RB[1]])
                    if c == 0:
                        P.op("dve", lambda e: e.tensor_copy(out=m[:], in_=B[1][:, :128]), reads=[RB[1]], writes=[r_m])
                    else:
                        P.op("dve", lambda e: e.scalar_tensor_tensor(out=m[:], in0=m[:], scalar=cols[:, 2:3], in1=B[1][:, :128], op0=ALU.mult, op1=ALU.add),
                             reads=[r_m, r_cols, RB[1]], writes=[r_m])
                    P.op("pool", lambda e: e.tensor_copy(out=m_bf[:], in_=m[:]), reads=[r_m], writes=[r_mbf])
    P.finish()
    return nc


def run_gdn(inputs, mod, hcat, n_heads=16):
    nc = build_gdn(n_heads)
    w = inputs["gdn_w_in"][0]
    wqkv = np.ascontiguousarray(w[:, 0:8192])
    kk = np.arange(128)[:, None]; ii = np.arange(128)[None, :]
    U = (kk <= ii).astype(np.float32)
    msl = (kk > ii).astype(np.float32)
    in_maps = []
    for core in range(8):
        b, d = core // 2, core % 2
        h = hcat[b]
        if d:
            h = _seg_flip_np(h)
        cw = inputs["gdn_conv"][0]
        if d:
            cw = cw[::-1]
        convw = np.ascontiguousarray(cw.T.reshape(64, 128, 5).transpose(1, 0, 2))
        wba = np.ascontiguousarray(np.concatenate([w[:, 12288 + d * 32:12288 + (d + 1) * 32], w[:, 12352 + d * 32:12352 + (d + 1) * 32]], 1))
        in_maps.append({
            "hT": np.ascontiguousarray(h.T), "msc": _msc(mod, 1, b, 1, 0), "wqkv": wqkv, "wba": wba, "convw": convw,
            "alog": np.ascontiguousarray(inputs["gdn_a_log"][0, d][None, :]), "dtb": np.ascontiguousarray(inputs["gdn_dt_bias"][0, d][None, :]),
            "identf": np.eye(128, dtype=np.float32), "U": U, "msl": msl})
    res = run_bass_kernel_spmd(nc, in_maps, core_ids=list(range(8)))
    out = np.zeros((4, 2, NTOT, 32, 128), np.float32)
    for core in range(8):
        b, d = core // 2, core % 2
        o = res.results[core]["o"]
        out[b, d] = _seg_flip_np(o) if d else o
    return out


def kernel(**inputs):
    inputs = {k: np.asarray(v, dtype=np.float32) for k, v in inputs.items()}
    mod = run_ada(inputs)
    hcat = np.concatenate([inputs["ctx"], inputs["x"]], axis=1)
    o = run_ret(inputs, mod, hcat)
    h1 = run_blk(inputs, mod, 0, o.reshape(4, 2, NTOT, 4096), hcat)
    o2 = run_gdn(inputs, mod, h1)
    out = run_blk(inputs, mod, 1, o2.reshape(4, 2, NTOT, 4096), h1)
    return out.astype(np.float32)
```

```python
import contextlib
import numpy as np
import concourse.bass as bass
import concourse.mybir as mybir
from concourse.bass_utils import run_bass_kernel_spmd

F32 = mybir.dt.float32
BF16 = mybir.dt.bfloat16
AF = mybir.ActivationFunctionType
ALU = mybir.AluOpType
AX = mybir.AxisListType

D = 2048
NCTX = 256
NLAT = 2048
NTOT = NCTX + NLAT
DFF = 5632
ALPHA = 4.0 ** 0.25
LN_EPS = 1e-5


class Res:
    __slots__ = ("w", "rd", "name", "excl")

    def __init__(self, name="", excl=False):
        self.w = []
        self.rd = {}
        self.name = name
        self.excl = excl


class Prog:
    ENG = ("pe", "act", "dve", "pool", "sp")

    def __init__(self, nc, n_slots=32):
        self.nc = nc
        self.stack = contextlib.ExitStack()
        self.sem = {}
        self.cnt = {}
        self.pending = {e: False for e in self.ENG}
        self.seen = {e: {} for e in self.ENG}
        self.streams = {e: [] for e in self.ENG}
        for e in self.ENG:
            if e == "sp":
                continue
            self.sem[e] = self.stack.enter_context(nc.semaphore("s_" + e))
            self.cnt[e] = 0
        self.n_slots = n_slots
        for i in range(n_slots):
            n = "d%d" % i
            self.sem[n] = self.stack.enter_context(nc.semaphore("s_" + n))
            self.cnt[n] = 0
        self.slot_i = 0
        self.slot_p = 0
        self.n_inst = {e: 0 for e in self.ENG}
        self._uid = 0

    def sbuf(self, shape, dtype, name=None):
        self._uid += 1
        return self.stack.enter_context(self.nc.sbuf_tensor(name or ("t%d" % self._uid), list(shape), dtype))

    def psum(self, shape, dtype, name=None):
        self._uid += 1
        return self.stack.enter_context(self.nc.psum_tensor(name or ("p%d" % self._uid), list(shape), dtype))

    def res(self, name="", excl=False):
        return Res(name, excl)

    def _need(self, eng, reads, writes):
        need = {}
        for r in reads:
            for (s, v) in r.w:
                if need.get(s, 0) < v:
                    need[s] = v
        for r in writes:
            for (s, v) in r.w:
                if need.get(s, 0) < v:
                    need[s] = v
            for s, v in r.rd.items():
                if need.get(s, 0) < v:
                    need[s] = v
        seen = self.seen[eng]
        for s, v in need.items():
            if s == "pe" and eng == "pe":
                continue
            if seen.get(s, 0) >= v:
                continue
            seen[s] = v
            sem = self.sem[s]
            self.streams[eng].append(lambda e, sem=sem, v=v: e.wait_ge(sem, v))

    def op(self, eng, fn, reads=(), writes=(), inc=True):
        if eng != "pe":
            ex = [r for r in reads if r.excl]
            if ex:
                reads = [r for r in reads if not r.excl]
                writes = list(writes) + ex
        self._need(eng, reads, writes)
        self.n_inst[eng] += 1
        if inc:
            self.cnt[eng] += 1
            val = self.cnt[eng]
            sem = self.sem[eng]
            self.streams[eng].append(lambda e, fn=fn, sem=sem: fn(e).then_inc(sem, 1))
            self.pending[eng] = False
        else:
            val = self.cnt[eng] + 1
            self.streams[eng].append(lambda e, fn=fn: fn(e))
            self.pending[eng] = True
        for r in reads:
            if r.rd.get(eng, 0) < val:
                r.rd[eng] = val
        for r in writes:
            r.w = [(eng, val)]
            r.rd = {}
        return val

    def dma(self, q, out, in_, reads=(), writes=(), **kw):
        half = self.n_slots // 2
        if q == "pool":
            s = "d%d" % (half + self.slot_p)
            self.slot_p = (self.slot_p + 1) % half
        else:
            s = "d%d" % self.slot_i
            self.slot_i = (self.slot_i + 1) % half
        self._need(q, reads, writes)
        seen = self.seen[q]
        if seen.get(s, 0) < self.cnt[s]:
            seen[s] = self.cnt[s]
            self.streams[q].append(lambda e, sem=self.sem[s], v=self.cnt[s]: e.wait_ge(sem, v))
        self.cnt[s] += 16
        val = self.cnt[s]
        sem = self.sem[s]
        self.streams[q].append(lambda e, sem=sem, out=out, in_=in_, kw=kw: e.dma_start(out=out, in_=in_, **kw).then_inc(sem, 16))
        self.n_inst[q] += 1
        for r in reads:
            if r.rd.get(s, 0) < val:
                r.rd[s] = val
        for r in writes:
            r.w = [(s, val)]
            r.rd = {}
        return (s, val)

    def finish(self):
        for e in self.ENG:
            assert not self.pending[e], "engine %s has pending un-signalled ops" % e
        for i in range(self.n_slots):
            s = "d%d" % i
            if self.cnt[s] > 0:
                self.streams["sp"].append(lambda e, sem=self.sem[s], v=self.cnt[s]: e.wait_ge(sem, v))
        for en in ("pe", "act", "dve", "pool"):
            if self.cnt[en] > 0:
                self.streams["sp"].append(lambda e, sem=self.sem[en], v=self.cnt[en]: e.wait_ge(sem, v))
        nc = self.nc
        streams = self.streams
        with nc.Block() as block:
            @block.sync
            def _(e):
                for f in streams["sp"]:
                    f(e)

            @block.tensor
            def _(e):
                for f in streams["pe"]:
                    f(e)

            @block.scalar
            def _(e):
                for f in streams["act"]:
                    f(e)

            @block.vector
            def _(e):
                for f in streams["dve"]:
                    f(e)

            @block.gpsimd
            def _(e):
                for f in streams["pool"]:
                    f(e)
        self.stack.close()


def new_nc():
    return bass.Bass("TRN2", target_bir_lowering=False)


def build_ada():
    nc = new_nc()
    cc = nc.dram_tensor("cc", [5, 2048], F32, kind="ExternalInput").ap()
    adaw = nc.dram_tensor("adaw", [2, 2048, 1536], F32, kind="ExternalInput").ap()
    adab = nc.dram_tensor("adab", [2, 1, 1536], F32, kind="ExternalInput").ap()
    ident_d = nc.dram_tensor("ident", [128, 128], F32, kind="ExternalInput").ap()
    mod = nc.dram_tensor("mod", [2, 5, 1536], F32, kind="ExternalOutput").ap()
    P = Prog(nc)
    cc_t = P.sbuf([5, 2048], F32); r_cc = P.res()
    sc = P.sbuf([5, 2048], F32); r_sc = P.res()
    ident = P.sbuf([128, 128], F32); r_id = P.res()
    ones = P.sbuf([1, 8], F32); r_ones = P.res()
    scT = P.sbuf([128, 16, 5], F32); r_scT = P.res()
    bias = P.sbuf([1, 2, 1536], F32); r_bias = P.res()
    P.dma("sp", cc_t[:], cc, writes=[r_cc])
    P.dma("sp", ident[:], ident_d, writes=[r_id])
    P.dma("sp", bias[:], adab.rearrange("l o n -> o l n"), writes=[r_bias])
    P.op("dve", lambda e: e.memset(ones[:], 1.0), writes=[r_ones])
    P.op("act", lambda e: e.activation(out=sc[:], in_=cc_t[:], func=AF.Silu), reads=[r_cc], writes=[r_sc])
    ps_t = P.psum([128, 16, 5], F32); r_pst = P.res()
    for kc in range(16):
        P.op("pe", lambda e, kc=kc: e.transpose(ps_t[:, kc, :], sc[:, kc * 128:(kc + 1) * 128], ident[:5, :5]),
             reads=[r_sc, r_id], writes=[r_pst], inc=(kc == 15))
    P.op("dve", lambda e: e.tensor_copy(out=scT[:], in_=ps_t[:]), reads=[r_pst], writes=[r_scT])
    wt = [P.sbuf([128, 16, 512], F32) for _ in range(3)]; r_wt = [P.res() for _ in range(3)]
    ps = [P.psum([5, 512], F32) for _ in range(2)]; r_ps = [P.res() for _ in range(2)]
    ot = [P.sbuf([5, 512], F32) for _ in range(2)]; r_ot = [P.res() for _ in range(2)]
    it = 0
    for l in range(2):
        for g in range(3):
            b = it % 3; pb = it % 2
            q = ["sp", "pool", "act"][it % 3]
            P.dma(q, wt[b][:], adaw[l, :, g * 512:(g + 1) * 512].rearrange("(kc p) n -> p kc n", p=128), writes=[r_wt[b]])
            for kc in range(16):
                P.op("pe", lambda e, kc=kc, b=b, pb=pb: e.matmul(ps[pb][:], lhsT=scT[:, kc, :], rhs=wt[b][:, kc, :], start=(kc == 0), stop=False),
                     reads=[r_scT, r_wt[b]], writes=[r_ps[pb]], inc=False)
            P.op("pe", lambda e, l=l, g=g, pb=pb: e.matmul(ps[pb][:], lhsT=ones[:, :5], rhs=bias[:, l, g * 512:(g + 1) * 512], start=False, stop=True),
                 reads=[r_ones, r_bias], writes=[r_ps[pb]], inc=True)
            P.op("dve", lambda e, pb=pb: e.tensor_copy(out=ot[pb][:], in_=ps[pb][:]), reads=[r_ps[pb]], writes=[r_ot[pb]])
            P.dma("sp", mod[l, :, g * 512:(g + 1) * 512], ot[pb][:], reads=[r_ot[pb]])
            it += 1
    P.finish()
    return nc


def emit_uT(P, hT, msc_d, uT, r_uT, T, segs):
    msc = P.sbuf([128, 16, 4], F32); r_msc = P.res()
    P.dma("sp", msc[:], msc_d, writes=[r_msc])
    P.op("dve", lambda e: e.tensor_scalar_add(out=msc[:, :, 0:1], in0=msc[:, :, 0:1], scalar1=1.0), reads=[r_msc], writes=[r_msc])
    P.op("dve", lambda e: e.tensor_scalar_add(out=msc[:, :, 2:3], in0=msc[:, :, 2:3], scalar1=1.0), reads=[r_msc], writes=[r_msc])
    stg = [P.sbuf([128, T], F32) for _ in range(2)]; r_stg = [P.res() for _ in range(2)]
    for kc in range(16):
        b = kc % 2
        P.dma(["sp", "act"][kc % 2], stg[b][:], hT[kc * 128:(kc + 1) * 128, :], writes=[r_stg[b]])
        for (t0, t1, w) in segs:
            P.op("act", lambda e, kc=kc, b=b, t0=t0, t1=t1, w=w: e.activation(
                out=uT[:, kc, t0:t1], in_=stg[b][:, t0:t1], func=AF.Identity,
                scale=msc[:, kc, 2 * w:2 * w + 1], bias=msc[:, kc, 2 * w + 1:2 * w + 2]),
                reads=[r_stg[b], r_msc], writes=[r_uT])


def build_ret():
    nc = new_nc()
    T = NTOT
    hT = nc.dram_tensor("hT", [D, T], F32, kind="ExternalInput").ap()
    msc_d = nc.dram_tensor("msc", [128, 16, 4], F32, kind="ExternalInput").ap()
    wq = nc.dram_tensor("wq", [D, 2048], F32, kind="ExternalInput").ap()
    wk = nc.dram_tensor("wk", [D, 2048], F32, kind="ExternalInput").ap()
    wv = nc.dram_tensor("wv", [D, 4096], F32, kind="ExternalInput").ap()
    cos_d = nc.dram_tensor("cosT", [128, NLAT], F32, kind="ExternalInput").ap()
    sin_d = nc.dram_tensor("sinT", [128, NLAT], F32, kind="ExternalInput").ap()
    mask_d = nc.dram_tensor("mask01", [128, 128], F32, kind="ExternalInput").ap()
    expo_d = nc.dram_tensor("expo", [128, 128], F32, kind="ExternalInput").ap()
    idx_d = nc.dram_tensor("idxs", [128, 4], F32, kind="ExternalInput").ap()
    rdec_d = nc.dram_tensor("rdec", [1, 8], F32, kind="ExternalInput").ap()
    identb_d = nc.dram_tensor("identb", [128, 128], F32, kind="ExternalInput").ap()
    o_d = nc.dram_tensor("o", [T, 8, 512], F32, kind="ExternalOutput").ap()
    P = Prog(nc)
    uT = P.sbuf([128, 16, T], BF16); r_uT = P.res()
    emit_uT(P, hT, msc_d, uT, r_uT, T, [(0, NCTX, 0), (NCTX, T, 1)])
    cosT = P.sbuf([128, NLAT], F32); r_cos = P.res()
    sinT = P.sbuf([128, NLAT], F32); r_sin = P.res()
    mask01 = P.sbuf([128, 128], F32); expo = P.sbuf([128, 128], F32); idxs = P.sbuf([128, 4], F32)
    rdec = P.sbuf([128, 8], F32); identf = P.sbuf([128, 128], F32); identb = P.sbuf([128, 128], BF16)
    r_c = P.res()
    P.dma("sp", cosT[:], cos_d, writes=[r_cos])
    P.dma("sp", sinT[:], sin_d, writes=[r_sin])
    P.dma("sp", mask01[:], mask_d, writes=[r_c])
    r_c2 = P.res(); r_c3 = P.res(); r_c4 = P.res(); r_c5 = P.res()
    P.dma("sp", expo[:], expo_d, writes=[r_c2])
    P.dma("sp", idxs[:], idx_d, writes=[r_c3])
    P.dma("sp", rdec[:], rdec_d.partition_broadcast(128), writes=[r_c4])
    P.dma("sp", identf[:], identb_d, writes=[r_c5])
    r_idb = P.res()
    P.op("dve", lambda e: e.tensor_copy(out=identb[:], in_=identf[:]), reads=[r_c5], writes=[r_idb])
    lg = P.sbuf([128, 8], F32); r_lg = P.res()
    P.op("act", lambda e: e.activation(out=lg[:], in_=rdec[:], func=AF.Exp), reads=[r_c4], writes=[r_lg])
    P.op("dve", lambda e: e.tensor_scalar_mul(out=lg[:], in0=lg[:], scalar1=-1.0), reads=[r_lg], writes=[r_lg])
    maskT = P.sbuf([128, 8, 128], F32); r_mask = P.res()
    decs = P.sbuf([128, 8, 4], F32); r_decs = P.res()
    for h in range(8):
        P.op("act", lambda e, h=h: e.activation(out=maskT[:, h, :], in_=expo[:], func=AF.Exp, scale=lg[:, h:h + 1]),
             reads=[r_c2, r_lg], writes=[r_mask])
        P.op("dve", lambda e, h=h: e.scalar_tensor_tensor(out=maskT[:, h, :], in0=maskT[:, h, :], scalar=1.0 / 16.0, in1=mask01[:],
                                                          op0=ALU.mult, op1=ALU.mult), reads=[r_mask, r_c], writes=[r_mask])
        P.op("act", lambda e, h=h: e.activation(out=decs[:, h, 0:3], in_=idxs[:, 0:3], func=AF.Exp, scale=lg[:, h:h + 1]),
             reads=[r_c3, r_lg], writes=[r_decs])
        P.op("dve", lambda e, h=h: e.tensor_scalar_mul(out=decs[:, h, 0:1], in0=decs[:, h, 0:1], scalar1=1.0 / 16.0),
             reads=[r_decs], writes=[r_decs])
    qT = P.sbuf([128, 2, T], BF16); r_qT = P.res()
    kT = P.sbuf([128, 2, T], BF16); r_kT = P.res()
    vt = P.sbuf([128, 18, 512], BF16); r_v = P.res()
    wq_t = [P.sbuf([128, 16, 256], BF16)] * 2; r_wq = [P.res()] * 2
    wk_t = [P.sbuf([128, 16, 256], BF16)] * 2; r_wk = [P.res()] * 2
    wv_t = [P.sbuf([128, 16, 512], BF16)] * 2; r_wv = [P.res()] * 2
    ps_a = [P.psum([128, 512], F32) for _ in range(2)]; r_psa = [P.res() for _ in range(2)]
    ps_S = P.psum([128, 512], F32); r_psS = P.res()
    ps_kt = P.psum([128, 1024], BF16); r_pskt = P.res()
    ps_o = P.psum([128, 512], F32); r_pso = P.res()
    ps_i = P.psum([128, 512], F32); r_psi = P.res()
    ps_s = [P.psum([128, 512], F32) for _ in range(2)]; r_pss = [P.res() for _ in range(2)]
    tmp = [P.sbuf([128, 512], F32) for _ in range(4)]; r_tmp = [P.res() for _ in range(4)]
    AT = P.sbuf([128, 128], BF16); r_AT = P.res()
    kd = P.sbuf([128, 256], BF16); r_kd = P.res()
    o_sb = [P.sbuf([128, 512], F32) for _ in range(2)]; r_osb = [P.res() for _ in range(2)]
    st = P.sbuf([128, 2, 512], F32); r_st = P.res()
    st_bf = P.sbuf([128, 2, 512], BF16); r_stbf = P.res()
    groups = [(0, 256, 0), (256, 768, 1), (768, 1280, 1), (1280, 1792, 1), (1792, 2304, 1)]

    def load_w(h):
        b = h % 2
        P.dma("pool", wq_t[b][:], wq[:, h * 256:(h + 1) * 256].rearrange("(kc p) n -> p kc n", p=128), writes=[r_wq[b]])
        P.dma("pool", wk_t[b][:], wk[:, h * 256:(h + 1) * 256].rearrange("(kc p) n -> p kc n", p=128), writes=[r_wk[b]])
        P.dma("pool", wv_t[b][:], wv[:, h * 512:(h + 1) * 512].rearrange("(kc p) n -> p kc n", p=128), writes=[r_wv[b]])

    for h in range(8):
        b = h % 2
        load_w(h)
        for (w_t, r_w, dst, r_dst) in ((wq_t[b], r_wq[b], qT, r_qT), (wk_t[b], r_wk[b], kT, r_kT)):
            for (t0, t1, lat) in groups:
                n = t1 - t0
                for dc in range(2):
                    for kc in range(16):
                        P.op("pe", lambda e, dc=dc, kc=kc, w_t=w_t, t0=t0, t1=t1, n=n: e.matmul(
                            ps_a[dc][:, :n], lhsT=w_t[:, kc, dc * 128:(dc + 1) * 128], rhs=uT[:, kc, t0:t1],
                            start=(kc == 0), stop=(kc == 15)), reads=[r_w, r_uT], writes=[r_psa[dc]], inc=(kc == 15))
                if not lat:
                    for dc in range(2):
                        P.op("act", lambda e, dc=dc, dst=dst, t0=t0, t1=t1, n=n: e.activation(out=dst[:, dc, t0:t1], in_=ps_a[dc][:, :n], func=AF.Copy),
                             reads=[r_psa[dc]], writes=[r_dst])
                else:
                    l0 = t0 - NCTX; l1 = t1 - NCTX
                    P.op("dve", lambda e, l0=l0, l1=l1: e.tensor_tensor(out=tmp[0][:], in0=ps_a[0][:], in1=cosT[:, l0:l1], op=ALU.mult),
                         reads=[r_psa[0], r_cos], writes=[r_tmp[0]])
                    P.op("dve", lambda e, l0=l0, l1=l1: e.tensor_tensor(out=tmp[1][:], in0=ps_a[1][:], in1=sinT[:, l0:l1], op=ALU.mult),
                         reads=[r_psa[1], r_sin], writes=[r_tmp[1]])
                    P.op("dve", lambda e, l0=l0, l1=l1: e.tensor_tensor(out=tmp[2][:], in0=ps_a[0][:], in1=sinT[:, l0:l1], op=ALU.mult),
                         reads=[r_psa[0], r_sin], writes=[r_tmp[2]])
                    P.op("dve", lambda e, l0=l0, l1=l1: e.tensor_tensor(out=tmp[3][:], in0=ps_a[1][:], in1=cosT[:, l0:l1], op=ALU.mult),
                         reads=[r_psa[1], r_cos], writes=[r_tmp[3]])
                    P.op("pool", lambda e, dst=dst, t0=t0, t1=t1: e.tensor_tensor(out=dst[:, 0, t0:t1], in0=tmp[0][:], in1=tmp[1][:], op=ALU.subtract),
                         reads=[r_tmp[0], r_tmp[1]], writes=[r_dst])
                    P.op("pool", lambda e, dst=dst, t0=t0, t1=t1: e.tensor_tensor(out=dst[:, 1, t0:t1], in0=tmp[2][:], in1=tmp[3][:], op=ALU.add),
                         reads=[r_tmp[2], r_tmp[3]], writes=[r_dst])
        for c in range(18):
            pb = c % 2
            for kc in range(16):
                P.op("pe", lambda e, c=c, kc=kc, pb=pb, b=b: e.matmul(ps_a[pb][:], lhsT=uT[:, kc, c * 128:(c + 1) * 128], rhs=wv_t[b][:, kc, :],
                                                                 start=(kc == 0), stop=(kc == 15)), reads=[r_wv[b], r_uT], writes=[r_psa[pb]], inc=(kc == 15))
            P.op("act", lambda e, c=c, pb=pb: e.activation(out=vt[:, c, :], in_=ps_a[pb][:], func=AF.Copy), reads=[r_psa[pb]], writes=[r_v])
        for c in range(18):
            cs = slice(c * 128, (c + 1) * 128)
            ob = c % 2
            for dc in range(2):
                P.op("pe", lambda e, dc=dc, cs=cs: e.matmul(ps_S[:, :128], lhsT=kT[:, dc, cs], rhs=qT[:, dc, cs], start=(dc == 0), stop=(dc == 1)),
                     reads=[r_kT, r_qT], writes=[r_psS], inc=(dc == 1))
            P.op("dve", lambda e, h=h: e.tensor_tensor(out=AT[:], in0=ps_S[:, :128], in1=maskT[:, h, :], op=ALU.mult),
                 reads=[r_psS, r_mask], writes=[r_AT])
            P.op("pe", lambda e, c=c: e.matmul(ps_o[:], lhsT=AT[:], rhs=vt[:, c, :], start=True, stop=True), reads=[r_AT, r_v], writes=[r_pso])
            P.op("act", lambda e, ob=ob: e.activation(out=o_sb[ob][:], in_=ps_o[:], func=AF.Copy), reads=[r_pso], writes=[r_osb[ob]])
            if c > 0:
                for dc in range(2):
                    P.op("pe", lambda e, dc=dc, cs=cs: e.matmul(ps_i[:], lhsT=qT[:, dc, cs], rhs=st_bf[:, dc, :], start=(dc == 0), stop=(dc == 1)),
                         reads=[r_qT, r_stbf], writes=[r_psi], inc=(dc == 1))
                P.op("dve", lambda e, ob=ob, h=h: e.scalar_tensor_tensor(out=o_sb[ob][:], in0=ps_i[:], scalar=decs[:, h, 0:1], in1=o_sb[ob][:],
                                                                         op0=ALU.mult, op1=ALU.add), reads=[r_psi, r_decs, r_osb[ob]], writes=[r_osb[ob]])
            P.dma("sp", o_d[c * 128:(c + 1) * 128, h, :], o_sb[ob][:], reads=[r_osb[ob]])
            if c < 17:
                for dc in range(2):
                    P.op("pe", lambda e, dc=dc, cs=cs: e.transpose(ps_kt[:, dc * 128:(dc + 1) * 128], kT[:, dc, cs], identb[:]),
                         reads=[r_kT, r_idb], writes=[r_pskt], inc=(dc == 1))
                P.op("act", lambda e, h=h: e.activation(out=kd[:], in_=ps_kt[:, :256], func=AF.Identity, scale=decs[:, h, 1:2]),
                     reads=[r_pskt, r_decs], writes=[r_kd])
                for dc in range(2):
                    P.op("pe", lambda e, dc=dc, c=c: e.matmul(ps_s[dc][:], lhsT=kd[:, dc * 128:(dc + 1) * 128], rhs=vt[:, c, :], start=True, stop=True),
                         reads=[r_kd, r_v], writes=[r_pss[dc]])
                    if c == 0:
                        P.op("dve", lambda e, dc=dc: e.tensor_copy(out=st[:, dc, :], in_=ps_s[dc][:]), reads=[r_pss[dc]], writes=[r_st])
                    else:
                        P.op("dve", lambda e, dc=dc, h=h: e.scalar_tensor_tensor(out=st[:, dc, :], in0=st[:, dc, :], scalar=decs[:, h, 2:3], in1=ps_s[dc][:],
                                                                                 op0=ALU.mult, op1=ALU.add), reads=[r_st, r_decs, r_pss[dc]], writes=[r_st])
                P.op("pool", lambda e: e.tensor_copy(out=st_bf[:], in_=st[:]), reads=[r_st], writes=[r_stbf])
    P.finish()
    return nc


def _seg_flip_np(t):
    return np.concatenate([t[:NCTX][::-1], t[NCTX:][::-1]], axis=0)


def _rope_tables():
    rows = NLAT // 64
    row = np.repeat(np.arange(rows, dtype=np.float32), 64)
    col = np.tile(np.arange(64, dtype=np.float32), rows)
    n_freq = 64
    inv_freq = (np.float32(10000.0) ** (-np.arange(n_freq, dtype=np.float32) / np.float32(n_freq))).astype(np.float32)
    ang = np.concatenate([row[:, None] * inv_freq, col[:, None] * inv_freq], axis=-1).astype(np.float32)
    return np.cos(ang).astype(np.float32), np.sin(ang).astype(np.float32)


def _msc(mod, layer, b, j_scale, j_shift):
    m = mod[layer].reshape(5, 6, D)
    cols = [m[4, j_scale], m[4, j_shift], m[b, j_scale], m[b, j_shift]]
    a = np.stack(cols, axis=-1)
    return np.ascontiguousarray(a.reshape(16, 128, 4).transpose(1, 0, 2))


def run_ada(inputs):
    nc = build_ada()
    cc = np.concatenate([inputs["c"], inputs["c_ctx"][None]], 0).astype(np.float32)
    in_maps = []
    for i in range(8):
        in_maps.append({"cc": cc, "adaw": np.ascontiguousarray(inputs["ada_w"][:, :, i * 1536:(i + 1) * 1536]),
                        "adab": np.ascontiguousarray(inputs["ada_b"][:, None, i * 1536:(i + 1) * 1536]),
                        "ident": np.eye(128, dtype=np.float32)})
    res = run_bass_kernel_spmd(nc, in_maps, core_ids=list(range(8)))
    return np.concatenate([r["mod"] for r in res.results], axis=2)


def run_ret(inputs, mod, hcat):
    nc = build_ret()
    cos, sin = _rope_tables()
    w = inputs["ret_w_in"][0]
    jj = np.arange(128)[:, None]; ii = np.arange(128)[None, :]
    expo = np.maximum(ii - jj, 0).astype(np.float32)
    idxs = np.stack([np.arange(128) + 1.0, 127.0 - np.arange(128), np.full(128, 128.0), np.zeros(128)], -1).astype(np.float32)
    in_maps = []
    for core in range(8):
        b, d = core // 2, core % 2
        h = hcat[b]
        if d:
            h = _seg_flip_np(h)
        cs, sn = (cos[::-1], sin[::-1]) if d else (cos, sin)
        mask01 = (ii > jj) if d else (ii >= jj)
        in_maps.append({
            "hT": np.ascontiguousarray(h.T), "msc": _msc(mod, 0, b, 1, 0),
            "wq": np.ascontiguousarray(w[:, 0:2048]), "wk": np.ascontiguousarray(w[:, 2048:4096]),
            "wv": np.ascontiguousarray(w[:, 4096:8192]),
            "cosT": np.ascontiguousarray(cs.T), "sinT": np.ascontiguousarray(sn.T),
            "mask01": mask01.astype(np.float32), "expo": expo, "idxs": idxs,
            "rdec": np.ascontiguousarray(inputs["ret_decay"][0, d][None, :]),
            "identb": np.eye(128, dtype=np.float32)})
    res = run_bass_kernel_spmd(nc, in_maps, core_ids=list(range(8)))
    out = np.zeros((4, 2, NTOT, 8, 512), np.float32)
    for core in range(8):
        b, d = core // 2, core % 2
        o = res.results[core]["o"]
        out[b, d] = _seg_flip_np(o) if d else o
    return out


def emit_rsqrt(P, out_ap, in_ap, scale, eps_ap, r_s):
    P.op("act", lambda e: e.activation(out=out_ap, in_=in_ap, func=AF.Sqrt, bias=eps_ap, scale=scale), reads=[r_s], writes=[r_s])
    P.op("dve", lambda e: e.reciprocal(out=out_ap, in_=out_ap), reads=[r_s], writes=[r_s])


def emit_ln(P, hh, r_hh, t, lnrow_d, gi, bi, scr):
    stats, mv, rstd, gbc, bbc, r_s, r_g, r_b, eps5 = scr
    for j in range(4):
        P.op("dve", lambda e, j=j: e.bn_stats(out=stats[:, j, :], in_=hh[:, t, j * 512:(j + 1) * 512]), reads=[r_hh[t]], writes=[r_s])
    P.op("dve", lambda e: e.bn_aggr(out=mv[:], in_=stats[:].rearrange("p a b -> p (a b)")), reads=[r_s], writes=[r_s])
    emit_rsqrt(P, rstd[:, 0:1], mv[:, 1:2], 1.0, eps5[:, 0:1], r_s)
    P.op("dve", lambda e: e.tensor_scalar(out=hh[:, t, :], in0=hh[:, t, :], scalar1=mv[:, 0:1], scalar2=rstd[:, 0:1], op0=ALU.subtract, op1=ALU.mult),
         reads=[r_s, r_hh[t]], writes=[r_hh[t]])
    for j in range(4):
        cs = slice(j * 512, (j + 1) * 512)
        P.dma("sp", gbc[:], lnrow_d[gi:gi + 1, cs].partition_broadcast(128), writes=[r_g])
        P.dma("sp", bbc[:], lnrow_d[bi:bi + 1, cs].partition_broadcast(128), writes=[r_b])
        P.op("pool", lambda e, cs=cs: e.tensor_tensor(out=hh[:, t, cs], in0=hh[:, t, cs], in1=gbc[:], op=ALU.mult), reads=[r_g, r_hh[t]], writes=[r_hh[t]])
        P.op("pool", lambda e, cs=cs: e.tensor_tensor(out=hh[:, t, cs], in0=hh[:, t, cs], in1=bbc[:], op=ALU.add), reads=[r_b, r_hh[t]], writes=[r_hh[t]])


def build_blk(layer):
    nc = new_nc()
    T = 1152 if layer == 0 else 1024
    nt = T // 128
    tgs = [(0, 384), (384, 768), (768, 1152)] if layer == 0 else [(0, 512), (512, 1024)]
    of_d = nc.dram_tensor("of", [T, 4096], F32, kind="ExternalInput").ap()
    ob_d = nc.dram_tensor("ob", [T, 4096], F32, kind="ExternalInput").ap()
    h_d = nc.dram_tensor("h", [T, D], F32, kind="ExternalInput").ap()
    hT_d = nc.dram_tensor("hT", [D, T], F32, kind="ExternalInput").ap()
    msc1_d = nc.dram_tensor("msc1", [128, 16, nt, 2], F32, kind="ExternalInput").ap()
    msc2_d = nc.dram_tensor("msc2", [128, 16, nt, 2], F32, kind="ExternalInput").ap()
    g1_d = nc.dram_tensor("g1", [nt, D], F32, kind="ExternalInput").ap()
    g2_d = nc.dram_tensor("g2", [nt, D], F32, kind="ExternalInput").ap()
    ln_d = nc.dram_tensor("lnrows", [4, D], F32, kind="ExternalInput").ap()
    wg_d = nc.dram_tensor("wg", [D, 4096], F32, kind="ExternalInput").ap()
    wo_d = nc.dram_tensor("wo", [4096, D], F32, kind="ExternalInput").ap()
    w1_d = nc.dram_tensor("w1", [D, 2 * DFF], F32, kind="ExternalInput").ap()
    w2_d = nc.dram_tensor("w2", [DFF, D], F32, kind="ExternalInput").ap()
    nw_d = nc.dram_tensor("nw", [1, 128], F32, kind="ExternalInput").ap()
    id_d = nc.dram_tensor("identf", [128, 128], F32, kind="ExternalInput").ap()
    out_d = nc.dram_tensor("out", [T, D], F32, kind="ExternalOutput").ap()
    P = Prog(nc)
    uT = P.sbuf([128, 16, T], BF16); r_uT = P.res()
    big = P.sbuf([128, 11, T], BF16); r_big = P.res()
    hh = P.sbuf([128, nt, D], F32); r_hh = [P.res() for _ in range(nt)]
    W = [P.sbuf([128, 16, 512], BF16) for _ in range(2)]; r_W = [P.res() for _ in range(2)]
    Wgu = [P.sbuf([128, 16, 256], BF16) for _ in range(2)]; r_Wg = [P.res() for _ in range(2)]; r_Wu = [P.res() for _ in range(2)]
    msc1 = P.sbuf([128, 16, nt, 2], F32); r_m1 = P.res()
    msc2 = P.sbuf([128, 16, nt, 2], F32); r_m2 = P.res()
    identf = P.sbuf([128, 128], F32); r_idf = P.res()
    identb = P.sbuf([128, 128], BF16); r_idb = P.res()
    nw = P.sbuf([128, 128], F32); r_nw = P.res()
    P.dma("sp", msc1[:], msc1_d, writes=[r_m1])
    P.dma("sp", msc2[:], msc2_d, writes=[r_m2])
    P.dma("sp", identf[:], id_d, writes=[r_idf])
    P.dma("sp", nw[:], nw_d.partition_broadcast(128), writes=[r_nw])
    P.op("dve", lambda e: e.tensor_copy(out=identb[:], in_=identf[:]), reads=[r_idf], writes=[r_idb])
    P.op("dve", lambda e: e.tensor_scalar_add(out=msc1[:, :, :, 0:1], in0=msc1[:, :, :, 0:1], scalar1=1.0), reads=[r_m1], writes=[r_m1])
    P.op("dve", lambda e: e.tensor_scalar_add(out=msc2[:, :, :, 0:1], in0=msc2[:, :, :, 0:1], scalar1=1.0), reads=[r_m2], writes=[r_m2])
    for t in range(nt):
        P.dma(["sp", "act"][t % 2], hh[:, t, :], h_d[t * 128:(t + 1) * 128, :], writes=[r_hh[t]])
    stg = P.sbuf([128, T], F32); r_stg = P.res()
    for kc in range(16):
        P.dma("sp", stg[:], hT_d[kc * 128:(kc + 1) * 128, :], writes=[r_stg])
        for t in range(nt):
            P.op("act", lambda e, kc=kc, t=t: e.activation(out=uT[:, kc, t * 128:(t + 1) * 128], in_=stg[:, t * 128:(t + 1) * 128], func=AF.Identity,
                                                           scale=msc1[:, kc, t, 0:1], bias=msc1[:, kc, t, 1:2]), reads=[r_stg, r_m1], writes=[r_uT])
    ps_m = [P.psum([128, 512], F32) for _ in range(2)]; r_psm = [P.res() for _ in range(2)]
    ps_u = [P.psum([128, 512], F32) for _ in range(2)]; r_psu = [P.res() for _ in range(2)]
    ps_tr = P.psum([128, 1024], BF16); r_pstr = P.res()
    ps_tf = [P.psum([128, 512], F32) for _ in range(2)]; r_pstf = [P.res() for _ in range(2)]
    sg = P.sbuf([128, 512], F32); r_sg = P.res()
    oft = P.sbuf([128, 512], F32); r_of = P.res()
    obt = P.sbuf([128, 512], F32); r_ob = P.res()
    xn = P.sbuf([128, 512], F32); r_xn = P.res()
    og = P.sbuf([128, 512], BF16); r_og = P.res()
    gbc = P.sbuf([128, 512], F32); r_gbc = P.res()
    tmp = P.sbuf([128, 512], F32); r_tmp = P.res()
    stats = P.sbuf([128, 4, 6], F32); mv = P.sbuf([128, 2], F32); rstd = P.sbuf([128, 4], F32); r_s = P.res()
    lg_t = P.sbuf([128, 512], F32); lb_t = P.sbuf([128, 512], F32); r_lg = P.res(); r_lb = P.res()
    eps5 = P.sbuf([128, 2], F32)
    P.op("dve", lambda e: e.memset(eps5[:, 0:1], LN_EPS), writes=[r_s])
    P.op("dve", lambda e: e.memset(eps5[:, 1:2], 1e-6), writes=[r_s])
    ln_scr = (stats, mv, rstd, lg_t, lb_t, r_s, r_lg, r_lb, eps5)
    mmi = [0]

    def accum_h(t, cs, ps, r_ps, g_d, first):
        P.dma("sp", gbc[:], g_d[t:t + 1, cs].partition_broadcast(128), writes=[r_gbc])
        P.op("dve", lambda e: e.tensor_tensor(out=tmp[:], in0=ps[:], in1=gbc[:], op=ALU.mult), reads=[r_ps, r_gbc], writes=[r_tmp])
        P.op("dve", lambda e: e.scalar_tensor_tensor(out=hh[:, t, cs], in0=hh[:, t, cs], scalar=(ALPHA if first else 1.0), in1=tmp[:],
                                                     op0=ALU.mult, op1=ALU.add), reads=[r_tmp, r_hh[t]], writes=[r_hh[t]])

    for kq in range(4):
        for cgi in range(2):
            cg = kq * 2 + cgi
            cols = slice(cg * 512, (cg + 1) * 512)
            wb = mmi[0] % 2; mmi[0] += 1
            P.dma("pool", W[wb][:], wg_d[:, cols].rearrange("(kc p) n -> p kc n", p=128), writes=[r_W[wb]])
            for t in range(nt):
                pb = t % 2
                ts_ = slice(t * 128, (t + 1) * 128)
                for kc in range(16):
                    P.op("pe", lambda e, kc=kc, pb=pb, wb=wb, ts_=ts_: e.matmul(ps_m[pb][:], lhsT=uT[:, kc, ts_], rhs=W[wb][:, kc, :],
                                                                          start=(kc == 0), stop=(kc == 15)), reads=[r_uT, r_W[wb]], writes=[r_psm[pb]], inc=(kc == 15))
                P.op("act", lambda e, pb=pb: e.activation(out=sg[:], in_=ps_m[pb][:], func=AF.Silu), reads=[r_psm[pb]], writes=[r_sg])
                P.dma("sp", oft[:], of_d[ts_, cols], writes=[r_of])
                P.dma("act", obt[:], ob_d[ts_, cols], writes=[r_ob])
                P.op("pool", lambda e: e.tensor_tensor(out=oft[:], in0=oft[:], in1=obt[:], op=ALU.add), reads=[r_of, r_ob], writes=[r_of])
                if layer == 0:
                    P.op("dve", lambda e: e.bn_stats(out=stats[:, 0, :], in_=oft[:]), reads=[r_of], writes=[r_s])
                    P.op("dve", lambda e: e.bn_aggr(out=mv[:], in_=stats[:, 0, :]), reads=[r_s], writes=[r_s])
                    emit_rsqrt(P, rstd[:, 0:1], mv[:, 1:2], 1.0, eps5[:, 0:1], r_s)
                    P.op("dve", lambda e: e.tensor_scalar(out=xn[:], in0=oft[:], scalar1=mv[:, 0:1], scalar2=rstd[:, 0:1], op0=ALU.subtract, op1=ALU.mult),
                         reads=[r_s, r_of], writes=[r_xn])
                else:
                    P.op("pool", lambda e: e.tensor_tensor(out=xn[:], in0=oft[:], in1=oft[:], op=ALU.mult), reads=[r_of], writes=[r_xn])
                    P.op("dve", lambda e: e.tensor_reduce(out=rstd[:], in_=xn[:].rearrange("p (a b) -> p a b", a=4), axis=AX.X, op=ALU.add),
                         reads=[r_xn], writes=[r_s])
                    emit_rsqrt(P, rstd[:], rstd[:], 1.0 / 128.0, eps5[:, 1:2], r_s)
                    for j in range(4):
                        P.op("dve", lambda e, j=j: e.scalar_tensor_tensor(out=xn[:, j * 128:(j + 1) * 128], in0=oft[:, j * 128:(j + 1) * 128],
                                                                          scalar=rstd[:, j:j + 1], in1=nw[:], op0=ALU.mult, op1=ALU.mult),
                             reads=[r_s, r_of, r_nw, r_xn], writes=[r_xn])
                P.op("dve", lambda e: e.tensor_tensor(out=og[:], in0=xn[:], in1=sg[:], op=ALU.mult), reads=[r_xn, r_sg], writes=[r_og])
                for j in range(4):
                    P.op("pe", lambda e, j=j: e.transpose(ps_tr[:, j * 128:(j + 1) * 128], og[:, j * 128:(j + 1) * 128], identb[:]),
                         reads=[r_og, r_idb], writes=[r_pstr], inc=(j == 3))
                P.op("act", lambda e, cgi=cgi, ts_=ts_: e.activation(out=big[:, cgi * 4:(cgi + 1) * 4, ts_],
                                                                    in_=ps_tr[:, :512].rearrange("p (a b) -> p a b", a=4), func=AF.Copy),
                     reads=[r_pstr], writes=[r_big])
        for cg2 in range(4):
            cs = slice(cg2 * 512, (cg2 + 1) * 512)
            wb = mmi[0] % 2; mmi[0] += 1
            P.dma("pool", W[wb][:, 0:8, :], wo_d[kq * 1024:(kq + 1) * 1024, cs].rearrange("(kc p) n -> p kc n", p=128), writes=[r_W[wb]])
            for t in range(nt):
                pb = t % 2
                ts_ = slice(t * 128, (t + 1) * 128)
                for kc in range(8):
                    P.op("pe", lambda e, kc=kc, pb=pb, wb=wb, ts_=ts_: e.matmul(ps_u[pb][:], lhsT=big[:, kc, ts_], rhs=W[wb][:, kc, :],
                                                                          start=(kc == 0), stop=(kc == 7)), reads=[r_big, r_W[wb]], writes=[r_psu[pb]], inc=(kc == 7))
                accum_h(t, cs, ps_u[pb], r_psu[pb], g1_d, kq == 0)
    for t in range(nt):
        emit_ln(P, hh, r_hh, t, ln_d, 0, 1, ln_scr)
        ts_ = slice(t * 128, (t + 1) * 128)
        for kg in range(4):
            pb = kg % 2
            for j in range(4):
                kc = kg * 4 + j
                P.op("pe", lambda e, kc=kc, j=j, pb=pb, t=t: e.transpose(ps_tf[pb][:, j * 128:(j + 1) * 128], hh[:, t, kc * 128:(kc + 1) * 128], identf[:]),
                     reads=[r_hh[t], r_idf], writes=[r_pstf[pb]], inc=(j == 3))
            for j in range(4):
                kc = kg * 4 + j
                P.op("act", lambda e, kc=kc, j=j, pb=pb, t=t, ts_=ts_: e.activation(out=uT[:, kc, ts_], in_=ps_tf[pb][:, j * 128:(j + 1) * 128], func=AF.Identity,
                                                                               scale=msc2[:, kc, t, 0:1], bias=msc2[:, kc, t, 1:2]),
                     reads=[r_pstf[pb], r_m2], writes=[r_uT])
    for fq in range(4):
        for fi in range(11):
            f = fq * 11 + fi
            wb = f % 2
            P.dma("pool", Wgu[wb][:, :, 0:128], w1_d[:, f * 128:(f + 1) * 128].rearrange("(kc p) n -> p kc n", p=128), writes=[r_Wg[wb]])
            P.dma("pool", Wgu[wb][:, :, 128:256], w1_d[:, DFF + f * 128:DFF + (f + 1) * 128].rearrange("(kc p) n -> p kc n", p=128), writes=[r_Wu[wb]])
            for gi_, (t0, t1) in enumerate(tgs):
                n = t1 - t0
                pb = gi_ % 2
                for kc in range(16):
                    P.op("pe", lambda e, kc=kc, pb=pb, wb=wb, t0=t0, t1=t1, n=n: e.matmul(ps_m[pb][:, :n], lhsT=Wgu[wb][:, kc, 0:128], rhs=uT[:, kc, t0:t1],
                                                                                    start=(kc == 0), stop=(kc == 15)), reads=[r_Wg[wb], r_uT], writes=[r_psm[pb]], inc=(kc == 15))
                for kc in range(16):
                    P.op("pe", lambda e, kc=kc, pb=pb, wb=wb, t0=t0, t1=t1, n=n: e.matmul(ps_u[pb][:, :n], lhsT=Wgu[wb][:, kc, 128:256], rhs=uT[:, kc, t0:t1],
                                                                                    start=(kc == 0), stop=(kc == 15)), reads=[r_Wu[wb], r_uT], writes=[r_psu[pb]], inc=(kc == 15))
                P.op("act", lambda e, pb=pb, n=n: e.activation(out=sg[:, :n], in_=ps_m[pb][:, :n], func=AF.Silu), reads=[r_psm[pb]], writes=[r_sg])
                P.op("dve", lambda e, pb=pb, n=n, fi=fi, t0=t0, t1=t1: e.tensor_tensor(out=big[:, fi, t0:t1], in0=ps_u[pb][:, :n], in1=sg[:, :n], op=ALU.mult),
                     reads=[r_psu[pb], r_sg], writes=[r_big])
        for cg2 in range(4):
            cs = slice(cg2 * 512, (cg2 + 1) * 512)
            wb = mmi[0] % 2; mmi[0] += 1
            P.dma("pool", W[wb][:, 0:11, :], w2_d[fq * 1408:(fq + 1) * 1408, cs].rearrange("(kc p) n -> p kc n", p=128), writes=[r_W[wb]])
            for t in range(nt):
                pb = t % 2
                ts_ = slice(t * 128, (t + 1) * 128)
                for kc in range(11):
                    P.op("pe", lambda e, kc=kc, pb=pb, wb=wb, ts_=ts_: e.matmul(ps_tf[pb][:], lhsT=big[:, kc, ts_], rhs=W[wb][:, kc, :],
                                                                          start=(kc == 0), stop=(kc == 10)), reads=[r_big, r_W[wb]], writes=[r_pstf[pb]], inc=(kc == 10))
                accum_h(t, cs, ps_tf[pb], r_pstf[pb], g2_d, fq == 0)
    for t in range(nt):
        emit_ln(P, hh, r_hh, t, ln_d, 2, 3, ln_scr)
        P.dma("sp", out_d[t * 128:(t + 1) * 128, :], hh[:, t, :], reads=[r_hh[t]])
    P.finish()
    return nc


def run_blk(inputs, mod, layer, o, hcat):
    nc = build_blk(layer)
    T = 1152 if layer == 0 else 1024
    nt = T // 128
    m = mod[layer].reshape(5, 6, D)
    if layer == 0:
        wg = np.ascontiguousarray(inputs["ret_w_in"][0][:, 8192:12288]); wo = inputs["ret_w_out"][0]
        nw = np.ones((1, 128), np.float32)
    else:
        wg = np.ascontiguousarray(inputs["gdn_w_in"][0][:, 8192:12288]); wo = inputs["gdn_w_out"][0]
        nw = np.ascontiguousarray(inputs["gdn_norm"][0][None, :])
    lnrows = np.stack([inputs["ln_g"][layer, 0], inputs["ln_b"][layer, 0], inputs["ln_g"][layer, 1], inputs["ln_b"][layer, 1]], 0)
    in_maps = []
    for core in range(8):
        b, s = core // 2, core % 2
        t0 = s * T + (0 if layer == 0 else NCTX)
        rows = [(4 if (t0 + t * 128) < NCTX else b) for t in range(nt)]

        def msc(js, jh):
            a = np.stack([np.stack([m[r, js], m[r, jh]], -1) for r in rows], 1)
            return np.ascontiguousarray(a.reshape(16, 128, nt, 2).transpose(1, 0, 2, 3))
        h = hcat[b, t0:t0 + T]
        in_maps.append({
            "of": np.ascontiguousarray(o[b, 0, t0:t0 + T].reshape(T, 4096)), "ob": np.ascontiguousarray(o[b, 1, t0:t0 + T].reshape(T, 4096)),
            "h": np.ascontiguousarray(h), "hT": np.ascontiguousarray(h.T),
            "msc1": msc(1, 0), "msc2": msc(4, 3),
            "g1": np.ascontiguousarray(np.stack([m[r, 2] for r in rows], 0)), "g2": np.ascontiguousarray(np.stack([m[r, 5] for r in rows], 0)),
            "lnrows": np.ascontiguousarray(lnrows), "wg": wg, "wo": wo, "w1": inputs["ffn_w_in"][layer], "w2": inputs["ffn_w_out"][layer],
            "nw": nw, "identf": np.eye(128, dtype=np.float32)})
    res = run_bass_kernel_spmd(nc, in_maps, core_ids=list(range(8)))
    if layer == 0:
        out = np.zeros((4, NTOT, D), np.float32)
        for core in range(8):
            b, s = core // 2, core % 2
            out[b, s * T:(s + 1) * T] = res.results[core]["out"]
    else:
        out = np.zeros((4, NLAT, D), np.float32)
        for core in range(8):
            b, s = core // 2, core % 2
            out[b, s * T:(s + 1) * T] = res.results[core]["out"]
    return out


def build_gdn(n_heads=16):
    nc = new_nc()
    T = NTOT
    NCH = T // 128
    hT = nc.dram_tensor("hT", [D, T], F32, kind="ExternalInput").ap()
    msc_d = nc.dram_tensor("msc", [128, 16, 4], F32, kind="ExternalInput").ap()
    wqkv = nc.dram_tensor("wqkv", [D, 8192], F32, kind="ExternalInput").ap()
    wba = nc.dram_tensor("wba", [D, 64], F32, kind="ExternalInput").ap()
    convw_d = nc.dram_tensor("convw", [128, 64, 5], F32, kind="ExternalInput").ap()
    alog_d = nc.dram_tensor("alog", [1, 32], F32, kind="ExternalInput").ap()
    dtb_d = nc.dram_tensor("dtb", [1, 32], F32, kind="ExternalInput").ap()
    id_d = nc.dram_tensor("identf", [128, 128], F32, kind="ExternalInput").ap()
    U_d = nc.dram_tensor("U", [128, 128], F32, kind="ExternalInput").ap()
    msl_d = nc.dram_tensor("msl", [128, 128], F32, kind="ExternalInput").ap()
    o_d = nc.dram_tensor("o", [T, 32, 128], F32, kind="ExternalOutput").ap()
    P = Prog(nc)
    uT = P.sbuf([128, 16, T], BF16); r_uT = P.res()
    emit_uT(P, hT, msc_d, uT, r_uT, T, [(0, NCTX, 0), (NCTX, T, 1)])
    identf = P.sbuf([128, 128], F32); r_idf = P.res()
    identb = P.sbuf([128, 128], BF16); r_idb = P.res()
    U = P.sbuf([128, 128], F32); r_U = P.res()
    msl = P.sbuf([128, 128], F32); r_msl = P.res()
    onesf = P.sbuf([128, 128], F32); r_ones = P.res()
    cst = P.sbuf([128, 2], F32); r_cst = P.res()
    convw = P.sbuf([128, 64, 5], F32); r_cw = P.res()
    alog = P.sbuf([128, 32], F32); r_al = P.res()
    dtb = P.sbuf([128, 32], F32); r_dtb = P.res()
    P.dma("sp", identf[:], id_d, writes=[r_idf])
    P.dma("sp", U[:], U_d, writes=[r_U])
    P.dma("sp", msl[:], msl_d, writes=[r_msl])
    P.dma("sp", convw[:], convw_d, writes=[r_cw])
    P.dma("sp", alog[:], alog_d.partition_broadcast(128), writes=[r_al])
    P.dma("sp", dtb[:], dtb_d.partition_broadcast(128), writes=[r_dtb])
    P.op("dve", lambda e: e.tensor_copy(out=identb[:], in_=identf[:]), reads=[r_idf], writes=[r_idb])
    P.op("dve", lambda e: e.memset(onesf[:], 1.0), writes=[r_ones])
    P.op("dve", lambda e: e.memset(cst[:, 0:1], 1e-6), writes=[r_cst])
    P.op("dve", lambda e: e.memset(cst[:, 1:2], 1.0), writes=[r_cst])
    P.op("act", lambda e: e.activation(out=alog[:], in_=alog[:], func=AF.Exp), reads=[r_al], writes=[r_al])
    bank = [P.psum([128, 512], F32) for _ in range(8)]; r_bk = [P.res(excl=True) for _ in range(8)]
    ps_p = [bank[0], bank[4]]; r_pp = [r_bk[0], r_bk[4]]
    bank_bf = bank[3]
    beta = P.sbuf([128, NCH, 32], F32); r_beta = P.res()
    negb = P.sbuf([128, NCH, 32], F32); r_negb = P.res()
    la = P.sbuf([128, NCH, 32], F32); r_la = P.res()
    wba_t = P.sbuf([128, 16, 64], BF16); r_wba = P.res()
    t32 = P.sbuf([128, 32], F32); r_t32 = P.res()
    P.dma("pool", wba_t[:], wba.rearrange("(kc p) n -> p kc n", p=128), writes=[r_wba])
    for c in range(NCH):
        pb = c % 2
        for kc in range(16):
            P.op("pe", lambda e, c=c, kc=kc, pb=pb: e.matmul(ps_p[pb][:, :64], lhsT=uT[:, kc, c * 128:(c + 1) * 128], rhs=wba_t[:, kc, :],
                                                       start=(kc == 0), stop=(kc == 15)), reads=[r_uT, r_wba], writes=[r_pp[pb]], inc=(kc == 15))
        P.op("act", lambda e, c=c, pb=pb: e.activation(out=beta[:, c, :], in_=ps_p[pb][:, 0:32], func=AF.Sigmoid), reads=[r_pp[pb]], writes=[r_beta])
        P.op("dve", lambda e, pb=pb: e.tensor_tensor(out=t32[:], in0=ps_p[pb][:, 32:64], in1=dtb[:], op=ALU.add), reads=[r_pp[pb], r_dtb], writes=[r_t32])
        P.op("act", lambda e: e.activation(out=t32[:], in_=t32[:], func=AF.Exp), reads=[r_t32], writes=[r_t32])
        P.op("act", lambda e: e.activation(out=t32[:], in_=t32[:], func=AF.Ln, bias=cst[:, 1:2], scale=1.0), reads=[r_t32, r_cst], writes=[r_t32])
        P.op("dve", lambda e, c=c: e.scalar_tensor_tensor(out=la[:, c, :], in0=t32[:], scalar=-1.0, in1=alog[:], op0=ALU.mult, op1=ALU.mult),
             reads=[r_t32, r_al], writes=[r_la])
    P.op("dve", lambda e: e.tensor_scalar_mul(out=negb[:], in0=beta[:], scalar1=-1.0), reads=[r_beta], writes=[r_negb])
    LP = 2 + NCTX + 2 + 2 + NLAT + 2
    xp = P.sbuf([128, LP], F32); r_xp = P.res()
    acc = P.sbuf([128, T], F32); r_acc = P.res()
    ys = P.sbuf([128, T], F32); r_ys = P.res()
    sq = P.sbuf([128, 512], F32); r_sq = P.res()
    rn = P.sbuf([128, 512], F32); r_rn = P.res()
    qT = P.sbuf([128, T], BF16); r_qT = P.res()
    kT = P.sbuf([128, T], BF16); r_kT = P.res()
    kTok = P.sbuf([128, NCH, 128], BF16); r_kTok = P.res()
    vb = [P.sbuf([128, NCH, 128], BF16) for _ in range(2)]; r_vb = [P.res() for _ in range(2)]
    Wp = [P.sbuf([128, 16, 128], BF16) for _ in range(2)]; r_Wp = [P.res() for _ in range(2)]
    P.op("pool", lambda e: e.memset(xp[:], 0.0), writes=[r_xp])
    ps_trb = P.psum([128, 1024], BF16) if False else None
    class NS:
        pass

    def f32t():
        return P.sbuf([128, 128], F32), P.res()

    def bf16t():
        return P.sbuf([128, 128], BF16), P.res()
    ST = []
    for s_ in range(2):
        S = NS()
        S.LAU, S.r_LAU = f32t(); S.Zabs, S.r_Z = f32t(); S.decA, S.r_decA = f32t(); S.egR, S.r_egR = f32t()
        S.t1, S.r_t1 = f32t(); S.t2, S.r_t2 = f32t()
        S.Xs = [f32t() for _ in range(2)]; S.Ys = [f32t() for _ in range(2)]; S.Ts = [f32t() for _ in range(2)]
        S.Ttb, S.r_Ttb = bf16t()
        S.cols = P.sbuf([128, 8], F32); S.r_cols = P.res()
        S.qdT, S.r_qdT = bf16t(); S.kbg, S.r_kbg = bf16t(); S.kd, S.r_kd = bf16t(); S.nwT, S.r_nwT = bf16t()
        S.u_sb, S.r_u = bf16t(); S.QKT, S.r_QKT = bf16t()
        S.o_sb = [P.sbuf([128, 128], F32) for _ in range(2)]; S.r_osb = [P.res() for _ in range(2)]
        S.m, S.r_m = f32t(); S.m_bf, S.r_mbf = bf16t()
        S.b = bank[4 * s_:4 * s_ + 4]; S.rb = r_bk[4 * s_:4 * s_ + 4]
        ST.append(S)
    groups = [(0, 256), (256, 768), (768, 1280), (1280, 1792), (1792, 2304)]
    segs = [(2, 0, NCTX), (2 + NCTX + 4, NCTX, NLAT)]
    B = [bank[1], bank[5]]; RB = [r_bk[1], r_bk[5]]
    wi = [0]

    def project_conv_silu(g):
        wb = wi[0] % 2; wi[0] += 1
        P.dma("pool", Wp[wb][:], wqkv[:, g * 128:(g + 1) * 128].rearrange("(kc p) n -> p kc n", p=128), writes=[r_Wp[wb]])
        for gi, (t0, t1) in enumerate(groups):
            n = t1 - t0; pb = gi % 2
            for kc in range(16):
                P.op("pe", lambda e, kc=kc, pb=pb, wb=wb, t0=t0, t1=t1, n=n: e.matmul(ps_p[pb][:, :n], lhsT=Wp[wb][:, kc, :], rhs=uT[:, kc, t0:t1],
                                                                                start=(kc == 0), stop=(kc == 15)), reads=[r_Wp[wb], r_uT], writes=[r_pp[pb]], inc=(kc == 15))
            off = 2 + t0 if t0 < NCTX else 2 + NCTX + 4 + (t0 - NCTX)
            P.op("act", lambda e, pb=pb, n=n, off=off: e.activation(out=xp[:, off:off + n], in_=ps_p[pb][:, :n], func=AF.Copy), reads=[r_pp[pb]], writes=[r_xp])
        for (xo, to, L) in segs:
            P.op("dve", lambda e, xo=xo, to=to, L=L, g=g: e.tensor_scalar_mul(out=acc[:, to:to + L], in0=xp[:, xo - 2:xo - 2 + L], scalar1=convw[:, g, 0:1]),
                 reads=[r_xp, r_cw], writes=[r_acc])
            for j in range(1, 5):
                P.op("dve", lambda e, xo=xo, to=to, L=L, g=g, j=j: e.scalar_tensor_tensor(out=acc[:, to:to + L], in0=xp[:, xo - 2 + j:xo - 2 + j + L], scalar=convw[:, g, j:j + 1],
                                                                                        in1=acc[:, to:to + L], op0=ALU.mult, op1=ALU.add), reads=[r_xp, r_cw, r_acc], writes=[r_acc])
        P.op("act", lambda e: e.activation(out=ys[:], in_=acc[:], func=AF.Silu), reads=[r_acc], writes=[r_ys])

    def l2norm_to(dst, r_dst):
        for gi, (t0, t1) in enumerate(groups):
            n = t1 - t0; pb = gi % 2
            P.op("pool", lambda e, t0=t0, t1=t1, n=n: e.tensor_tensor(out=sq[:, :n], in0=ys[:, t0:t1], in1=ys[:, t0:t1], op=ALU.mult), reads=[r_ys], writes=[r_sq])
            P.op("pe", lambda e, pb=pb, n=n: e.matmul(ps_p[pb][:, :n], lhsT=onesf[:], rhs=sq[:, :n], start=True, stop=True), reads=[r_ones, r_sq], writes=[r_pp[pb]])
            P.op("act", lambda e, pb=pb, n=n: e.activation(out=rn[:, :n], in_=ps_p[pb][:, :n], func=AF.Sqrt, bias=cst[:, 0:1], scale=1.0), reads=[r_pp[pb], r_cst], writes=[r_rn])
            P.op("dve", lambda e, n=n: e.reciprocal(out=rn[:, :n], in_=rn[:, :n]), reads=[r_rn], writes=[r_rn])
            P.op("dve", lambda e, t0=t0, t1=t1, n=n: e.tensor_tensor(out=dst[:, t0:t1], in0=ys[:, t0:t1], in1=rn[:, :n], op=ALU.mult), reads=[r_ys, r_rn], writes=[r_dst])

    for hq in range(n_heads):
        project_conv_silu(hq); l2norm_to(qT, r_qT)
        project_conv_silu(16 + hq); l2norm_to(kT, r_kT)
        for c in range(NCH):
            P.op("pe", lambda e, c=c: e.matmul(B[0][:, :128], lhsT=kT[:, c * 128:(c + 1) * 128], rhs=identb[:], start=True, stop=True), reads=[r_kT, r_idb], writes=[RB[0]])
            P.op("act", lambda e, c=c: e.activation(out=kTok[:, c, :], in_=B[0][:, :128], func=AF.Copy), reads=[RB[0]], writes=[r_kTok])
        for e_ in range(2):
            hv = 2 * hq + e_
            project_conv_silu(32 + hv)
            for c in range(NCH):
                pb = c % 2
                P.op("pe", lambda e, c=c, pb=pb: e.matmul(B[pb][:, :128], lhsT=ys[:, c * 128:(c + 1) * 128], rhs=identf[:], start=True, stop=True), reads=[r_ys, r_idf], writes=[RB[pb]])
                P.op("act", lambda e, c=c, pb=pb, e_=e_, hv=hv: e.activation(out=vb[e_][:, c, :], in_=B[pb][:, :128], func=AF.Identity, scale=beta[:, c, hv:hv + 1]),
                     reads=[RB[pb], r_beta], writes=[r_vb[e_]])
        def unit(e_, c, S):
            hv = 2 * hq + e_
            cs = slice(c * 128, (c + 1) * 128)
            ob = c % 2
            b, rb, cols, r_cols = S.b, S.rb, S.cols, S.r_cols
            R_ = b[0][:, 0:128]; G_ = b[1][:, 128:256]; Y2_ = b[1][:, 256:384]; M_ = b[1][:, 384:512]
            TT_ = b[2][:, 0:128]; UU_ = b[2][:, 128:256]; X2_ = b[3][:, 0:128]; W_ = b[3][:, 128:256]
            P.op("dve", lambda e: e.tensor_scalar_mul(out=S.LAU[:], in0=U[:], scalar1=la[:, c, hv:hv + 1]), reads=[r_U, r_la], writes=[S.r_LAU]); yield
            P.op("pe", lambda e: e.matmul(R_, lhsT=onesf[:], rhs=S.LAU[:], start=True, stop=True), reads=[r_ones, S.r_LAU], writes=[rb[0]]); yield
            P.op("pe", lambda e: e.matmul(b[1][:, 0:128], lhsT=S.LAU[:], rhs=onesf[:], start=True, stop=True), reads=[r_ones, S.r_LAU], writes=[rb[1]]); yield
            P.op("act", lambda e: e.activation(out=cols[:, 0:1], in_=b[1][:, 0:1], func=AF.Copy), reads=[rb[1]], writes=[r_cols]); yield
            P.op("act", lambda e: e.activation(out=cols[:, 5:6], in_=b[1][:, 0:1], func=AF.Identity, scale=-1.0), reads=[rb[1], r_cols], writes=[r_cols]); yield
            P.op("act", lambda e: e.activation(out=S.Zabs[:], in_=R_, func=AF.Abs, bias=cols[:, 5:6], scale=1.0), reads=[rb[0], r_cols], writes=[S.r_Z]); yield
            P.op("act", lambda e: e.activation(out=S.decA[:], in_=S.Zabs[:], func=AF.Exp, scale=-1.0), reads=[S.r_Z], writes=[S.r_decA]); yield
            P.op("dve", lambda e: e.tensor_scalar(out=cols[:, 3:4], in0=b[0][:, 127:128], scalar1=cols[:, 0:1], scalar2=None, op0=ALU.subtract),
                 reads=[rb[0], r_cols], writes=[r_cols]); yield
            P.op("act", lambda e: e.activation(out=cols[:, 1:2], in_=cols[:, 0:1], func=AF.Exp), reads=[r_cols], writes=[r_cols]); yield
            P.op("act", lambda e: e.activation(out=cols[:, 2:3], in_=b[0][:, 127:128], func=AF.Exp), reads=[rb[0], r_cols], writes=[r_cols]); yield
            P.op("act", lambda e: e.activation(out=cols[:, 3:4], in_=cols[:, 3:4], func=AF.Exp), reads=[r_cols], writes=[r_cols]); yield
            P.op("act", lambda e: e.activation(out=S.egR[:], in_=R_, func=AF.Exp), reads=[rb[0]], writes=[S.r_egR]); yield
            P.op("dve", lambda e: e.tensor_tensor(out=S.qdT[:], in0=qT[:, cs], in1=S.egR[:], op=ALU.mult), reads=[r_qT, S.r_egR], writes=[S.r_qdT]); yield
            P.op("dve", lambda e: e.tensor_tensor(out=cols[:, 4:5], in0=cols[:, 1:2], in1=beta[:, c, hv:hv + 1], op=ALU.mult), reads=[r_cols, r_beta], writes=[r_cols]); yield
            P.op("pool", lambda e: e.tensor_tensor(out=S.t1[:], in0=S.decA[:], in1=msl[:], op=ALU.mult), reads=[S.r_decA, r_msl], writes=[S.r_t1]); yield
            P.op("pool", lambda e: e.tensor_tensor(out=S.t2[:], in0=S.decA[:], in1=U[:], op=ALU.mult), reads=[S.r_decA, r_U], writes=[S.r_t2]); yield
            X0, rX0 = S.Xs[0]; Y0, rY0 = S.Ys[0]; T0, rT0 = S.Ts[0]
            P.op("pe", lambda e: e.matmul(G_, lhsT=kT[:, cs], rhs=kT[:, cs], start=True, stop=True), reads=[r_kT], writes=[rb[1]]); yield
            P.op("dve", lambda e: e.scalar_tensor_tensor(out=X0[:], in0=G_, scalar=negb[:, c, hv:hv + 1], in1=S.t1[:], op0=ALU.mult, op1=ALU.mult),
                 reads=[rb[1], r_negb, S.r_t1], writes=[rX0]); yield
            P.op("pe", lambda e: e.matmul(TT_, lhsT=X0[:], rhs=identf[:], start=True, stop=True), reads=[rX0, r_idf], writes=[rb[2]]); yield
            P.op("act", lambda e: e.activation(out=Y0[:], in_=TT_, func=AF.Copy), reads=[rb[2]], writes=[rY0]); yield
            P.op("dve", lambda e: e.tensor_tensor(out=T0[:], in0=TT_, in1=identf[:], op=ALU.add), reads=[rb[2], r_idf], writes=[rT0]); yield
            cur = 0
            for lvl in range(1, 7):
                Xc, rXc = S.Xs[cur]; Yc, rYc = S.Ys[cur]; Tc, rTc = S.Ts[cur]
                Xn, rXn = S.Xs[1 - cur]; Yn, rYn = S.Ys[1 - cur]; Tn, rTn = S.Ts[1 - cur]
                last = (lvl == 6)
                P.op("pe", lambda e, Xc=Xc, Yc=Yc: e.matmul(X2_, lhsT=Yc[:], rhs=Xc[:], start=True, stop=True), reads=[rXc, rYc], writes=[rb[3]]); yield
                if not last:
                    P.op("pe", lambda e, Xc=Xc, Yc=Yc: e.matmul(Y2_, lhsT=Xc[:], rhs=Yc[:], start=True, stop=True), reads=[rXc, rYc], writes=[rb[1]]); yield
                P.op("act", lambda e, Xn=Xn: e.activation(out=Xn[:], in_=X2_, func=AF.Copy), reads=[rb[3]], writes=[rXn]); yield
                if not last:
                    P.op("dve", lambda e, Yn=Yn: e.tensor_copy(out=Yn[:], in_=Y2_), reads=[rb[1]], writes=[rYn]); yield
                P.op("pe", lambda e, Xn=Xn, Tc=Tc: e.matmul(TT_, lhsT=Xn[:], rhs=Tc[:], start=True, stop=False), reads=[rXn, rTc], writes=[rb[2]], inc=False)
                P.op("pe", lambda e, Tc=Tc: e.matmul(TT_, lhsT=identf[:], rhs=Tc[:], start=False, stop=True), reads=[r_idf, rTc], writes=[rb[2]]); yield
                if not last:
                    P.op("dve", lambda e, Tn=Tn: e.tensor_copy(out=Tn[:], in_=TT_), reads=[rb[2]], writes=[rTn]); yield
                else:
                    P.op("dve", lambda e: e.tensor_copy(out=S.Ttb[:], in_=TT_), reads=[rb[2]], writes=[S.r_Ttb]); yield
                cur = 1 - cur
            P.op("act", lambda e: e.activation(out=S.kbg[:], in_=kTok[:, c, :], func=AF.Identity, scale=cols[:, 4:5]), reads=[r_kTok, r_cols], writes=[S.r_kbg]); yield
            P.op("act", lambda e: e.activation(out=S.kd[:], in_=kTok[:, c, :], func=AF.Identity, scale=cols[:, 3:4]), reads=[r_kTok, r_cols], writes=[S.r_kd]); yield
            P.op("pe", lambda e: e.matmul(W_, lhsT=S.kbg[:], rhs=S.Ttb[:], start=True, stop=True), reads=[S.r_kbg, S.r_Ttb], writes=[rb[3]]); yield
            P.op("act", lambda e: e.activation(out=S.nwT[:], in_=W_, func=AF.Identity, scale=-1.0), reads=[rb[3]], writes=[S.r_nwT]); yield
            if c > 0:
                P.op("pe", lambda e: e.matmul(UU_, lhsT=S.Ttb[:], rhs=vb[e_][:, c, :], start=True, stop=False), reads=[S.r_Ttb, r_vb[e_]], writes=[rb[2]], inc=False)
                P.op("pe", lambda e: e.matmul(UU_, lhsT=S.nwT[:], rhs=S.m_bf[:], start=False, stop=True), reads=[S.r_nwT, S.r_mbf], writes=[rb[2]]); yield
            else:
                P.op("pe", lambda e: e.matmul(UU_, lhsT=S.Ttb[:], rhs=vb[e_][:, c, :], start=True, stop=True), reads=[S.r_Ttb, r_vb[e_]], writes=[rb[2]]); yield
            P.op("act", lambda e: e.activation(out=S.u_sb[:], in_=UU_, func=AF.Copy), reads=[rb[2]], writes=[S.r_u]); yield
            P.op("pe", lambda e: e.matmul(G_, lhsT=kT[:, cs], rhs=qT[:, cs], start=True, stop=True), reads=[r_kT, r_qT], writes=[rb[1]]); yield
            P.op("dve", lambda e: e.tensor_tensor(out=S.QKT[:], in0=G_, in1=S.t2[:], op=ALU.mult), reads=[rb[1], S.r_t2], writes=[S.r_QKT]); yield
            if c > 0:
                P.op("pe", lambda e: e.matmul(R_, lhsT=S.qdT[:], rhs=S.m_bf[:], start=True, stop=False), reads=[S.r_qdT, S.r_mbf], writes=[rb[0]], inc=False)
            P.op("pe", lambda e: e.matmul(R_, lhsT=S.QKT[:], rhs=S.u_sb[:], start=(c == 0), stop=True), reads=[S.r_QKT, S.r_u], writes=[rb[0]]); yield
            P.op("act", lambda e: e.activation(out=S.o_sb[ob][:], in_=R_, func=AF.Identity, scale=128.0 ** -0.5), reads=[rb[0]], writes=[S.r_osb[ob]]); yield
            P.dma("sp", o_d[c * 128:(c + 1) * 128, hv, :], S.o_sb[ob][:], reads=[S.r_osb[ob]]); yield
            if c < NCH - 1:
                P.op("pe", lambda e: e.matmul(M_, lhsT=S.kd[:], rhs=S.u_sb[:], start=True, stop=True), reads=[S.r_kd, S.r_u], writes=[rb[1]]); yield
                if c == 0:
                    P.op("dve", lambda e: e.tensor_copy(out=S.m[:], in_=M_), reads=[rb[1]], writes=[S.r_m]); yield
                else:
                    P.op("dve", lambda e: e.scalar_tensor_tensor(out=S.m[:], in0=S.m[:], scalar=cols[:, 2:3], in1=M_, op0=ALU.mult, op1=ALU.add),
                         reads=[S.r_m, r_cols, rb[1]], writes=[S.r_m]); yield
                P.op("pool", lambda e: e.tensor_copy(out=S.m_bf[:], in_=S.m[:]), reads=[S.r_m], writes=[S.r_mbf]); yield

        for c in range(NCH):
            g0 = unit(0, c, ST[0]); g1 = unit(1, c, ST[1])
            d0 = d1 = False
            while not (d0 and d1):
                if not d0:
                    try:
                        next(g0)
                    except StopIteration:
                        d0 = True
                if not d1:
                    try:
                        next(g1)
                    except StopIteration:
                        d1 = True
    P.finish()
    return nc


def run_gdn(inputs, mod, hcat, n_heads=16):
    nc = build_gdn(n_heads)
    w = inputs["gdn_w_in"][0]
    wqkv = np.ascontiguousarray(w[:, 0:8192])
    kk = np.arange(128)[:, None]; ii = np.arange(128)[None, :]
    U = (kk <= ii).astype(np.float32)
    msl = (kk > ii).astype(np.float32)
    in_maps = []
    for core in range(8):
        b, d = core // 2, core % 2
        h = hcat[b]
        if d:
            h = _seg_flip_np(h)
        cw = inputs["gdn_conv"][0]
        if d:
            cw = cw[::-1]
        convw = np.ascontiguousarray(cw.T.reshape(64, 128, 5).transpose(1, 0, 2))
        wba = np.ascontiguousarray(np.concatenate([w[:, 12288 + d * 32:12288 + (d + 1) * 32], w[:, 12352 + d * 32:12352 + (d + 1) * 32]], 1))
        in_maps.append({
            "hT": np.ascontiguousarray(h.T), "msc": _msc(mod, 1, b, 1, 0), "wqkv": wqkv, "wba": wba, "convw": convw,
            "alog": np.ascontiguousarray(inputs["gdn_a_log"][0, d][None, :]), "dtb": np.ascontiguousarray(inputs["gdn_dt_bias"][0, d][None, :]),
            "identf": np.eye(128, dtype=np.float32), "U": U, "msl": msl})
    res = run_bass_kernel_spmd(nc, in_maps, core_ids=list(range(8)))
    out = np.zeros((4, 2, NTOT, 32, 128), np.float32)
    for core in range(8):
        b, d = core // 2, core % 2
        o = res.results[core]["o"]
        out[b, d] = _seg_flip_np(o) if d else o
    return out


def kernel(**inputs):
    inputs = {k: np.asarray(v, dtype=np.float32) for k, v in inputs.items()}
    mod = run_ada(inputs)
    hcat = np.concatenate([inputs["ctx"], inputs["x"]], axis=1)
    o = run_ret(inputs, mod, hcat)
    h1 = run_blk(inputs, mod, 0, o.reshape(4, 2, NTOT, 4096), hcat)
    o2 = run_gdn(inputs, mod, h1)
    out = run_blk(inputs, mod, 1, o2.reshape(4, 2, NTOT, 4096), h1)
    return out.astype(np.float32)
```

```python
import contextlib
import numpy as np
import concourse.bass as bass
import concourse.mybir as mybir
from concourse.bass_utils import run_bass_kernel_spmd

F32 = mybir.dt.float32
BF16 = mybir.dt.bfloat16
AF = mybir.ActivationFunctionType
ALU = mybir.AluOpType
AX = mybir.AxisListType

D = 2048
NCTX = 256
NLAT = 2048
NTOT = NCTX + NLAT
DFF = 5632
ALPHA = 4.0 ** 0.25
LN_EPS = 1e-5


class Res:
    __slots__ = ("w", "rd", "name", "excl")

    def __init__(self, name="", excl=False):
        self.w = []
        self.rd = {}
        self.name = name
        self.excl = excl


class Prog:
    ENG = ("pe", "act", "dve", "pool", "sp")

    def __init__(self, nc, n_slots=32):
        self.nc = nc
        self.stack = contextlib.ExitStack()
        self.sem = {}
        self.cnt = {}
        self.pending = {e: False for e in self.ENG}
        self.seen = {e: {} for e in self.ENG}
        self.streams = {e: [] for e in self.ENG}
        for e in self.ENG:
            if e == "sp":
                continue
            self.sem[e] = self.stack.enter_context(nc.semaphore("s_" + e))
            self.cnt[e] = 0
        self.n_slots = n_slots
        for i in range(n_slots):
            n = "d%d" % i
            self.sem[n] = self.stack.enter_context(nc.semaphore("s_" + n))
            self.cnt[n] = 0
        self.slot_i = 0
        self.slot_p = 0
        self.n_inst = {e: 0 for e in self.ENG}
        self._uid = 0

    def sbuf(self, shape, dtype, name=None):
        self._uid += 1
        return self.stack.enter_context(self.nc.sbuf_tensor(name or ("t%d" % self._uid), list(shape), dtype))

    def psum(self, shape, dtype, name=None):
        self._uid += 1
        return self.stack.enter_context(self.nc.psum_tensor(name or ("p%d" % self._uid), list(shape), dtype))

    def res(self, name="", excl=False):
        return Res(name, excl)

    def _need(self, eng, reads, writes):
        need = {}
        for r in reads:
            for (s, v) in r.w:
                if need.get(s, 0) < v:
                    need[s] = v
        for r in writes:
            for (s, v) in r.w:
                if need.get(s, 0) < v:
                    need[s] = v
            for s, v in r.rd.items():
                if need.get(s, 0) < v:
                    need[s] = v
        seen = self.seen[eng]
        for s, v in need.items():
            if s == "pe" and eng == "pe":
                continue
            if seen.get(s, 0) >= v:
                continue
            seen[s] = v
            sem = self.sem[s]
            self.streams[eng].append(lambda e, sem=sem, v=v: e.wait_ge(sem, v))

    def op(self, eng, fn, reads=(), writes=(), inc=True):
        if eng != "pe":
            ex = [r for r in reads if r.excl]
            if ex:
                reads = [r for r in reads if not r.excl]
                writes = list(writes) + ex
        self._need(eng, reads, writes)
        self.n_inst[eng] += 1
        if inc:
            self.cnt[eng] += 1
            val = self.cnt[eng]
            sem = self.sem[eng]
            self.streams[eng].append(lambda e, fn=fn, sem=sem: fn(e).then_inc(sem, 1))
            self.pending[eng] = False
        else:
            val = self.cnt[eng] + 1
            self.streams[eng].append(lambda e, fn=fn: fn(e))
            self.pending[eng] = True
        for r in reads:
            if r.rd.get(eng, 0) < val:
                r.rd[eng] = val
        for r in writes:
            r.w = [(eng, val)]
            r.rd = {}
        return val

    def dma(self, q, out, in_, reads=(), writes=(), **kw):
        half = self.n_slots // 2
        if q == "pool":
            s = "d%d" % (half + self.slot_p)
            self.slot_p = (self.slot_p + 1) % half
        else:
            s = "d%d" % self.slot_i
            self.slot_i = (self.slot_i + 1) % half
        self._need(q, reads, writes)
        seen = self.seen[q]
        if seen.get(s, 0) < self.cnt[s]:
            seen[s] = self.cnt[s]
            self.streams[q].append(lambda e, sem=self.sem[s], v=self.cnt[s]: e.wait_ge(sem, v))
        self.cnt[s] += 16
        val = self.cnt[s]
        sem = self.sem[s]
        self.streams[q].append(lambda e, sem=sem, out=out, in_=in_, kw=kw: e.dma_start(out=out, in_=in_, **kw).then_inc(sem, 16))
        self.n_inst[q] += 1
        for r in reads:
            if r.rd.get(s, 0) < val:
                r.rd[s] = val
        for r in writes:
            r.w = [(s, val)]
            r.rd = {}
        return (s, val)

    def finish(self):
        for e in self.ENG:
            assert not self.pending[e], "engine %s has pending un-signalled ops" % e
        for i in range(self.n_slots):
            s = "d%d" % i
            if self.cnt[s] > 0:
                self.streams["sp"].append(lambda e, sem=self.sem[s], v=self.cnt[s]: e.wait_ge(sem, v))
        for en in ("pe", "act", "dve", "pool"):
            if self.cnt[en] > 0:
                self.streams["sp"].append(lambda e, sem=self.sem[en], v=self.cnt[en]: e.wait_ge(sem, v))
        nc = self.nc
        streams = self.streams
        with nc.Block() as block:
            @block.sync
            def _(e):
                for f in streams["sp"]:
                    f(e)

            @block.tensor
            def _(e):
                for f in streams["pe"]:
                    f(e)

            @block.scalar
            def _(e):
                for f in streams["act"]:
                    f(e)

            @block.vector
            def _(e):
                for f in streams["dve"]:
                    f(e)

            @block.gpsimd
            def _(e):
                for f in streams["pool"]:
                    f(e)
        self.stack.close()


def new_nc():
    return bass.Bass("TRN2", target_bir_lowering=False)


def build_ada():
    nc = new_nc()
    cc = nc.dram_tensor("cc", [5, 2048], F32, kind="ExternalInput").ap()
    adaw = nc.dram_tensor("adaw", [2, 2048, 1536], F32, kind="ExternalInput").ap()
    adab = nc.dram_tensor("adab", [2, 1, 1536], F32, kind="ExternalInput").ap()
    ident_d = nc.dram_tensor("ident", [128, 128], F32, kind="ExternalInput").ap()
    mod = nc.dram_tensor("mod", [2, 5, 1536], F32, kind="ExternalOutput").ap()
    P = Prog(nc)
    cc_t = P.sbuf([5, 2048], F32); r_cc = P.res()
    sc = P.sbuf([5, 2048], F32); r_sc = P.res()
    ident = P.sbuf([128, 128], F32); r_id = P.res()
    ones = P.sbuf([1, 8], F32); r_ones = P.res()
    scT = P.sbuf([128, 16, 5], F32); r_scT = P.res()
    bias = P.sbuf([1, 2, 1536], F32); r_bias = P.res()
    P.dma("sp", cc_t[:], cc, writes=[r_cc])
    P.dma("sp", ident[:], ident_d, writes=[r_id])
    P.dma("sp", bias[:], adab.rearrange("l o n -> o l n"), writes=[r_bias])
    P.op("dve", lambda e: e.memset(ones[:], 1.0), writes=[r_ones])
    P.op("act", lambda e: e.activation(out=sc[:], in_=cc_t[:], func=AF.Silu), reads=[r_cc], writes=[r_sc])
    ps_t = P.psum([128, 16, 5], F32); r_pst = P.res()
    for kc in range(16):
        P.op("pe", lambda e, kc=kc: e.transpose(ps_t[:, kc, :], sc[:, kc * 128:(kc + 1) * 128], ident[:5, :5]),
             reads=[r_sc, r_id], writes=[r_pst], inc=(kc == 15))
    P.op("dve", lambda e: e.tensor_copy(out=scT[:], in_=ps_t[:]), reads=[r_pst], writes=[r_scT])
    wt = [P.sbuf([128, 16, 512], F32) for _ in range(3)]; r_wt = [P.res() for _ in range(3)]
    ps = [P.psum([5, 512], F32) for _ in range(2)]; r_ps = [P.res() for _ in range(2)]
    ot = [P.sbuf([5, 512], F32) for _ in range(2)]; r_ot = [P.res() for _ in range(2)]
    it = 0
    for l in range(2):
        for g in range(3):
            b = it % 3; pb = it % 2
            q = ["sp", "pool", "act"][it % 3]
            P.dma(q, wt[b][:], adaw[l, :, g * 512:(g + 1) * 512].rearrange("(kc p) n -> p kc n", p=128), writes=[r_wt[b]])
            for kc in range(16):
                P.op("pe", lambda e, kc=kc, b=b, pb=pb: e.matmul(ps[pb][:], lhsT=scT[:, kc, :], rhs=wt[b][:, kc, :], start=(kc == 0), stop=False),
                     reads=[r_scT, r_wt[b]], writes=[r_ps[pb]], inc=False)
            P.op("pe", lambda e, l=l, g=g, pb=pb: e.matmul(ps[pb][:], lhsT=ones[:, :5], rhs=bias[:, l, g * 512:(g + 1) * 512], start=False, stop=True),
                 reads=[r_ones, r_bias], writes=[r_ps[pb]], inc=True)
            P.op("dve", lambda e, pb=pb: e.tensor_copy(out=ot[pb][:], in_=ps[pb][:]), reads=[r_ps[pb]], writes=[r_ot[pb]])
            P.dma("sp", mod[l, :, g * 512:(g + 1) * 512], ot[pb][:], reads=[r_ot[pb]])
            it += 1
    P.finish()
    return nc


def emit_uT(P, hT, msc_d, uT, r_uT, T, segs):
    msc = P.sbuf([128, 16, 4], F32); r_msc = P.res()
    P.dma("sp", msc[:], msc_d, writes=[r_msc])
    P.op("dve", lambda e: e.tensor_scalar_add(out=msc[:, :, 0:1], in0=msc[:, :, 0:1], scalar1=1.0), reads=[r_msc], writes=[r_msc])
    P.op("dve", lambda e: e.tensor_scalar_add(out=msc[:, :, 2:3], in0=msc[:, :, 2:3], scalar1=1.0), reads=[r_msc], writes=[r_msc])
    stg = [P.sbuf([128, T], F32) for _ in range(2)]; r_stg = [P.res() for _ in range(2)]
    for kc in range(16):
        b = kc % 2
        P.dma(["sp", "act"][kc % 2], stg[b][:], hT[kc * 128:(kc + 1) * 128, :], writes=[r_stg[b]])
        for (t0, t1, w) in segs:
            P.op("act", lambda e, kc=kc, b=b, t0=t0, t1=t1, w=w: e.activation(
                out=uT[:, kc, t0:t1], in_=stg[b][:, t0:t1], func=AF.Identity,
                scale=msc[:, kc, 2 * w:2 * w + 1], bias=msc[:, kc, 2 * w + 1:2 * w + 2]),
                reads=[r_stg[b], r_msc], writes=[r_uT])


def build_ret():
    nc = new_nc()
    T = NTOT
    hT = nc.dram_tensor("hT", [D, T], F32, kind="ExternalInput").ap()
    msc_d = nc.dram_tensor("msc", [128, 16, 4], F32, kind="ExternalInput").ap()
    wq = nc.dram_tensor("wq", [D, 2048], F32, kind="ExternalInput").ap()
    wk = nc.dram_tensor("wk", [D, 2048], F32, kind="ExternalInput").ap()
    wv = nc.dram_tensor("wv", [D, 4096], F32, kind="ExternalInput").ap()
    cos_d = nc.dram_tensor("cosT", [128, NLAT], F32, kind="ExternalInput").ap()
    sin_d = nc.dram_tensor("sinT", [128, NLAT], F32, kind="ExternalInput").ap()
    mask_d = nc.dram_tensor("mask01", [128, 128], F32, kind="ExternalInput").ap()
    expo_d = nc.dram_tensor("expo", [128, 128], F32, kind="ExternalInput").ap()
    idx_d = nc.dram_tensor("idxs", [128, 4], F32, kind="ExternalInput").ap()
    rdec_d = nc.dram_tensor("rdec", [1, 8], F32, kind="ExternalInput").ap()
    identb_d = nc.dram_tensor("identb", [128, 128], F32, kind="ExternalInput").ap()
    o_d = nc.dram_tensor("o", [T, 8, 512], F32, kind="ExternalOutput").ap()
    P = Prog(nc)
    uT = P.sbuf([128, 16, T], BF16); r_uT = P.res()
    emit_uT(P, hT, msc_d, uT, r_uT, T, [(0, NCTX, 0), (NCTX, T, 1)])
    cosT = P.sbuf([128, NLAT], F32); r_cos = P.res()
    sinT = P.sbuf([128, NLAT], F32); r_sin = P.res()
    mask01 = P.sbuf([128, 128], F32); expo = P.sbuf([128, 128], F32); idxs = P.sbuf([128, 4], F32)
    rdec = P.sbuf([128, 8], F32); identf = P.sbuf([128, 128], F32); identb = P.sbuf([128, 128], BF16)
    r_c = P.res()
    P.dma("sp", cosT[:], cos_d, writes=[r_cos])
    P.dma("sp", sinT[:], sin_d, writes=[r_sin])
    P.dma("sp", mask01[:], mask_d, writes=[r_c])
    r_c2 = P.res(); r_c3 = P.res(); r_c4 = P.res(); r_c5 = P.res()
    P.dma("sp", expo[:], expo_d, writes=[r_c2])
    P.dma("sp", idxs[:], idx_d, writes=[r_c3])
    P.dma("sp", rdec[:], rdec_d.partition_broadcast(128), writes=[r_c4])
    P.dma("sp", identf[:], identb_d, writes=[r_c5])
    r_idb = P.res()
    P.op("dve", lambda e: e.tensor_copy(out=identb[:], in_=identf[:]), reads=[r_c5], writes=[r_idb])
    lg = P.sbuf([128, 8], F32); r_lg = P.res()
    P.op("act", lambda e: e.activation(out=lg[:], in_=rdec[:], func=AF.Exp), reads=[r_c4], writes=[r_lg])
    P.op("dve", lambda e: e.tensor_scalar_mul(out=lg[:], in0=lg[:], scalar1=-1.0), reads=[r_lg], writes=[r_lg])
    maskT = P.sbuf([128, 8, 128], F32); r_mask = P.res()
    decs = P.sbuf([128, 8, 4], F32); r_decs = P.res()
    for h in range(8):
        P.op("act", lambda e, h=h: e.activation(out=maskT[:, h, :], in_=expo[:], func=AF.Exp, scale=lg[:, h:h + 1]),
             reads=[r_c2, r_lg], writes=[r_mask])
        P.op("dve", lambda e, h=h: e.scalar_tensor_tensor(out=maskT[:, h, :], in0=maskT[:, h, :], scalar=1.0 / 16.0, in1=mask01[:],
                                                          op0=ALU.mult, op1=ALU.mult), reads=[r_mask, r_c], writes=[r_mask])
        P.op("act", lambda e, h=h: e.activation(out=decs[:, h, 0:3], in_=idxs[:, 0:3], func=AF.Exp, scale=lg[:, h:h + 1]),
             reads=[r_c3, r_lg], writes=[r_decs])
        P.op("dve", lambda e, h=h: e.tensor_scalar_mul(out=decs[:, h, 0:1], in0=decs[:, h, 0:1], scalar1=1.0 / 16.0),
             reads=[r_decs], writes=[r_decs])
    qT = P.sbuf([128, 2, T], BF16); r_qT = P.res()
    kT = P.sbuf([128, 2, T], BF16); r_kT = P.res()
    vt = P.sbuf([128, 18, 512], BF16); r_v = P.res()
    wq_t = [P.sbuf([128, 16, 256], BF16)] * 2; r_wq = [P.res()] * 2
    wk_t = [P.sbuf([128, 16, 256], BF16)] * 2; r_wk = [P.res()] * 2
    wv_t = [P.sbuf([128, 16, 512], BF16)] * 2; r_wv = [P.res()] * 2
    ps_a = [P.psum([128, 512], F32) for _ in range(2)]; r_psa = [P.res() for _ in range(2)]
    ps_S = P.psum([128, 512], F32); r_psS = P.res()
    ps_kt = P.psum([128, 1024], BF16); r_pskt = P.res()
    ps_o = P.psum([128, 512], F32); r_pso = P.res()
    ps_i = P.psum([128, 512], F32); r_psi = P.res()
    ps_s = [P.psum([128, 512], F32) for _ in range(2)]; r_pss = [P.res() for _ in range(2)]
    tmp = [P.sbuf([128, 512], F32) for _ in range(4)]; r_tmp = [P.res() for _ in range(4)]
    AT = P.sbuf([128, 128], BF16); r_AT = P.res()
    kd = P.sbuf([128, 256], BF16); r_kd = P.res()
    o_sb = [P.sbuf([128, 512], F32) for _ in range(2)]; r_osb = [P.res() for _ in range(2)]
    st = P.sbuf([128, 2, 512], F32); r_st = P.res()
    st_bf = P.sbuf([128, 2, 512], BF16); r_stbf = P.res()
    groups = [(0, 256, 0), (256, 768, 1), (768, 1280, 1), (1280, 1792, 1), (1792, 2304, 1)]

    def load_w(h):
        b = h % 2
        P.dma("pool", wq_t[b][:], wq[:, h * 256:(h + 1) * 256].rearrange("(kc p) n -> p kc n", p=128), writes=[r_wq[b]])
        P.dma("pool", wk_t[b][:], wk[:, h * 256:(h + 1) * 256].rearrange("(kc p) n -> p kc n", p=128), writes=[r_wk[b]])
        P.dma("pool", wv_t[b][:], wv[:, h * 512:(h + 1) * 512].rearrange("(kc p) n -> p kc n", p=128), writes=[r_wv[b]])

    for h in range(8):
        b = h % 2
        load_w(h)
        for (w_t, r_w, dst, r_dst) in ((wq_t[b], r_wq[b], qT, r_qT), (wk_t[b], r_wk[b], kT, r_kT)):
            for (t0, t1, lat) in groups:
                n = t1 - t0
                for dc in range(2):
                    for kc in range(16):
                        P.op("pe", lambda e, dc=dc, kc=kc, w_t=w_t, t0=t0, t1=t1, n=n: e.matmul(
                            ps_a[dc][:, :n], lhsT=w_t[:, kc, dc * 128:(dc + 1) * 128], rhs=uT[:, kc, t0:t1],
                            start=(kc == 0), stop=(kc == 15)), reads=[r_w, r_uT], writes=[r_psa[dc]], inc=(kc == 15))
                if not lat:
                    for dc in range(2):
                        P.op("act", lambda e, dc=dc, dst=dst, t0=t0, t1=t1, n=n: e.activation(out=dst[:, dc, t0:t1], in_=ps_a[dc][:, :n], func=AF.Copy),
                             reads=[r_psa[dc]], writes=[r_dst])
                else:
                    l0 = t0 - NCTX; l1 = t1 - NCTX
                    P.op("dve", lambda e, l0=l0, l1=l1: e.tensor_tensor(out=tmp[0][:], in0=ps_a[0][:], in1=cosT[:, l0:l1], op=ALU.mult),
                         reads=[r_psa[0], r_cos], writes=[r_tmp[0]])
                    P.op("dve", lambda e, l0=l0, l1=l1: e.tensor_tensor(out=tmp[1][:], in0=ps_a[1][:], in1=sinT[:, l0:l1], op=ALU.mult),
                         reads=[r_psa[1], r_sin], writes=[r_tmp[1]])
                    P.op("dve", lambda e, l0=l0, l1=l1: e.tensor_tensor(out=tmp[2][:], in0=ps_a[0][:], in1=sinT[:, l0:l1], op=ALU.mult),
                         reads=[r_psa[0], r_sin], writes=[r_tmp[2]])
                    P.op("dve", lambda e, l0=l0, l1=l1: e.tensor_tensor(out=tmp[3][:], in0=ps_a[1][:], in1=cosT[:, l0:l1], op=ALU.mult),
                         reads=[r_psa[1], r_cos], writes=[r_tmp[3]])
                    P.op("pool", lambda e, dst=dst, t0=t0, t1=t1: e.tensor_tensor(out=dst[:, 0, t0:t1], in0=tmp[0][:], in1=tmp[1][:], op=ALU.subtract),
                         reads=[r_tmp[0], r_tmp[1]], writes=[r_dst])
                    P.op("pool", lambda e, dst=dst, t0=t0, t1=t1: e.tensor_tensor(out=dst[:, 1, t0:t1], in0=tmp[2][:], in1=tmp[3][:], op=ALU.add),
                         reads=[r_tmp[2], r_tmp[3]], writes=[r_dst])
        for c in range(18):
            pb = c % 2
            for kc in range(16):
                P.op("pe", lambda e, c=c, kc=kc, pb=pb, b=b: e.matmul(ps_a[pb][:], lhsT=uT[:, kc, c * 128:(c + 1) * 128], rhs=wv_t[b][:, kc, :],
                                                                 start=(kc == 0), stop=(kc == 15)), reads=[r_wv[b], r_uT], writes=[r_psa[pb]], inc=(kc == 15))
            P.op("act", lambda e, c=c, pb=pb: e.activation(out=vt[:, c, :], in_=ps_a[pb][:], func=AF.Copy), reads=[r_psa[pb]], writes=[r_v])
        for c in range(18):
            cs = slice(c * 128, (c + 1) * 128)
            ob = c % 2
            for dc in range(2):
                P.op("pe", lambda e, dc=dc, cs=cs: e.matmul(ps_S[:, :128], lhsT=kT[:, dc, cs], rhs=qT[:, dc, cs], start=(dc == 0), stop=(dc == 1)),
                     reads=[r_kT, r_qT], writes=[r_psS], inc=(dc == 1))
            P.op("dve", lambda e, h=h: e.tensor_tensor(out=AT[:], in0=ps_S[:, :128], in1=maskT[:, h, :], op=ALU.mult),
                 reads=[r_psS, r_mask], writes=[r_AT])
            P.op("pe", lambda e, c=c: e.matmul(ps_o[:], lhsT=AT[:], rhs=vt[:, c, :], start=True, stop=True), reads=[r_AT, r_v], writes=[r_pso])
            P.op("act", lambda e, ob=ob: e.activation(out=o_sb[ob][:], in_=ps_o[:], func=AF.Copy), reads=[r_pso], writes=[r_osb[ob]])
            if c > 0:
                for dc in range(2):
                    P.op("pe", lambda e, dc=dc, cs=cs: e.matmul(ps_i[:], lhsT=qT[:, dc, cs], rhs=st_bf[:, dc, :], start=(dc == 0), stop=(dc == 1)),
                         reads=[r_qT, r_stbf], writes=[r_psi], inc=(dc == 1))
                P.op("dve", lambda e, ob=ob, h=h: e.scalar_tensor_tensor(out=o_sb[ob][:], in0=ps_i[:], scalar=decs[:, h, 0:1], in1=o_sb[ob][:],
                                                                         op0=ALU.mult, op1=ALU.add), reads=[r_psi, r_decs, r_osb[ob]], writes=[r_osb[ob]])
            P.dma("sp", o_d[c * 128:(c + 1) * 128, h, :], o_sb[ob][:], reads=[r_osb[ob]])
            if c < 17:
                for dc in range(2):
                    P.op("pe", lambda e, dc=dc, cs=cs: e.transpose(ps_kt[:, dc * 128:(dc + 1) * 128], kT[:, dc, cs], identb[:]),
                         reads=[r_kT, r_idb], writes=[r_pskt], inc=(dc == 1))
                P.op("act", lambda e, h=h: e.activation(out=kd[:], in_=ps_kt[:, :256], func=AF.Identity, scale=decs[:, h, 1:2]),
                     reads=[r_pskt, r_decs], writes=[r_kd])
                for dc in range(2):
                    P.op("pe", lambda e, dc=dc, c=c: e.matmul(ps_s[dc][:], lhsT=kd[:, dc * 128:(dc + 1) * 128], rhs=vt[:, c, :], start=True, stop=True),
                         reads=[r_kd, r_v], writes=[r_pss[dc]])
                    if c == 0:
                        P.op("dve", lambda e, dc=dc: e.tensor_copy(out=st[:, dc, :], in_=ps_s[dc][:]), reads=[r_pss[dc]], writes=[r_st])
                    else:
                        P.op("dve", lambda e, dc=dc, h=h: e.scalar_tensor_tensor(out=st[:, dc, :], in0=st[:, dc, :], scalar=decs[:, h, 2:3], in1=ps_s[dc][:],
                                                                                 op0=ALU.mult, op1=ALU.add), reads=[r_st, r_decs, r_pss[dc]], writes=[r_st])
                P.op("pool", lambda e: e.tensor_copy(out=st_bf[:], in_=st[:]), reads=[r_st], writes=[r_stbf])
    P.finish()
    return nc


def _seg_flip_np(t):
    return np.concatenate([t[:NCTX][::-1], t[NCTX:][::-1]], axis=0)


def _rope_tables():
    rows = NLAT // 64
    row = np.repeat(np.arange(rows, dtype=np.float32), 64)
    col = np.tile(np.arange(64, dtype=np.float32), rows)
    n_freq = 64
    inv_freq = (np.float32(10000.0) ** (-np.arange(n_freq, dtype=np.float32) / np.float32(n_freq))).astype(np.float32)
    ang = np.concatenate([row[:, None] * inv_freq, col[:, None] * inv_freq], axis=-1).astype(np.float32)
    return np.cos(ang).astype(np.float32), np.sin(ang).astype(np.float32)


def _msc(mod, layer, b, j_scale, j_shift):
    m = mod[layer].reshape(5, 6, D)
    cols = [m[4, j_scale], m[4, j_shift], m[b, j_scale], m[b, j_shift]]
    a = np.stack(cols, axis=-1)
    return np.ascontiguousarray(a.reshape(16, 128, 4).transpose(1, 0, 2))


def run_ada(inputs):
    nc = build_ada()
    cc = np.concatenate([inputs["c"], inputs["c_ctx"][None]], 0).astype(np.float32)
    in_maps = []
    for i in range(8):
        in_maps.append({"cc": cc, "adaw": np.ascontiguousarray(inputs["ada_w"][:, :, i * 1536:(i + 1) * 1536]),
                        "adab": np.ascontiguousarray(inputs["ada_b"][:, None, i * 1536:(i + 1) * 1536]),
                        "ident": np.eye(128, dtype=np.float32)})
    res = run_bass_kernel_spmd(nc, in_maps, core_ids=list(range(8)))
    return np.concatenate([r["mod"] for r in res.results], axis=2)


def run_ret(inputs, mod, hcat):
    nc = build_ret()
    cos, sin = _rope_tables()
    w = inputs["ret_w_in"][0]
    jj = np.arange(128)[:, None]; ii = np.arange(128)[None, :]
    expo = np.maximum(ii - jj, 0).astype(np.float32)
    idxs = np.stack([np.arange(128) + 1.0, 127.0 - np.arange(128), np.full(128, 128.0), np.zeros(128)], -1).astype(np.float32)
    in_maps = []
    for core in range(8):
        b, d = core // 2, core % 2
        h = hcat[b]
        if d:
            h = _seg_flip_np(h)
        cs, sn = (cos[::-1], sin[::-1]) if d else (cos, sin)
        mask01 = (ii > jj) if d else (ii >= jj)
        in_maps.append({
            "hT": np.ascontiguousarray(h.T), "msc": _msc(mod, 0, b, 1, 0),
            "wq": np.ascontiguousarray(w[:, 0:2048]), "wk": np.ascontiguousarray(w[:, 2048:4096]),
            "wv": np.ascontiguousarray(w[:, 4096:8192]),
            "cosT": np.ascontiguousarray(cs.T), "sinT": np.ascontiguousarray(sn.T),
            "mask01": mask01.astype(np.float32), "expo": expo, "idxs": idxs,
            "rdec": np.ascontiguousarray(inputs["ret_decay"][0, d][None, :]),
            "identb": np.eye(128, dtype=np.float32)})
    res = run_bass_kernel_spmd(nc, in_maps, core_ids=list(range(8)))
    out = np.zeros((4, 2, NTOT, 8, 512), np.float32)
    for core in range(8):
        b, d = core // 2, core % 2
        o = res.results[core]["o"]
        out[b, d] = _seg_flip_np(o) if d else o
    return out


def emit_rsqrt(P, out_ap, in_ap, scale, eps_ap, r_s):
    P.op("act", lambda e: e.activation(out=out_ap, in_=in_ap, func=AF.Sqrt, bias=eps_ap, scale=scale), reads=[r_s], writes=[r_s])
    P.op("dve", lambda e: e.reciprocal(out=out_ap, in_=out_ap), reads=[r_s], writes=[r_s])


def emit_ln(P, hh, r_hh, t, lnrow_d, gi, bi, scr):
    stats, mv, rstd, gbc, bbc, r_s, r_g, r_b, eps5 = scr
    for j in range(4):
        P.op("dve", lambda e, j=j: e.bn_stats(out=stats[:, j, :], in_=hh[:, t, j * 512:(j + 1) * 512]), reads=[r_hh[t]], writes=[r_s])
    P.op("dve", lambda e: e.bn_aggr(out=mv[:], in_=stats[:].rearrange("p a b -> p (a b)")), reads=[r_s], writes=[r_s])
    emit_rsqrt(P, rstd[:, 0:1], mv[:, 1:2], 1.0, eps5[:, 0:1], r_s)
    P.op("dve", lambda e: e.tensor_scalar(out=hh[:, t, :], in0=hh[:, t, :], scalar1=mv[:, 0:1], scalar2=rstd[:, 0:1], op0=ALU.subtract, op1=ALU.mult),
         reads=[r_s, r_hh[t]], writes=[r_hh[t]])
    for j in range(4):
        cs = slice(j * 512, (j + 1) * 512)
        P.dma("sp", gbc[:], lnrow_d[gi:gi + 1, cs].partition_broadcast(128), writes=[r_g])
        P.dma("sp", bbc[:], lnrow_d[bi:bi + 1, cs].partition_broadcast(128), writes=[r_b])
        P.op("pool", lambda e, cs=cs: e.tensor_tensor(out=hh[:, t, cs], in0=hh[:, t, cs], in1=gbc[:], op=ALU.mult), reads=[r_g, r_hh[t]], writes=[r_hh[t]])
        P.op("pool", lambda e, cs=cs: e.tensor_tensor(out=hh[:, t, cs], in0=hh[:, t, cs], in1=bbc[:], op=ALU.add), reads=[r_b, r_hh[t]], writes=[r_hh[t]])


def build_blk(layer):
    nc = new_nc()
    T = 1152 if layer == 0 else 1024
    nt = T // 128
    tgs = [(0, 384), (384, 768), (768, 1152)] if layer == 0 else [(0, 512), (512, 1024)]
    of_d = nc.dram_tensor("of", [T, 4096], F32, kind="ExternalInput").ap()
    ob_d = nc.dram_tensor("ob", [T, 4096], F32, kind="ExternalInput").ap()
    h_d = nc.dram_tensor("h", [T, D], F32, kind="ExternalInput").ap()
    hT_d = nc.dram_tensor("hT", [D, T], F32, kind="ExternalInput").ap()
    msc1_d = nc.dram_tensor("msc1", [128, 16, nt, 2], F32, kind="ExternalInput").ap()
    msc2_d = nc.dram_tensor("msc2", [128, 16, nt, 2], F32, kind="ExternalInput").ap()
    g1_d = nc.dram_tensor("g1", [nt, D], F32, kind="ExternalInput").ap()
    g2_d = nc.dram_tensor("g2", [nt, D], F32, kind="ExternalInput").ap()
    ln_d = nc.dram_tensor("lnrows", [4, D], F32, kind="ExternalInput").ap()
    wg_d = nc.dram_tensor("wg", [D, 4096], F32, kind="ExternalInput").ap()
    wo_d = nc.dram_tensor("wo", [4096, D], F32, kind="ExternalInput").ap()
    w1_d = nc.dram_tensor("w1", [D, 2 * DFF], F32, kind="ExternalInput").ap()
    w2_d = nc.dram_tensor("w2", [DFF, D], F32, kind="ExternalInput").ap()
    nw_d = nc.dram_tensor("nw", [1, 128], F32, kind="ExternalInput").ap()
    id_d = nc.dram_tensor("identf", [128, 128], F32, kind="ExternalInput").ap()
    out_d = nc.dram_tensor("out", [T, D], F32, kind="ExternalOutput").ap()
    P = Prog(nc)
    uT = P.sbuf([128, 16, T], BF16); r_uT = P.res()
    big = P.sbuf([128, 11, T], BF16); r_big = P.res()
    hh = P.sbuf([128, nt, D], F32); r_hh = [P.res() for _ in range(nt)]
    W = [P.sbuf([128, 16, 512], BF16) for _ in range(2)]; r_W = [P.res() for _ in range(2)]
    Wgu = [W[0][:, :, 0:256], W[0][:, :, 256:512]]; r_Wg = [P.res() for _ in range(2)]; r_Wu = [P.res() for _ in range(2)]
    msc1 = P.sbuf([128, 16, nt, 2], F32); r_m1 = P.res()
    msc2 = P.sbuf([128, 16, nt, 2], F32); r_m2 = P.res()
    identf = P.sbuf([128, 128], F32); r_idf = P.res()
    identb = P.sbuf([128, 128], BF16); r_idb = P.res()
    nw = P.sbuf([128, 128], F32); r_nw = P.res()
    P.dma("sp", msc1[:], msc1_d, writes=[r_m1])
    P.dma("sp", msc2[:], msc2_d, writes=[r_m2])
    P.dma("sp", identf[:], id_d, writes=[r_idf])
    P.dma("sp", nw[:], nw_d.partition_broadcast(128), writes=[r_nw])
    P.op("dve", lambda e: e.tensor_copy(out=identb[:], in_=identf[:]), reads=[r_idf], writes=[r_idb])
    P.op("dve", lambda e: e.tensor_scalar_add(out=msc1[:, :, :, 0:1], in0=msc1[:, :, :, 0:1], scalar1=1.0), reads=[r_m1], writes=[r_m1])
    P.op("dve", lambda e: e.tensor_scalar_add(out=msc2[:, :, :, 0:1], in0=msc2[:, :, :, 0:1], scalar1=1.0), reads=[r_m2], writes=[r_m2])
    for t in range(nt):
        P.dma(["sp", "act"][t % 2], hh[:, t, :], h_d[t * 128:(t + 1) * 128, :], writes=[r_hh[t]])
    stg = P.sbuf([128, T], F32); r_stg = P.res()
    for kc in range(16):
        P.dma("sp", stg[:], hT_d[kc * 128:(kc + 1) * 128, :], writes=[r_stg])
        for t in range(nt):
            P.op("act", lambda e, kc=kc, t=t: e.activation(out=uT[:, kc, t * 128:(t + 1) * 128], in_=stg[:, t * 128:(t + 1) * 128], func=AF.Identity,
                                                           scale=msc1[:, kc, t, 0:1], bias=msc1[:, kc, t, 1:2]), reads=[r_stg, r_m1], writes=[r_uT])
    ps_m = [P.psum([128, 512], F32) for _ in range(2)]; r_psm = [P.res() for _ in range(2)]
    ps_u = [P.psum([128, 512], F32) for _ in range(2)]; r_psu = [P.res() for _ in range(2)]
    ps_tr = P.psum([128, 1024], BF16); r_pstr = P.res()
    ps_tf = [P.psum([128, 512], F32) for _ in range(2)]; r_pstf = [P.res() for _ in range(2)]
    class Rot:
        def __init__(self, n, dt):
            self.items = [(P.sbuf([128, 512], dt), P.res()) for _ in range(n)]
            self.i = 0

        def next(self):
            it = self.items[self.i % len(self.items)]
            self.i += 1
            return it
    rot_sg = Rot(2, F32); rot_of = Rot(2, F32); rot_ob = Rot(2, F32); rot_xn = Rot(2, F32); rot_og = Rot(2, BF16)
    rot_gbc = Rot(3, F32); rot_tmp = Rot(2, F32)
    stats = P.sbuf([128, 4, 6], F32); mv = P.sbuf([128, 2], F32); rstd = P.sbuf([128, 4], F32); r_s = P.res()
    lg_t = P.sbuf([128, 512], F32); lb_t = P.sbuf([128, 512], F32); r_lg = P.res(); r_lb = P.res()
    eps5 = P.sbuf([128, 2], F32)
    P.op("dve", lambda e: e.memset(eps5[:, 0:1], LN_EPS), writes=[r_s])
    P.op("dve", lambda e: e.memset(eps5[:, 1:2], 1e-6), writes=[r_s])
    ln_scr = (stats, mv, rstd, lg_t, lb_t, r_s, r_lg, r_lb, eps5)
    mmi = [0]

    def accum_h(t, cs, ps, r_ps, g_d, first):
        gbc, r_gbc = rot_gbc.next(); tmp, r_tmp = rot_tmp.next()
        P.dma("sp", gbc[:], g_d[t:t + 1, cs].partition_broadcast(128), writes=[r_gbc])
        P.op("dve", lambda e: e.tensor_tensor(out=tmp[:], in0=ps[:], in1=gbc[:], op=ALU.mult), reads=[r_ps, r_gbc], writes=[r_tmp])
        P.op("dve", lambda e: e.scalar_tensor_tensor(out=hh[:, t, cs], in0=hh[:, t, cs], scalar=(ALPHA if first else 1.0), in1=tmp[:],
                                                     op0=ALU.mult, op1=ALU.add), reads=[r_tmp, r_hh[t]], writes=[r_hh[t]])

    for kq in range(4):
        for cgi in range(2):
            cg = kq * 2 + cgi
            cols = slice(cg * 512, (cg + 1) * 512)
            wb = mmi[0] % 2; mmi[0] += 1
            P.dma("pool", W[wb][:], wg_d[:, cols].rearrange("(kc p) n -> p kc n", p=128), writes=[r_W[wb]])
            for t in range(nt):
                pb = t % 2
                ts_ = slice(t * 128, (t + 1) * 128)
                for kc in range(16):
                    P.op("pe", lambda e, kc=kc, pb=pb, wb=wb, ts_=ts_: e.matmul(ps_m[pb][:], lhsT=uT[:, kc, ts_], rhs=W[wb][:, kc, :],
                                                                          start=(kc == 0), stop=(kc == 15)), reads=[r_uT, r_W[wb]], writes=[r_psm[pb]], inc=(kc == 15))
                sg, r_sg = rot_sg.next(); oft, r_of = rot_of.next(); obt, r_ob = rot_ob.next(); xn, r_xn = rot_xn.next(); og, r_og = rot_og.next()
                P.op("act", lambda e, pb=pb, sg=sg, oft=oft, obt=obt, xn=xn, og=og: e.activation(out=sg[:], in_=ps_m[pb][:], func=AF.Silu), reads=[r_psm[pb]], writes=[r_sg])
                P.dma("sp", oft[:], of_d[ts_, cols], writes=[r_of])
                P.dma("act", obt[:], ob_d[ts_, cols], writes=[r_ob])
                P.op("pool", lambda e, sg=sg, oft=oft, obt=obt, xn=xn, og=og: e.tensor_tensor(out=oft[:], in0=oft[:], in1=obt[:], op=ALU.add), reads=[r_of, r_ob], writes=[r_of])
                if layer == 0:
                    P.op("dve", lambda e, sg=sg, oft=oft, obt=obt, xn=xn, og=og: e.bn_stats(out=stats[:, 0, :], in_=oft[:]), reads=[r_of], writes=[r_s])
                    P.op("dve", lambda e, sg=sg, oft=oft, obt=obt, xn=xn, og=og: e.bn_aggr(out=mv[:], in_=stats[:, 0, :]), reads=[r_s], writes=[r_s])
                    emit_rsqrt(P, rstd[:, 0:1], mv[:, 1:2], 1.0, eps5[:, 0:1], r_s)
                    P.op("dve", lambda e, sg=sg, oft=oft, obt=obt, xn=xn, og=og: e.tensor_scalar(out=xn[:], in0=oft[:], scalar1=mv[:, 0:1], scalar2=rstd[:, 0:1], op0=ALU.subtract, op1=ALU.mult),
                         reads=[r_s, r_of], writes=[r_xn])
                else:
                    P.op("pool", lambda e, sg=sg, oft=oft, obt=obt, xn=xn, og=og: e.tensor_tensor(out=xn[:], in0=oft[:], in1=oft[:], op=ALU.mult), reads=[r_of], writes=[r_xn])
                    P.op("dve", lambda e, sg=sg, oft=oft, obt=obt, xn=xn, og=og: e.tensor_reduce(out=rstd[:], in_=xn[:].rearrange("p (a b) -> p a b", a=4), axis=AX.X, op=ALU.add),
                         reads=[r_xn], writes=[r_s])
                    emit_rsqrt(P, rstd[:], rstd[:], 1.0 / 128.0, eps5[:, 1:2], r_s)
                    for j in range(4):
                        P.op("dve", lambda e, j=j, sg=sg, oft=oft, obt=obt, xn=xn, og=og: e.scalar_tensor_tensor(out=xn[:, j * 128:(j + 1) * 128], in0=oft[:, j * 128:(j + 1) * 128],
                                                                          scalar=rstd[:, j:j + 1], in1=nw[:], op0=ALU.mult, op1=ALU.mult),
                             reads=[r_s, r_of, r_nw, r_xn], writes=[r_xn])
                P.op("dve", lambda e, sg=sg, oft=oft, obt=obt, xn=xn, og=og: e.tensor_tensor(out=og[:], in0=xn[:], in1=sg[:], op=ALU.mult), reads=[r_xn, r_sg], writes=[r_og])
                for j in range(4):
                    P.op("pe", lambda e, j=j, sg=sg, oft=oft, obt=obt, xn=xn, og=og: e.transpose(ps_tr[:, j * 128:(j + 1) * 128], og[:, j * 128:(j + 1) * 128], identb[:]),
                         reads=[r_og, r_idb], writes=[r_pstr], inc=(j == 3))
                P.op("act", lambda e, cgi=cgi, ts_=ts_, sg=sg, oft=oft, obt=obt, xn=xn, og=og: e.activation(out=big[:, cgi * 4:(cgi + 1) * 4, ts_],
                                                                    in_=ps_tr[:, :512].rearrange("p (a b) -> p a b", a=4), func=AF.Copy),
                     reads=[r_pstr], writes=[r_big])
        for cg2 in range(4):
            cs = slice(cg2 * 512, (cg2 + 1) * 512)
            wb = mmi[0] % 2; mmi[0] += 1
            P.dma("pool", W[wb][:, 0:8, :], wo_d[kq * 1024:(kq + 1) * 1024, cs].rearrange("(kc p) n -> p kc n", p=128), writes=[r_W[wb]])
            for t in range(nt):
                pb = t % 2
                ts_ = slice(t * 128, (t + 1) * 128)
                for kc in range(8):
                    P.op("pe", lambda e, kc=kc, pb=pb, wb=wb, ts_=ts_: e.matmul(ps_u[pb][:], lhsT=big[:, kc, ts_], rhs=W[wb][:, kc, :],
                                                                          start=(kc == 0), stop=(kc == 7)), reads=[r_big, r_W[wb]], writes=[r_psu[pb]], inc=(kc == 7))
                accum_h(t, cs, ps_u[pb], r_psu[pb], g1_d, kq == 0)
    for t in range(nt):
        emit_ln(P, hh, r_hh, t, ln_d, 0, 1, ln_scr)
        ts_ = slice(t * 128, (t + 1) * 128)
        for kg in range(4):
            pb = kg % 2
            for j in range(4):
                kc = kg * 4 + j
                P.op("pe", lambda e, kc=kc, j=j, pb=pb, t=t: e.transpose(ps_tf[pb][:, j * 128:(j + 1) * 128], hh[:, t, kc * 128:(kc + 1) * 128], identf[:]),
                     reads=[r_hh[t], r_idf], writes=[r_pstf[pb]], inc=(j == 3))
            for j in range(4):
                kc = kg * 4 + j
                P.op("act", lambda e, kc=kc, j=j, pb=pb, t=t, ts_=ts_: e.activation(out=uT[:, kc, ts_], in_=ps_tf[pb][:, j * 128:(j + 1) * 128], func=AF.Identity,
                                                                               scale=msc2[:, kc, t, 0:1], bias=msc2[:, kc, t, 1:2]),
                     reads=[r_pstf[pb], r_m2], writes=[r_uT])
    P.op("dve", lambda e: e.memset(mv[:], 0.0), reads=[r_W[0], r_W[1]], writes=[r_Wg[0], r_Wg[1], r_Wu[0], r_Wu[1], r_s])
    for fq in range(4):
        for fi in range(11):
            f = fq * 11 + fi
            wb = f % 2
            P.dma("pool", Wgu[wb][:, :, 0:128], w1_d[:, f * 128:(f + 1) * 128].rearrange("(kc p) n -> p kc n", p=128), writes=[r_Wg[wb]])
            P.dma("pool", Wgu[wb][:, :, 128:256], w1_d[:, DFF + f * 128:DFF + (f + 1) * 128].rearrange("(kc p) n -> p kc n", p=128), writes=[r_Wu[wb]])
            for gi_, (t0, t1) in enumerate(tgs):
                n = t1 - t0
                pb = gi_ % 2
                for kc in range(16):
                    P.op("pe", lambda e, kc=kc, pb=pb, wb=wb, t0=t0, t1=t1, n=n: e.matmul(ps_m[pb][:, :n], lhsT=Wgu[wb][:, kc, 0:128], rhs=uT[:, kc, t0:t1],
                                                                                    start=(kc == 0), stop=(kc == 15)), reads=[r_Wg[wb], r_uT], writes=[r_psm[pb]], inc=(kc == 15))
                for kc in range(16):
                    P.op("pe", lambda e, kc=kc, pb=pb, wb=wb, t0=t0, t1=t1, n=n: e.matmul(ps_u[pb][:, :n], lhsT=Wgu[wb][:, kc, 128:256], rhs=uT[:, kc, t0:t1],
                                                                                    start=(kc == 0), stop=(kc == 15)), reads=[r_Wu[wb], r_uT], writes=[r_psu[pb]], inc=(kc == 15))
                sg, r_sg = rot_sg.next()
                P.op("act", lambda e, pb=pb, n=n, sg=sg: e.activation(out=sg[:, :n], in_=ps_m[pb][:, :n], func=AF.Silu), reads=[r_psm[pb]], writes=[r_sg])
                P.op("dve", lambda e, pb=pb, n=n, fi=fi, t0=t0, t1=t1, sg=sg: e.tensor_tensor(out=big[:, fi, t0:t1], in0=ps_u[pb][:, :n], in1=sg[:, :n], op=ALU.mult),
                     reads=[r_psu[pb], r_sg], writes=[r_big])
        for cg2 in range(4):
            cs = slice(cg2 * 512, (cg2 + 1) * 512)
            wb = 1
            P.dma("pool", W[wb][:, 0:11, :], w2_d[fq * 1408:(fq + 1) * 1408, cs].rearrange("(kc p) n -> p kc n", p=128), writes=[r_W[wb]])
            for t in range(nt):
                pb = t % 2
                ts_ = slice(t * 128, (t + 1) * 128)
                for kc in range(11):
                    P.op("pe", lambda e, kc=kc, pb=pb, wb=wb, ts_=ts_: e.matmul(ps_tf[pb][:], lhsT=big[:, kc, ts_], rhs=W[wb][:, kc, :],
                                                                          start=(kc == 0), stop=(kc == 10)), reads=[r_big, r_W[wb]], writes=[r_pstf[pb]], inc=(kc == 10))
                accum_h(t, cs, ps_tf[pb], r_pstf[pb], g2_d, fq == 0)
    for t in range(nt):
        emit_ln(P, hh, r_hh, t, ln_d, 2, 3, ln_scr)
        P.dma("sp", out_d[t * 128:(t + 1) * 128, :], hh[:, t, :], reads=[r_hh[t]])
    P.finish()
    return nc


def run_blk(inputs, mod, layer, o, hcat):
    nc = build_blk(layer)
    T = 1152 if layer == 0 else 1024
    nt = T // 128
    m = mod[layer].reshape(5, 6, D)
    if layer == 0:
        wg = np.ascontiguousarray(inputs["ret_w_in"][0][:, 8192:12288]); wo = inputs["ret_w_out"][0]
        nw = np.ones((1, 128), np.float32)
    else:
        wg = np.ascontiguousarray(inputs["gdn_w_in"][0][:, 8192:12288]); wo = inputs["gdn_w_out"][0]
        nw = np.ascontiguousarray(inputs["gdn_norm"][0][None, :])
    lnrows = np.stack([inputs["ln_g"][layer, 0], inputs["ln_b"][layer, 0], inputs["ln_g"][layer, 1], inputs["ln_b"][layer, 1]], 0)
    in_maps = []
    for core in range(8):
        b, s = core // 2, core % 2
        t0 = s * T + (0 if layer == 0 else NCTX)
        rows = [(4 if (t0 + t * 128) < NCTX else b) for t in range(nt)]

        def msc(js, jh):
            a = np.stack([np.stack([m[r, js], m[r, jh]], -1) for r in rows], 1)
            return np.ascontiguousarray(a.reshape(16, 128, nt, 2).transpose(1, 0, 2, 3))
        h = hcat[b, t0:t0 + T]
        in_maps.append({
            "of": np.ascontiguousarray(o[b, 0, t0:t0 + T].reshape(T, 4096)), "ob": np.ascontiguousarray(o[b, 1, t0:t0 + T].reshape(T, 4096)),
            "h": np.ascontiguousarray(h), "hT": np.ascontiguousarray(h.T),
            "msc1": msc(1, 0), "msc2": msc(4, 3),
            "g1": np.ascontiguousarray(np.stack([m[r, 2] for r in rows], 0)), "g2": np.ascontiguousarray(np.stack([m[r, 5] for r in rows], 0)),
            "lnrows": np.ascontiguousarray(lnrows), "wg": wg, "wo": wo, "w1": inputs["ffn_w_in"][layer], "w2": inputs["ffn_w_out"][layer],
            "nw": nw, "identf": np.eye(128, dtype=np.float32)})
    res = run_bass_kernel_spmd(nc, in_maps, core_ids=list(range(8)))
    if layer == 0:
        out = np.zeros((4, NTOT, D), np.float32)
        for core in range(8):
            b, s = core // 2, core % 2
            out[b, s * T:(s + 1) * T] = res.results[core]["out"]
    else:
        out = np.zeros((4, NLAT, D), np.float32)
        for core in range(8):
            b, s = core // 2, core % 2
            out[b, s * T:(s + 1) * T] = res.results[core]["out"]
    return out


def build_gdn(n_heads=16):
    nc = new_nc()
    T = NTOT
    NCH = T // 128
    hT = nc.dram_tensor("hT", [D, T], F32, kind="ExternalInput").ap()
    msc_d = nc.dram_tensor("msc", [128, 16, 4], F32, kind="ExternalInput").ap()
    wqkv = nc.dram_tensor("wqkv", [D, 8192], F32, kind="ExternalInput").ap()
    wba = nc.dram_tensor("wba", [D, 64], F32, kind="ExternalInput").ap()
    convw_d = nc.dram_tensor("convw", [128, 64, 5], F32, kind="ExternalInput").ap()
    alog_d = nc.dram_tensor("alog", [1, 32], F32, kind="ExternalInput").ap()
    dtb_d = nc.dram_tensor("dtb", [1, 32], F32, kind="ExternalInput").ap()
    id_d = nc.dram_tensor("identf", [128, 128], F32, kind="ExternalInput").ap()
    U_d = nc.dram_tensor("U", [128, 128], F32, kind="ExternalInput").ap()
    msl_d = nc.dram_tensor("msl", [128, 128], F32, kind="ExternalInput").ap()
    o_d = nc.dram_tensor("o", [T, 32, 128], F32, kind="ExternalOutput").ap()
    P = Prog(nc)
    uT = P.sbuf([128, 16, T], BF16); r_uT = P.res()
    emit_uT(P, hT, msc_d, uT, r_uT, T, [(0, NCTX, 0), (NCTX, T, 1)])
    identf = P.sbuf([128, 128], F32); r_idf = P.res()
    identb = P.sbuf([128, 128], BF16); r_idb = P.res()
    U = P.sbuf([128, 128], F32); r_U = P.res()
    msl = P.sbuf([128, 128], F32); r_msl = P.res()
    onesf = P.sbuf([128, 128], F32); r_ones = P.res()
    cst = P.sbuf([128, 2], F32); r_cst = P.res()
    convw = P.sbuf([128, 64, 5], F32); r_cw = P.res()
    alog = P.sbuf([128, 32], F32); r_al = P.res()
    dtb = P.sbuf([128, 32], F32); r_dtb = P.res()
    P.dma("sp", identf[:], id_d, writes=[r_idf])
    P.dma("sp", U[:], U_d, writes=[r_U])
    P.dma("sp", msl[:], msl_d, writes=[r_msl])
    P.dma("sp", convw[:], convw_d, writes=[r_cw])
    P.dma("sp", alog[:], alog_d.partition_broadcast(128), writes=[r_al])
    P.dma("sp", dtb[:], dtb_d.partition_broadcast(128), writes=[r_dtb])
    P.op("dve", lambda e: e.tensor_copy(out=identb[:], in_=identf[:]), reads=[r_idf], writes=[r_idb])
    P.op("dve", lambda e: e.memset(onesf[:], 1.0), writes=[r_ones])
    P.op("dve", lambda e: e.memset(cst[:, 0:1], 1e-6), writes=[r_cst])
    P.op("dve", lambda e: e.memset(cst[:, 1:2], 1.0), writes=[r_cst])
    P.op("act", lambda e: e.activation(out=alog[:], in_=alog[:], func=AF.Exp), reads=[r_al], writes=[r_al])
    bank = [P.psum([128, 512], F32) for _ in range(8)]; r_bk = [P.res(excl=True) for _ in range(8)]
    ps_p = [bank[0], bank[4]]; r_pp = [r_bk[0], r_bk[4]]
    bank_bf = bank[3]
    beta = P.sbuf([128, NCH, 32], F32); r_beta = P.res()
    negb = P.sbuf([128, NCH, 32], F32); r_negb = P.res()
    la = P.sbuf([128, NCH, 32], F32); r_la = P.res()
    wba_t = P.sbuf([128, 16, 64], BF16); r_wba = P.res()
    t32 = P.sbuf([128, 32], F32); r_t32 = P.res()
    P.dma("pool", wba_t[:], wba.rearrange("(kc p) n -> p kc n", p=128), writes=[r_wba])
    for c in range(NCH):
        pb = c % 2
        for kc in range(16):
            P.op("pe", lambda e, c=c, kc=kc, pb=pb: e.matmul(ps_p[pb][:, :64], lhsT=uT[:, kc, c * 128:(c + 1) * 128], rhs=wba_t[:, kc, :],
                                                       start=(kc == 0), stop=(kc == 15)), reads=[r_uT, r_wba], writes=[r_pp[pb]], inc=(kc == 15))
        P.op("act", lambda e, c=c, pb=pb: e.activation(out=beta[:, c, :], in_=ps_p[pb][:, 0:32], func=AF.Sigmoid), reads=[r_pp[pb]], writes=[r_beta])
        P.op("dve", lambda e, pb=pb: e.tensor_tensor(out=t32[:], in0=ps_p[pb][:, 32:64], in1=dtb[:], op=ALU.add), reads=[r_pp[pb], r_dtb], writes=[r_t32])
        P.op("act", lambda e: e.activation(out=t32[:], in_=t32[:], func=AF.Exp), reads=[r_t32], writes=[r_t32])
        P.op("act", lambda e: e.activation(out=t32[:], in_=t32[:], func=AF.Ln, bias=cst[:, 1:2], scale=1.0), reads=[r_t32, r_cst], writes=[r_t32])
        P.op("dve", lambda e, c=c: e.scalar_tensor_tensor(out=la[:, c, :], in0=t32[:], scalar=-1.0, in1=alog[:], op0=ALU.mult, op1=ALU.mult),
             reads=[r_t32, r_al], writes=[r_la])
    P.op("dve", lambda e: e.tensor_scalar_mul(out=negb[:], in0=beta[:], scalar1=-1.0), reads=[r_beta], writes=[r_negb])
    LP = 2 + NCTX + 2 + 2 + NLAT + 2
    xp = P.sbuf([128, LP], F32); r_xp = P.res()
    acc = P.sbuf([128, T], F32); r_acc = P.res()
    ys = P.sbuf([128, T], F32); r_ys = P.res()
    sq = P.sbuf([128, 512], F32); r_sq = P.res()
    rn = P.sbuf([128, 512], F32); r_rn = P.res()
    qT = P.sbuf([128, T], BF16); r_qT = P.res()
    kT = P.sbuf([128, T], BF16); r_kT = P.res()
    kTok = P.sbuf([128, NCH, 128], BF16); r_kTok = P.res()
    vb = [P.sbuf([128, NCH, 128], BF16) for _ in range(2)]; r_vb = [P.res() for _ in range(2)]
    Wp = [P.sbuf([128, 16, 128], BF16) for _ in range(2)]; r_Wp = [P.res() for _ in range(2)]
    P.op("pool", lambda e: e.memset(xp[:], 0.0), writes=[r_xp])
    ps_trb = P.psum([128, 1024], BF16) if False else None
    class NS:
        pass

    def f32t():
        return P.sbuf([128, 128], F32), P.res()

    def bf16t():
        return P.sbuf([128, 128], BF16), P.res()
    ST = []
    for s_ in range(2):
        S = NS()
        S.LAU, S.r_LAU = f32t(); S.Zabs, S.r_Z = f32t(); S.decA, S.r_decA = f32t(); S.egR, S.r_egR = f32t()
        S.t1, S.r_t1 = f32t(); S.t2, S.r_t2 = f32t()
        S.Xs = [f32t() for _ in range(2)]; S.Ys = [f32t() for _ in range(2)]; S.Ts = [f32t() for _ in range(2)]
        S.Ttb, S.r_Ttb = bf16t()
        S.cols = P.sbuf([128, 8], F32); S.r_cols = P.res()
        S.qdT, S.r_qdT = bf16t(); S.kbg, S.r_kbg = bf16t(); S.kd, S.r_kd = bf16t(); S.nwT, S.r_nwT = bf16t()
        S.u_sb, S.r_u = bf16t(); S.QKT, S.r_QKT = bf16t()
        S.o_sb = [P.sbuf([128, 128], F32) for _ in range(2)]; S.r_osb = [P.res() for _ in range(2)]
        S.m, S.r_m = f32t(); S.m_bf, S.r_mbf = bf16t()
        S.b = bank[4 * s_:4 * s_ + 4]; S.rb = r_bk[4 * s_:4 * s_ + 4]
        ST.append(S)
    groups = [(0, 256), (256, 768), (768, 1280), (1280, 1792), (1792, 2304)]
    segs = [(2, 0, NCTX), (2 + NCTX + 4, NCTX, NLAT)]
    B = [bank[1], bank[5]]; RB = [r_bk[1], r_bk[5]]
    wi = [0]

    def project_conv_silu(g):
        wb = wi[0] % 2; wi[0] += 1
        P.dma("pool", Wp[wb][:], wqkv[:, g * 128:(g + 1) * 128].rearrange("(kc p) n -> p kc n", p=128), writes=[r_Wp[wb]])
        for gi, (t0, t1) in enumerate(groups):
            n = t1 - t0; pb = gi % 2
            for kc in range(16):
                P.op("pe", lambda e, kc=kc, pb=pb, wb=wb, t0=t0, t1=t1, n=n: e.matmul(ps_p[pb][:, :n], lhsT=Wp[wb][:, kc, :], rhs=uT[:, kc, t0:t1],
                                                                                start=(kc == 0), stop=(kc == 15)), reads=[r_Wp[wb], r_uT], writes=[r_pp[pb]], inc=(kc == 15))
            off = 2 + t0 if t0 < NCTX else 2 + NCTX + 4 + (t0 - NCTX)
            P.op("act", lambda e, pb=pb, n=n, off=off: e.activation(out=xp[:, off:off + n], in_=ps_p[pb][:, :n], func=AF.Copy), reads=[r_pp[pb]], writes=[r_xp])
        for (xo, to, L) in segs:
            P.op("dve", lambda e, xo=xo, to=to, L=L, g=g: e.tensor_scalar_mul(out=acc[:, to:to + L], in0=xp[:, xo - 2:xo - 2 + L], scalar1=convw[:, g, 0:1]),
                 reads=[r_xp, r_cw], writes=[r_acc])
            for j in range(1, 5):
                P.op("dve", lambda e, xo=xo, to=to, L=L, g=g, j=j: e.scalar_tensor_tensor(out=acc[:, to:to + L], in0=xp[:, xo - 2 + j:xo - 2 + j + L], scalar=convw[:, g, j:j + 1],
                                                                                        in1=acc[:, to:to + L], op0=ALU.mult, op1=ALU.add), reads=[r_xp, r_cw, r_acc], writes=[r_acc])
        P.op("act", lambda e: e.activation(out=ys[:], in_=acc[:], func=AF.Silu), reads=[r_acc], writes=[r_ys])

    def l2norm_to(dst, r_dst):
        for gi, (t0, t1) in enumerate(groups):
            n = t1 - t0; pb = gi % 2
            P.op("pool", lambda e, t0=t0, t1=t1, n=n: e.tensor_tensor(out=sq[:, :n], in0=ys[:, t0:t1], in1=ys[:, t0:t1], op=ALU.mult), reads=[r_ys], writes=[r_sq])
            P.op("pe", lambda e, pb=pb, n=n: e.matmul(ps_p[pb][:, :n], lhsT=onesf[:], rhs=sq[:, :n], start=True, stop=True), reads=[r_ones, r_sq], writes=[r_pp[pb]])
            P.op("act", lambda e, pb=pb, n=n: e.activation(out=rn[:, :n], in_=ps_p[pb][:, :n], func=AF.Sqrt, bias=cst[:, 0:1], scale=1.0), reads=[r_pp[pb], r_cst], writes=[r_rn])
            P.op("dve", lambda e, n=n: e.reciprocal(out=rn[:, :n], in_=rn[:, :n]), reads=[r_rn], writes=[r_rn])
            P.op("dve", lambda e, t0=t0, t1=t1, n=n: e.tensor_tensor(out=dst[:, t0:t1], in0=ys[:, t0:t1], in1=rn[:, :n], op=ALU.mult), reads=[r_ys, r_rn], writes=[r_dst])

    for hq in range(n_heads):
        project_conv_silu(hq); l2norm_to(qT, r_qT)
        project_conv_silu(16 + hq); l2norm_to(kT, r_kT)
        for c in range(NCH):
            P.op("pe", lambda e, c=c: e.matmul(B[0][:, :128], lhsT=kT[:, c * 128:(c + 1) * 128], rhs=identb[:], start=True, stop=True), reads=[r_kT, r_idb], writes=[RB[0]])
            P.op("act", lambda e, c=c: e.activation(out=kTok[:, c, :], in_=B[0][:, :128], func=AF.Copy), reads=[RB[0]], writes=[r_kTok])
        for e_ in range(2):
            hv = 2 * hq + e_
            project_conv_silu(32 + hv)
            for c in range(NCH):
                pb = c % 2
                P.op("pe", lambda e, c=c, pb=pb: e.matmul(B[pb][:, :128], lhsT=ys[:, c * 128:(c + 1) * 128], rhs=identf[:], start=True, stop=True), reads=[r_ys, r_idf], writes=[RB[pb]])
                P.op("act", lambda e, c=c, pb=pb, e_=e_, hv=hv: e.activation(out=vb[e_][:, c, :], in_=B[pb][:, :128], func=AF.Identity, scale=beta[:, c, hv:hv + 1]),
                     reads=[RB[pb], r_beta], writes=[r_vb[e_]])
        def unit(e_, c, S):
            hv = 2 * hq + e_
            cs = slice(c * 128, (c + 1) * 128)
            ob = c % 2
            b, rb, cols, r_cols = S.b, S.rb, S.cols, S.r_cols
            R_ = b[0][:, 0:128]; G_ = b[1][:, 128:256]; Y2_ = b[1][:, 256:384]; M_ = b[1][:, 384:512]
            TT_ = b[2][:, 0:128]; UU_ = b[2][:, 128:256]; X2_ = b[3][:, 0:128]; W_ = b[3][:, 128:256]
            P.op("dve", lambda e: e.tensor_scalar_mul(out=S.LAU[:], in0=U[:], scalar1=la[:, c, hv:hv + 1]), reads=[r_U, r_la], writes=[S.r_LAU]); yield
            P.op("pe", lambda e: e.matmul(R_, lhsT=onesf[:], rhs=S.LAU[:], start=True, stop=True), reads=[r_ones, S.r_LAU], writes=[rb[0]]); yield
            P.op("pe", lambda e: e.matmul(b[1][:, 0:128], lhsT=S.LAU[:], rhs=onesf[:], start=True, stop=True), reads=[r_ones, S.r_LAU], writes=[rb[1]]); yield
            P.op("act", lambda e: e.activation(out=cols[:, 0:1], in_=b[1][:, 0:1], func=AF.Copy), reads=[rb[1]], writes=[r_cols]); yield
            P.op("act", lambda e: e.activation(out=cols[:, 5:6], in_=b[1][:, 0:1], func=AF.Identity, scale=-1.0), reads=[rb[1], r_cols], writes=[r_cols]); yield
            P.op("act", lambda e: e.activation(out=S.Zabs[:], in_=R_, func=AF.Abs, bias=cols[:, 5:6], scale=1.0), reads=[rb[0], r_cols], writes=[S.r_Z]); yield
            P.op("act", lambda e: e.activation(out=S.decA[:], in_=S.Zabs[:], func=AF.Exp, scale=-1.0), reads=[S.r_Z], writes=[S.r_decA]); yield
            P.op("dve", lambda e: e.tensor_scalar(out=cols[:, 3:4], in0=b[0][:, 127:128], scalar1=cols[:, 0:1], scalar2=None, op0=ALU.subtract),
                 reads=[rb[0], r_cols], writes=[r_cols]); yield
            P.op("act", lambda e: e.activation(out=cols[:, 1:2], in_=cols[:, 0:1], func=AF.Exp), reads=[r_cols], writes=[r_cols]); yield
            P.op("act", lambda e: e.activation(out=cols[:, 2:3], in_=b[0][:, 127:128], func=AF.Exp), reads=[rb[0], r_cols], writes=[r_cols]); yield
            P.op("act", lambda e: e.activation(out=cols[:, 3:4], in_=cols[:, 3:4], func=AF.Exp), reads=[r_cols], writes=[r_cols]); yield
            P.op("act", lambda e: e.activation(out=S.egR[:], in_=R_, func=AF.Exp), reads=[rb[0]], writes=[S.r_egR]); yield
            P.op("dve", lambda e: e.tensor_tensor(out=S.qdT[:], in0=qT[:, cs], in1=S.egR[:], op=ALU.mult), reads=[r_qT, S.r_egR], writes=[S.r_qdT]); yield
            P.op("dve", lambda e: e.tensor_tensor(out=cols[:, 4:5], in0=cols[:, 1:2], in1=beta[:, c, hv:hv + 1], op=ALU.mult), reads=[r_cols, r_beta], writes=[r_cols]); yield
            P.op("pool", lambda e: e.tensor_tensor(out=S.t1[:], in0=S.decA[:], in1=msl[:], op=ALU.mult), reads=[S.r_decA, r_msl], writes=[S.r_t1]); yield
            P.op("pool", lambda e: e.tensor_tensor(out=S.t2[:], in0=S.decA[:], in1=U[:], op=ALU.mult), reads=[S.r_decA, r_U], writes=[S.r_t2]); yield
            X0, rX0 = S.Xs[0]; Y0, rY0 = S.Ys[0]; T0, rT0 = S.Ts[0]
            P.op("pe", lambda e: e.matmul(G_, lhsT=kT[:, cs], rhs=kT[:, cs], start=True, stop=True), reads=[r_kT], writes=[rb[1]]); yield
            P.op("dve", lambda e: e.scalar_tensor_tensor(out=X0[:], in0=G_, scalar=negb[:, c, hv:hv + 1], in1=S.t1[:], op0=ALU.mult, op1=ALU.mult),
                 reads=[rb[1], r_negb, S.r_t1], writes=[rX0]); yield
            P.op("pe", lambda e: e.matmul(TT_, lhsT=X0[:], rhs=identf[:], start=True, stop=True), reads=[rX0, r_idf], writes=[rb[2]]); yield
            P.op("act", lambda e: e.activation(out=Y0[:], in_=TT_, func=AF.Copy), reads=[rb[2]], writes=[rY0]); yield
            P.op("dve", lambda e: e.tensor_tensor(out=T0[:], in0=TT_, in1=identf[:], op=ALU.add), reads=[rb[2], r_idf], writes=[rT0]); yield
            cur = 0
            for lvl in range(1, 7):
                Xc, rXc = S.Xs[cur]; Yc, rYc = S.Ys[cur]; Tc, rTc = S.Ts[cur]
                Xn, rXn = S.Xs[1 - cur]; Yn, rYn = S.Ys[1 - cur]; Tn, rTn = S.Ts[1 - cur]
                last = (lvl == 6)
                P.op("pe", lambda e, Xc=Xc, Yc=Yc: e.matmul(X2_, lhsT=Yc[:], rhs=Xc[:], start=True, stop=True), reads=[rXc, rYc], writes=[rb[3]]); yield
                if not last:
                    P.op("pe", lambda e, Xc=Xc, Yc=Yc: e.matmul(Y2_, lhsT=Xc[:], rhs=Yc[:], start=True, stop=True), reads=[rXc, rYc], writes=[rb[1]]); yield
                P.op("act", lambda e, Xn=Xn: e.activation(out=Xn[:], in_=X2_, func=AF.Copy), reads=[rb[3]], writes=[rXn]); yield
                if not last:
                    P.op("dve", lambda e, Yn=Yn: e.tensor_copy(out=Yn[:], in_=Y2_), reads=[rb[1]], writes=[rYn]); yield
                P.op("pe", lambda e, Xn=Xn, Tc=Tc: e.matmul(TT_, lhsT=Xn[:], rhs=Tc[:], start=True, stop=False), reads=[rXn, rTc], writes=[rb[2]], inc=False)
                P.op("pe", lambda e, Tc=Tc: e.matmul(TT_, lhsT=identf[:], rhs=Tc[:], start=False, stop=True), reads=[r_idf, rTc], writes=[rb[2]]); yield
                if not last:
                    P.op("dve", lambda e, Tn=Tn: e.tensor_copy(out=Tn[:], in_=TT_), reads=[rb[2]], writes=[rTn]); yield
                else:
                    P.op("dve", lambda e: e.tensor_copy(out=S.Ttb[:], in_=TT_), reads=[rb[2]], writes=[S.r_Ttb]); yield
                cur = 1 - cur
            P.op("act", lambda e: e.activation(out=S.kbg[:], in_=kTok[:, c, :], func=AF.Identity, scale=cols[:, 4:5]), reads=[r_kTok, r_cols], writes=[S.r_kbg]); yield
            P.op("act", lambda e: e.activation(out=S.kd[:], in_=kTok[:, c, :], func=AF.Identity, scale=cols[:, 3:4]), reads=[r_kTok, r_cols], writes=[S.r_kd]); yield
            P.op("pe", lambda e: e.matmul(W_, lhsT=S.kbg[:], rhs=S.Ttb[:], start=True, stop=True), reads=[S.r_kbg, S.r_Ttb], writes=[rb[3]]); yield
            P.op("act", lambda e: e.activation(out=S.nwT[:], in_=W_, func=AF.Identity, scale=-1.0), reads=[rb[3]], writes=[S.r_nwT]); yield
            if c > 0:
                P.op("pe", lambda e: e.matmul(UU_, lhsT=S.Ttb[:], rhs=vb[e_][:, c, :], start=True, stop=False), reads=[S.r_Ttb, r_vb[e_]], writes=[rb[2]], inc=False)
                P.op("pe", lambda e: e.matmul(UU_, lhsT=S.nwT[:], rhs=S.m_bf[:], start=False, stop=True), reads=[S.r_nwT, S.r_mbf], writes=[rb[2]]); yield
            else:
                P.op("pe", lambda e: e.matmul(UU_, lhsT=S.Ttb[:], rhs=vb[e_][:, c, :], start=True, stop=True), reads=[S.r_Ttb, r_vb[e_]], writes=[rb[2]]); yield
            P.op("act", lambda e: e.activation(out=S.u_sb[:], in_=UU_, func=AF.Copy), reads=[rb[2]], writes=[S.r_u]); yield
            P.op("pe", lambda e: e.matmul(G_, lhsT=kT[:, cs], rhs=qT[:, cs], start=True, stop=True), reads=[r_kT, r_qT], writes=[rb[1]]); yield
            P.op("dve", lambda e: e.tensor_tensor(out=S.QKT[:], in0=G_, in1=S.t2[:], op=ALU.mult), reads=[rb[1], S.r_t2], writes=[S.r_QKT]); yield
            if c > 0:
                P.op("pe", lambda e: e.matmul(R_, lhsT=S.qdT[:], rhs=S.m_bf[:], start=True, stop=False), reads=[S.r_qdT, S.r_mbf], writes=[rb[0]], inc=False)
            P.op("pe", lambda e: e.matmul(R_, lhsT=S.QKT[:], rhs=S.u_sb[:], start=(c == 0), stop=True), reads=[S.r_QKT, S.r_u], writes=[rb[0]]); yield
            P.op("act", lambda e: e.activation(out=S.o_sb[ob][:], in_=R_, func=AF.Identity, scale=128.0 ** -0.5), reads=[rb[0]], writes=[S.r_osb[ob]]); yield
            P.dma("sp", o_d[c * 128:(c + 1) * 128, hv, :], S.o_sb[ob][:], reads=[S.r_osb[ob]]); yield
            if c < NCH - 1:
                P.op("pe", lambda e: e.matmul(M_, lhsT=S.kd[:], rhs=S.u_sb[:], start=True, stop=True), reads=[S.r_kd, S.r_u], writes=[rb[1]]); yield
                if c == 0:
                    P.op("dve", lambda e: e.tensor_copy(out=S.m[:], in_=M_), reads=[rb[1]], writes=[S.r_m]); yield
                else:
                    P.op("dve", lambda e: e.scalar_tensor_tensor(out=S.m[:], in0=S.m[:], scalar=cols[:, 2:3], in1=M_, op0=ALU.mult, op1=ALU.add),
                         reads=[S.r_m, r_cols, rb[1]], writes=[S.r_m]); yield
                P.op("pool", lambda e: e.tensor_copy(out=S.m_bf[:], in_=S.m[:]), reads=[S.r_m], writes=[S.r_mbf]); yield

        for c in range(NCH):
            g0 = unit(0, c, ST[0]); g1 = unit(1, c, ST[1])
            d0 = d1 = False
            while not (d0 and d1):
                if not d0:
                    try:
                        next(g0)
                    except StopIteration:
                        d0 = True
                if not d1:
                    try:
                        next(g1)
                    except StopIteration:
                        d1 = True
    P.finish()
    return nc


def run_gdn(inputs, mod, hcat, n_heads=16):
    nc = build_gdn(n_heads)
    w = inputs["gdn_w_in"][0]
    wqkv = np.ascontiguousarray(w[:, 0:8192])
    kk = np.arange(128)[:, None]; ii = np.arange(128)[None, :]
    U = (kk <= ii).astype(np.float32)
    msl = (kk > ii).astype(np.float32)
    in_maps = []
    for core in range(8):
        b, d = core // 2, core % 2
        h = hcat[b]
        if d:
            h = _seg_flip_np(h)
        cw = inputs["gdn_conv"][0]
        if d:
            cw = cw[::-1]
        convw = np.ascontiguousarray(cw.T.reshape(64, 128, 5).transpose(1, 0, 2))
        wba = np.ascontiguousarray(np.concatenate([w[:, 12288 + d * 32:12288 + (d + 1) * 32], w[:, 12352 + d * 32:12352 + (d + 1) * 32]], 1))
        in_maps.append({
            "hT": np.ascontiguousarray(h.T), "msc": _msc(mod, 1, b, 1, 0), "wqkv": wqkv, "wba": wba, "convw": convw,
            "alog": np.ascontiguousarray(inputs["gdn_a_log"][0, d][None, :]), "dtb": np.ascontiguousarray(inputs["gdn_dt_bias"][0, d][None, :]),
            "identf": np.eye(128, dtype=np.float32), "U": U, "msl": msl})
    res = run_bass_kernel_spmd(nc, in_maps, core_ids=list(range(8)))
    out = np.zeros((4, 2, NTOT, 32, 128), np.float32)
    for core in range(8):
        b, d = core // 2, core % 2
        o = res.results[core]["o"]
        out[b, d] = _seg_flip_np(o) if d else o
    return out


def kernel(**inputs):
    inputs = {k: np.asarray(v, dtype=np.float32) for k, v in inputs.items()}
    mod = run_ada(inputs)
    hcat = np.concatenate([inputs["ctx"], inputs["x"]], axis=1)
    o = run_ret(inputs, mod, hcat)
    h1 = run_blk(inputs, mod, 0, o.reshape(4, 2, NTOT, 4096), hcat)
    o2 = run_gdn(inputs, mod, h1)
    out = run_blk(inputs, mod, 1, o2.reshape(4, 2, NTOT, 4096), h1)
    return out.astype(np.float32)
```

```python
import contextlib
import numpy as np
import concourse.bass as bass
import concourse.mybir as mybir
from concourse.bass_utils import run_bass_kernel_spmd

F32 = mybir.dt.float32
BF16 = mybir.dt.bfloat16
AF = mybir.ActivationFunctionType
ALU = mybir.AluOpType
AX = mybir.AxisListType

D = 2048
NCTX = 256
NLAT = 2048
NTOT = NCTX + NLAT
DFF = 5632
ALPHA = 4.0 ** 0.25
LN_EPS = 1e-5


class Res:
    __slots__ = ("w", "rd", "name", "excl")

    def __init__(self, name="", excl=False):
        self.w = []
        self.rd = {}
        self.name = name
        self.excl = excl


class Prog:
    ENG = ("pe", "act", "dve", "pool", "sp")

    def __init__(self, nc, n_slots=32):
        self.nc = nc
        self.stack = contextlib.ExitStack()
        self.sem = {}
        self.cnt = {}
        self.pending = {e: False for e in self.ENG}
        self.seen = {e: {} for e in self.ENG}
        self.streams = {e: [] for e in self.ENG}
        for e in self.ENG:
            if e == "sp":
                continue
            self.sem[e] = self.stack.enter_context(nc.semaphore("s_" + e))
            self.cnt[e] = 0
        self.n_slots = n_slots
        for i in range(n_slots):
            n = "d%d" % i
            self.sem[n] = self.stack.enter_context(nc.semaphore("s_" + n))
            self.cnt[n] = 0
        self.slot_i = 0
        self.slot_p = 0
        self.n_inst = {e: 0 for e in self.ENG}
        self._uid = 0

    def sbuf(self, shape, dtype, name=None):
        self._uid += 1
        return self.stack.enter_context(self.nc.sbuf_tensor(name or ("t%d" % self._uid), list(shape), dtype))

    def psum(self, shape, dtype, name=None):
        self._uid += 1
        return self.stack.enter_context(self.nc.psum_tensor(name or ("p%d" % self._uid), list(shape), dtype))

    def res(self, name="", excl=False):
        return Res(name, excl)

    def _need(self, eng, reads, writes):
        need = {}
        for r in reads:
            for (s, v) in r.w:
                if need.get(s, 0) < v:
                    need[s] = v
        for r in writes:
            for (s, v) in r.w:
                if need.get(s, 0) < v:
                    need[s] = v
            for s, v in r.rd.items():
                if need.get(s, 0) < v:
                    need[s] = v
        seen = self.seen[eng]
        for s, v in need.items():
            if s == "pe" and eng == "pe":
                continue
            if seen.get(s, 0) >= v:
                continue
            seen[s] = v
            sem = self.sem[s]
            self.streams[eng].append(lambda e, sem=sem, v=v: e.wait_ge(sem, v))

    def op(self, eng, fn, reads=(), writes=(), inc=True):
        if eng != "pe":
            ex = [r for r in reads if r.excl]
            if ex:
                reads = [r for r in reads if not r.excl]
                writes = list(writes) + ex
        self._need(eng, reads, writes)
        self.n_inst[eng] += 1
        if inc:
            self.cnt[eng] += 1
            val = self.cnt[eng]
            sem = self.sem[eng]
            self.streams[eng].append(lambda e, fn=fn, sem=sem: fn(e).then_inc(sem, 1))
            self.pending[eng] = False
        else:
            val = self.cnt[eng] + 1
            self.streams[eng].append(lambda e, fn=fn: fn(e))
            self.pending[eng] = True
        for r in reads:
            if r.rd.get(eng, 0) < val:
                r.rd[eng] = val
        for r in writes:
            r.w = [(eng, val)]
            r.rd = {}
        return val

    def dma(self, q, out, in_, reads=(), writes=(), **kw):
        half = self.n_slots // 2
        if q == "pool":
            s = "d%d" % (half + self.slot_p)
            self.slot_p = (self.slot_p + 1) % half
        else:
            s = "d%d" % self.slot_i
            self.slot_i = (self.slot_i + 1) % half
        self._need(q, reads, writes)
        seen = self.seen[q]
        if seen.get(s, 0) < self.cnt[s]:
            seen[s] = self.cnt[s]
            self.streams[q].append(lambda e, sem=self.sem[s], v=self.cnt[s]: e.wait_ge(sem, v))
        self.cnt[s] += 16
        val = self.cnt[s]
        sem = self.sem[s]
        self.streams[q].append(lambda e, sem=sem, out=out, in_=in_, kw=kw: e.dma_start(out=out, in_=in_, **kw).then_inc(sem, 16))
        self.n_inst[q] += 1
        for r in reads:
            if r.rd.get(s, 0) < val:
                r.rd[s] = val
        for r in writes:
            r.w = [(s, val)]
            r.rd = {}
        return (s, val)

    def finish(self):
        for e in self.ENG:
            assert not self.pending[e], "engine %s has pending un-signalled ops" % e
        for i in range(self.n_slots):
            s = "d%d" % i
            if self.cnt[s] > 0:
                self.streams["sp"].append(lambda e, sem=self.sem[s], v=self.cnt[s]: e.wait_ge(sem, v))
        for en in ("pe", "act", "dve", "pool"):
            if self.cnt[en] > 0:
                self.streams["sp"].append(lambda e, sem=self.sem[en], v=self.cnt[en]: e.wait_ge(sem, v))
        nc = self.nc
        streams = self.streams
        with nc.Block() as block:
            @block.sync
            def _(e):
                for f in streams["sp"]:
                    f(e)

            @block.tensor
            def _(e):
                for f in streams["pe"]:
                    f(e)

            @block.scalar
            def _(e):
                for f in streams["act"]:
                    f(e)

            @block.vector
            def _(e):
                for f in streams["dve"]:
                    f(e)

            @block.gpsimd
            def _(e):
                for f in streams["pool"]:
                    f(e)
        self.stack.close()


def new_nc():
    return bass.Bass("TRN2", target_bir_lowering=False)


def build_ada():
    nc = new_nc()
    cc = nc.dram_tensor("cc", [5, 2048], F32, kind="ExternalInput").ap()
    adaw = nc.dram_tensor("adaw", [2, 2048, 1536], F32, kind="ExternalInput").ap()
    adab = nc.dram_tensor("adab", [2, 1, 1536], F32, kind="ExternalInput").ap()
    ident_d = nc.dram_tensor("ident", [128, 128], F32, kind="ExternalInput").ap()
    mod = nc.dram_tensor("mod", [2, 5, 1536], F32, kind="ExternalOutput").ap()
    P = Prog(nc)
    cc_t = P.sbuf([5, 2048], F32); r_cc = P.res()
    sc = P.sbuf([5, 2048], F32); r_sc = P.res()
    ident = P.sbuf([128, 128], F32); r_id = P.res()
    ones = P.sbuf([1, 8], F32); r_ones = P.res()
    scT = P.sbuf([128, 16, 5], F32); r_scT = P.res()
    bias = P.sbuf([1, 2, 1536], F32); r_bias = P.res()
    P.dma("sp", cc_t[:], cc, writes=[r_cc])
    P.dma("sp", ident[:], ident_d, writes=[r_id])
    P.dma("sp", bias[:], adab.rearrange("l o n -> o l n"), writes=[r_bias])
    P.op("dve", lambda e: e.memset(ones[:], 1.0), writes=[r_ones])
    P.op("act", lambda e: e.activation(out=sc[:], in_=cc_t[:], func=AF.Silu), reads=[r_cc], writes=[r_sc])
    ps_t = P.psum([128, 16, 5], F32); r_pst = P.res()
    for kc in range(16):
        P.op("pe", lambda e, kc=kc: e.transpose(ps_t[:, kc, :], sc[:, kc * 128:(kc + 1) * 128], ident[:5, :5]),
             reads=[r_sc, r_id], writes=[r_pst], inc=(kc == 15))
    P.op("dve", lambda e: e.tensor_copy(out=scT[:], in_=ps_t[:]), reads=[r_pst], writes=[r_scT])
    wt = [P.sbuf([128, 16, 512], F32) for _ in range(3)]; r_wt = [P.res() for _ in range(3)]
    ps = [P.psum([5, 512], F32) for _ in range(2)]; r_ps = [P.res() for _ in range(2)]
    ot = [P.sbuf([5, 512], F32) for _ in range(2)]; r_ot = [P.res() for _ in range(2)]
    it = 0
    for l in range(2):
        for g in range(3):
            b = it % 3; pb = it % 2
            q = ["sp", "pool", "act"][it % 3]
            P.dma(q, wt[b][:], adaw[l, :, g * 512:(g + 1) * 512].rearrange("(kc p) n -> p kc n", p=128), writes=[r_wt[b]])
            for kc in range(16):
                P.op("pe", lambda e, kc=kc, b=b, pb=pb: e.matmul(ps[pb][:], lhsT=scT[:, kc, :], rhs=wt[b][:, kc, :], start=(kc == 0), stop=False),
                     reads=[r_scT, r_wt[b]], writes=[r_ps[pb]], inc=False)
            P.op("pe", lambda e, l=l, g=g, pb=pb: e.matmul(ps[pb][:], lhsT=ones[:, :5], rhs=bias[:, l, g * 512:(g + 1) * 512], start=False, stop=True),
                 reads=[r_ones, r_bias], writes=[r_ps[pb]], inc=True)
            P.op("dve", lambda e, pb=pb: e.tensor_copy(out=ot[pb][:], in_=ps[pb][:]), reads=[r_ps[pb]], writes=[r_ot[pb]])
            P.dma("sp", mod[l, :, g * 512:(g + 1) * 512], ot[pb][:], reads=[r_ot[pb]])
            it += 1
    P.finish()
    return nc


def emit_uT(P, hT, msc_d, uT, r_uT, T, segs):
    msc = P.sbuf([128, 16, 4], F32); r_msc = P.res()
    P.dma("sp", msc[:], msc_d, writes=[r_msc])
    P.op("dve", lambda e: e.tensor_scalar_add(out=msc[:, :, 0:1], in0=msc[:, :, 0:1], scalar1=1.0), reads=[r_msc], writes=[r_msc])
    P.op("dve", lambda e: e.tensor_scalar_add(out=msc[:, :, 2:3], in0=msc[:, :, 2:3], scalar1=1.0), reads=[r_msc], writes=[r_msc])
    stg = [P.sbuf([128, T], F32) for _ in range(2)]; r_stg = [P.res() for _ in range(2)]
    for kc in range(16):
        b = kc % 2
        P.dma(["sp", "act"][kc % 2], stg[b][:], hT[kc * 128:(kc + 1) * 128, :], writes=[r_stg[b]])
        for (t0, t1, w) in segs:
            P.op("act", lambda e, kc=kc, b=b, t0=t0, t1=t1, w=w: e.activation(
                out=uT[:, kc, t0:t1], in_=stg[b][:, t0:t1], func=AF.Identity,
                scale=msc[:, kc, 2 * w:2 * w + 1], bias=msc[:, kc, 2 * w + 1:2 * w + 2]),
                reads=[r_stg[b], r_msc], writes=[r_uT])


def build_ret():
    nc = new_nc()
    T = NTOT
    hT = nc.dram_tensor("hT", [D, T], F32, kind="ExternalInput").ap()
    msc_d = nc.dram_tensor("msc", [128, 16, 4], F32, kind="ExternalInput").ap()
    wq = nc.dram_tensor("wq", [D, 2048], F32, kind="ExternalInput").ap()
    wk = nc.dram_tensor("wk", [D, 2048], F32, kind="ExternalInput").ap()
    wv = nc.dram_tensor("wv", [D, 4096], F32, kind="ExternalInput").ap()
    cos_d = nc.dram_tensor("cosT", [128, NLAT], F32, kind="ExternalInput").ap()
    sin_d = nc.dram_tensor("sinT", [128, NLAT], F32, kind="ExternalInput").ap()
    mask_d = nc.dram_tensor("mask01", [128, 128], F32, kind="ExternalInput").ap()
    expo_d = nc.dram_tensor("expo", [128, 128], F32, kind="ExternalInput").ap()
    idx_d = nc.dram_tensor("idxs", [128, 4], F32, kind="ExternalInput").ap()
    rdec_d = nc.dram_tensor("rdec", [1, 8], F32, kind="ExternalInput").ap()
    identb_d = nc.dram_tensor("identb", [128, 128], F32, kind="ExternalInput").ap()
    o_d = nc.dram_tensor("o", [T, 8, 512], F32, kind="ExternalOutput").ap()
    P = Prog(nc)
    uT = P.sbuf([128, 16, T], BF16); r_uT = P.res()
    emit_uT(P, hT, msc_d, uT, r_uT, T, [(0, NCTX, 0), (NCTX, T, 1)])
    cosT = P.sbuf([128, NLAT], F32); r_cos = P.res()
    sinT = P.sbuf([128, NLAT], F32); r_sin = P.res()
    mask01 = P.sbuf([128, 128], F32); expo = P.sbuf([128, 128], F32); idxs = P.sbuf([128, 4], F32)
    rdec = P.sbuf([128, 8], F32); identf = P.sbuf([128, 128], F32); identb = P.sbuf([128, 128], BF16)
    r_c = P.res()
    P.dma("sp", cosT[:], cos_d, writes=[r_cos])
    P.dma("sp", sinT[:], sin_d, writes=[r_sin])
    P.dma("sp", mask01[:], mask_d, writes=[r_c])
    r_c2 = P.res(); r_c3 = P.res(); r_c4 = P.res(); r_c5 = P.res()
    P.dma("sp", expo[:], expo_d, writes=[r_c2])
    P.dma("sp", idxs[:], idx_d, writes=[r_c3])
    P.dma("sp", rdec[:], rdec_d.partition_broadcast(128), writes=[r_c4])
    P.dma("sp", identf[:], identb_d, writes=[r_c5])
    r_idb = P.res()
    P.op("dve", lambda e: e.tensor_copy(out=identb[:], in_=identf[:]), reads=[r_c5], writes=[r_idb])
    lg = P.sbuf([128, 8], F32); r_lg = P.res()
    P.op("act", lambda e: e.activation(out=lg[:], in_=rdec[:], func=AF.Exp), reads=[r_c4], writes=[r_lg])
    P.op("dve", lambda e: e.tensor_scalar_mul(out=lg[:], in0=lg[:], scalar1=-1.0), reads=[r_lg], writes=[r_lg])
    maskT = P.sbuf([128, 8, 128], F32); r_mask = P.res()
    decs = P.sbuf([128, 8, 4], F32); r_decs = P.res()
    for h in range(8):
        P.op("act", lambda e, h=h: e.activation(out=maskT[:, h, :], in_=expo[:], func=AF.Exp, scale=lg[:, h:h + 1]),
             reads=[r_c2, r_lg], writes=[r_mask])
        P.op("dve", lambda e, h=h: e.scalar_tensor_tensor(out=maskT[:, h, :], in0=maskT[:, h, :], scalar=1.0 / 16.0, in1=mask01[:],
                                                          op0=ALU.mult, op1=ALU.mult), reads=[r_mask, r_c], writes=[r_mask])
        P.op("act", lambda e, h=h: e.activation(out=decs[:, h, 0:3], in_=idxs[:, 0:3], func=AF.Exp, scale=lg[:, h:h + 1]),
             reads=[r_c3, r_lg], writes=[r_decs])
        P.op("dve", lambda e, h=h: e.tensor_scalar_mul(out=decs[:, h, 0:1], in0=decs[:, h, 0:1], scalar1=1.0 / 16.0),
             reads=[r_decs], writes=[r_decs])
    qT = P.sbuf([128, 2, T], BF16); r_qT = P.res()
    kT = P.sbuf([128, 2, T], BF16); r_kT = P.res()
    vt = P.sbuf([128, 18, 512], BF16); r_v = P.res()
    wq_t = [P.sbuf([128, 16, 256], BF16)] * 2; r_wq = [P.res()] * 2
    wk_t = [P.sbuf([128, 16, 256], BF16)] * 2; r_wk = [P.res()] * 2
    wv_t = [P.sbuf([128, 16, 512], BF16)] * 2; r_wv = [P.res()] * 2
    ps_a = [P.psum([128, 512], F32) for _ in range(2)]; r_psa = [P.res() for _ in range(2)]
    ps_S = P.psum([128, 512], F32); r_psS = P.res()
    ps_kt = P.psum([128, 1024], BF16); r_pskt = P.res()
    ps_o = P.psum([128, 512], F32); r_pso = P.res()
    ps_i = P.psum([128, 512], F32); r_psi = P.res()
    ps_s = [P.psum([128, 512], F32) for _ in range(2)]; r_pss = [P.res() for _ in range(2)]
    tmp = [P.sbuf([128, 512], F32) for _ in range(4)]; r_tmp = [P.res() for _ in range(4)]
    AT = P.sbuf([128, 128], BF16); r_AT = P.res()
    kd = P.sbuf([128, 256], BF16); r_kd = P.res()
    o_sb = [P.sbuf([128, 512], F32) for _ in range(2)]; r_osb = [P.res() for _ in range(2)]
    st = P.sbuf([128, 2, 512], F32); r_st = P.res()
    st_bf = P.sbuf([128, 2, 512], BF16); r_stbf = P.res()
    groups = [(0, 256, 0), (256, 768, 1), (768, 1280, 1), (1280, 1792, 1), (1792, 2304, 1)]

    def load_w(h):
        b = h % 2
        P.dma("pool", wq_t[b][:], wq[:, h * 256:(h + 1) * 256].rearrange("(kc p) n -> p kc n", p=128), writes=[r_wq[b]])
        P.dma("pool", wk_t[b][:], wk[:, h * 256:(h + 1) * 256].rearrange("(kc p) n -> p kc n", p=128), writes=[r_wk[b]])
        P.dma("pool", wv_t[b][:], wv[:, h * 512:(h + 1) * 512].rearrange("(kc p) n -> p kc n", p=128), writes=[r_wv[b]])

    for h in range(8):
        b = h % 2
        load_w(h)
        for (w_t, r_w, dst, r_dst) in ((wq_t[b], r_wq[b], qT, r_qT), (wk_t[b], r_wk[b], kT, r_kT)):
            for (t0, t1, lat) in groups:
                n = t1 - t0
                for dc in range(2):
                    for kc in range(16):
                        P.op("pe", lambda e, dc=dc, kc=kc, w_t=w_t, t0=t0, t1=t1, n=n: e.matmul(
                            ps_a[dc][:, :n], lhsT=w_t[:, kc, dc * 128:(dc + 1) * 128], rhs=uT[:, kc, t0:t1],
                            start=(kc == 0), stop=(kc == 15)), reads=[r_w, r_uT], writes=[r_psa[dc]], inc=(kc == 15))
                if not lat:
                    for dc in range(2):
                        P.op("act", lambda e, dc=dc, dst=dst, t0=t0, t1=t1, n=n: e.activation(out=dst[:, dc, t0:t1], in_=ps_a[dc][:, :n], func=AF.Copy),
                             reads=[r_psa[dc]], writes=[r_dst])
                else:
                    l0 = t0 - NCTX; l1 = t1 - NCTX
                    P.op("dve", lambda e, l0=l0, l1=l1: e.tensor_tensor(out=tmp[0][:], in0=ps_a[0][:], in1=cosT[:, l0:l1], op=ALU.mult),
                         reads=[r_psa[0], r_cos], writes=[r_tmp[0]])
                    P.op("dve", lambda e, l0=l0, l1=l1: e.tensor_tensor(out=tmp[1][:], in0=ps_a[1][:], in1=sinT[:, l0:l1], op=ALU.mult),
                         reads=[r_psa[1], r_sin], writes=[r_tmp[1]])
                    P.op("dve", lambda e, l0=l0, l1=l1: e.tensor_tensor(out=tmp[2][:], in0=ps_a[0][:], in1=sinT[:, l0:l1], op=ALU.mult),
                         reads=[r_psa[0], r_sin], writes=[r_tmp[2]])
                    P.op("dve", lambda e, l0=l0, l1=l1: e.tensor_tensor(out=tmp[3][:], in0=ps_a[1][:], in1=cosT[:, l0:l1], op=ALU.mult),
                         reads=[r_psa[1], r_cos], writes=[r_tmp[3]])
                    P.op("pool", lambda e, dst=dst, t0=t0, t1=t1: e.tensor_tensor(out=dst[:, 0, t0:t1], in0=tmp[0][:], in1=tmp[1][:], op=ALU.subtract),
                         reads=[r_tmp[0], r_tmp[1]], writes=[r_dst])
                    P.op("pool", lambda e, dst=dst, t0=t0, t1=t1: e.tensor_tensor(out=dst[:, 1, t0:t1], in0=tmp[2][:], in1=tmp[3][:], op=ALU.add),
                         reads=[r_tmp[2], r_tmp[3]], writes=[r_dst])
        for c in range(18):
            pb = c % 2
            for kc in range(16):
                P.op("pe", lambda e, c=c, kc=kc, pb=pb, b=b: e.matmul(ps_a[pb][:], lhsT=uT[:, kc, c * 128:(c + 1) * 128], rhs=wv_t[b][:, kc, :],
                                                                 start=(kc == 0), stop=(kc == 15)), reads=[r_wv[b], r_uT], writes=[r_psa[pb]], inc=(kc == 15))
            P.op("act", lambda e, c=c, pb=pb: e.activation(out=vt[:, c, :], in_=ps_a[pb][:], func=AF.Copy), reads=[r_psa[pb]], writes=[r_v])
        for c in range(18):
            cs = slice(c * 128, (c + 1) * 128)
            ob = c % 2
            for dc in range(2):
                P.op("pe", lambda e, dc=dc, cs=cs: e.matmul(ps_S[:, :128], lhsT=kT[:, dc, cs], rhs=qT[:, dc, cs], start=(dc == 0), stop=(dc == 1)),
                     reads=[r_kT, r_qT], writes=[r_psS], inc=(dc == 1))
            P.op("dve", lambda e, h=h: e.tensor_tensor(out=AT[:], in0=ps_S[:, :128], in1=maskT[:, h, :], op=ALU.mult),
                 reads=[r_psS, r_mask], writes=[r_AT])
            P.op("pe", lambda e, c=c: e.matmul(ps_o[:], lhsT=AT[:], rhs=vt[:, c, :], start=True, stop=True), reads=[r_AT, r_v], writes=[r_pso])
            P.op("act", lambda e, ob=ob: e.activation(out=o_sb[ob][:], in_=ps_o[:], func=AF.Copy), reads=[r_pso], writes=[r_osb[ob]])
            if c > 0:
                for dc in range(2):
                    P.op("pe", lambda e, dc=dc, cs=cs: e.matmul(ps_i[:], lhsT=qT[:, dc, cs], rhs=st_bf[:, dc, :], start=(dc == 0), stop=(dc == 1)),
                         reads=[r_qT, r_stbf], writes=[r_psi], inc=(dc == 1))
                P.op("dve", lambda e, ob=ob, h=h: e.scalar_tensor_tensor(out=o_sb[ob][:], in0=ps_i[:], scalar=decs[:, h, 0:1], in1=o_sb[ob][:],
                                                                         op0=ALU.mult, op1=ALU.add), reads=[r_psi, r_decs, r_osb[ob]], writes=[r_osb[ob]])
            P.dma("sp", o_d[c * 128:(c + 1) * 128, h, :], o_sb[ob][:], reads=[r_osb[ob]])
            if c < 17:
                for dc in range(2):
                    P.op("pe", lambda e, dc=dc, cs=cs: e.transpose(ps_kt[:, dc * 128:(dc + 1) * 128], kT[:, dc, cs], identb[:]),
                         reads=[r_kT, r_idb], writes=[r_pskt], inc=(dc == 1))
                P.op("act", lambda e, h=h: e.activation(out=kd[:], in_=ps_kt[:, :256], func=AF.Identity, scale=decs[:, h, 1:2]),
                     reads=[r_pskt, r_decs], writes=[r_kd])
                for dc in range(2):
                    P.op("pe", lambda e, dc=dc, c=c: e.matmul(ps_s[dc][:], lhsT=kd[:, dc * 128:(dc + 1) * 128], rhs=vt[:, c, :], start=True, stop=True),
                         reads=[r_kd, r_v], writes=[r_pss[dc]])
                    if c == 0:
                        P.op("dve", lambda e, dc=dc: e.tensor_copy(out=st[:, dc, :], in_=ps_s[dc][:]), reads=[r_pss[dc]], writes=[r_st])
                    else:
                        P.op("dve", lambda e, dc=dc, h=h: e.scalar_tensor_tensor(out=st[:, dc, :], in0=st[:, dc, :], scalar=decs[:, h, 2:3], in1=ps_s[dc][:],
                                                                                 op0=ALU.mult, op1=ALU.add), reads=[r_st, r_decs, r_pss[dc]], writes=[r_st])
                P.op("pool", lambda e: e.tensor_copy(out=st_bf[:], in_=st[:]), reads=[r_st], writes=[r_stbf])
    P.finish()
    return nc


def _seg_flip_np(t):
    return np.concatenate([t[:NCTX][::-1], t[NCTX:][::-1]], axis=0)


def _rope_tables():
    rows = NLAT // 64
    row = np.repeat(np.arange(rows, dtype=np.float32), 64)
    col = np.tile(np.arange(64, dtype=np.float32), rows)
    n_freq = 64
    inv_freq = (np.float32(10000.0) ** (-np.arange(n_freq, dtype=np.float32) / np.float32(n_freq))).astype(np.float32)
    ang = np.concatenate([row[:, None] * inv_freq, col[:, None] * inv_freq], axis=-1).astype(np.float32)
    return np.cos(ang).astype(np.float32), np.sin(ang).astype(np.float32)


def _msc(mod, layer, b, j_scale, j_shift):
    m = mod[layer].reshape(5, 6, D)
    cols = [m[4, j_scale], m[4, j_shift], m[b, j_scale], m[b, j_shift]]
    a = np.stack(cols, axis=-1)
    return np.ascontiguousarray(a.reshape(16, 128, 4).transpose(1, 0, 2))


def run_ada(inputs):
    nc = build_ada()
    cc = np.concatenate([inputs["c"], inputs["c_ctx"][None]], 0).astype(np.float32)
    in_maps = []
    for i in range(8):
        in_maps.append({"cc": cc, "adaw": np.ascontiguousarray(inputs["ada_w"][:, :, i * 1536:(i + 1) * 1536]),
                        "adab": np.ascontiguousarray(inputs["ada_b"][:, None, i * 1536:(i + 1) * 1536]),
                        "ident": np.eye(128, dtype=np.float32)})
    res = run_bass_kernel_spmd(nc, in_maps, core_ids=list(range(8)))
    return np.concatenate([r["mod"] for r in res.results], axis=2)


def run_ret(inputs, mod, hcat):
    nc = build_ret()
    cos, sin = _rope_tables()
    w = inputs["ret_w_in"][0]
    jj = np.arange(128)[:, None]; ii = np.arange(128)[None, :]
    expo = np.maximum(ii - jj, 0).astype(np.float32)
    idxs = np.stack([np.arange(128) + 1.0, 127.0 - np.arange(128), np.full(128, 128.0), np.zeros(128)], -1).astype(np.float32)
    in_maps = []
    for core in range(8):
        b, d = core // 2, core % 2
        h = hcat[b]
        if d:
            h = _seg_flip_np(h)
        cs, sn = (cos[::-1], sin[::-1]) if d else (cos, sin)
        mask01 = (ii > jj) if d else (ii >= jj)
        in_maps.append({
            "hT": np.ascontiguousarray(h.T), "msc": _msc(mod, 0, b, 1, 0),
            "wq": np.ascontiguousarray(w[:, 0:2048]), "wk": np.ascontiguousarray(w[:, 2048:4096]),
            "wv": np.ascontiguousarray(w[:, 4096:8192]),
            "cosT": np.ascontiguousarray(cs.T), "sinT": np.ascontiguousarray(sn.T),
            "mask01": mask01.astype(np.float32), "expo": expo, "idxs": idxs,
            "rdec": np.ascontiguousarray(inputs["ret_decay"][0, d][None, :]),
            "identb": np.eye(128, dtype=np.float32)})
    res = run_bass_kernel_spmd(nc, in_maps, core_ids=list(range(8)))
    out = np.zeros((4, 2, NTOT, 8, 512), np.float32)
    for core in range(8):
        b, d = core // 2, core % 2
        o = res.results[core]["o"]
        out[b, d] = _seg_flip_np(o) if d else o
    return out


def emit_rsqrt(P, out_ap, in_ap, scale, eps_ap, r_s):
    P.op("act", lambda e: e.activation(out=out_ap, in_=in_ap, func=AF.Sqrt, bias=eps_ap, scale=scale), reads=[r_s], writes=[r_s])
    P.op("dve", lambda e: e.reciprocal(out=out_ap, in_=out_ap), reads=[r_s], writes=[r_s])


def emit_ln(P, hh, r_hh, t, lnrow_d, gi, bi, scr):
    stats, mv, rstd, gbc, bbc, r_s, r_g, r_b, eps5 = scr
    for j in range(4):
        P.op("dve", lambda e, j=j: e.bn_stats(out=stats[:, j, :], in_=hh[:, t, j * 512:(j + 1) * 512]), reads=[r_hh[t]], writes=[r_s])
    P.op("dve", lambda e: e.bn_aggr(out=mv[:], in_=stats[:].rearrange("p a b -> p (a b)")), reads=[r_s], writes=[r_s])
    emit_rsqrt(P, rstd[:, 0:1], mv[:, 1:2], 1.0, eps5[:, 0:1], r_s)
    P.op("dve", lambda e: e.tensor_scalar(out=hh[:, t, :], in0=hh[:, t, :], scalar1=mv[:, 0:1], scalar2=rstd[:, 0:1], op0=ALU.subtract, op1=ALU.mult),
         reads=[r_s, r_hh[t]], writes=[r_hh[t]])
    for j in range(4):
        cs = slice(j * 512, (j + 1) * 512)
        P.dma("sp", gbc[:], lnrow_d[gi:gi + 1, cs].partition_broadcast(128), writes=[r_g])
        P.dma("sp", bbc[:], lnrow_d[bi:bi + 1, cs].partition_broadcast(128), writes=[r_b])
        P.op("pool", lambda e, cs=cs: e.tensor_tensor(out=hh[:, t, cs], in0=hh[:, t, cs], in1=gbc[:], op=ALU.mult), reads=[r_g, r_hh[t]], writes=[r_hh[t]])
        P.op("pool", lambda e, cs=cs: e.tensor_tensor(out=hh[:, t, cs], in0=hh[:, t, cs], in1=bbc[:], op=ALU.add), reads=[r_b, r_hh[t]], writes=[r_hh[t]])


def build_blk(layer):
    nc = new_nc()
    T = 1152 if layer == 0 else 1024
    nt = T // 128
    tgs = [(0, 384), (384, 768), (768, 1152)] if layer == 0 else [(0, 512), (512, 1024)]
    of_d = nc.dram_tensor("of", [T, 4096], F32, kind="ExternalInput").ap()
    ob_d = nc.dram_tensor("ob", [T, 4096], F32, kind="ExternalInput").ap()
    h_d = nc.dram_tensor("h", [T, D], F32, kind="ExternalInput").ap()
    hT_d = nc.dram_tensor("hT", [D, T], F32, kind="ExternalInput").ap()
    msc1_d = nc.dram_tensor("msc1", [128, 16, nt, 2], F32, kind="ExternalInput").ap()
    msc2_d = nc.dram_tensor("msc2", [128, 16, nt, 2], F32, kind="ExternalInput").ap()
    g1_d = nc.dram_tensor("g1", [nt, D], F32, kind="ExternalInput").ap()
    g2_d = nc.dram_tensor("g2", [nt, D], F32, kind="ExternalInput").ap()
    ln_d = nc.dram_tensor("lnrows", [4, D], F32, kind="ExternalInput").ap()
    wg_d = nc.dram_tensor("wg", [D, 4096], F32, kind="ExternalInput").ap()
    wo_d = nc.dram_tensor("wo", [4096, D], F32, kind="ExternalInput").ap()
    w1_d = nc.dram_tensor("w1", [D, 2 * DFF], F32, kind="ExternalInput").ap()
    w2_d = nc.dram_tensor("w2", [DFF, D], F32, kind="ExternalInput").ap()
    nw_d = nc.dram_tensor("nw", [1, 128], F32, kind="ExternalInput").ap()
    id_d = nc.dram_tensor("identf", [128, 128], F32, kind="ExternalInput").ap()
    out_d = nc.dram_tensor("out", [T, D], F32, kind="ExternalOutput").ap()
    P = Prog(nc)
    uT = P.sbuf([128, 16, T], BF16); r_uT = P.res()
    big = P.sbuf([128, 11, T], BF16); r_big = P.res()
    hh = P.sbuf([128, nt, D], F32); r_hh = [P.res() for _ in range(nt)]
    W = [P.sbuf([128, 16, 512], BF16) for _ in range(2)]; r_W = [P.res() for _ in range(2)]
    Wgu = [W[0][:, :, 0:256], W[0][:, :, 256:512]]; r_Wg = [P.res() for _ in range(2)]; r_Wu = [P.res() for _ in range(2)]
    msc1 = P.sbuf([128, 16, nt, 2], F32); r_m1 = P.res()
    msc2 = P.sbuf([128, 16, nt, 2], F32); r_m2 = P.res()
    identf = P.sbuf([128, 128], F32); r_idf = P.res()
    identb = P.sbuf([128, 128], BF16); r_idb = P.res()
    nw = P.sbuf([128, 128], F32); r_nw = P.res()
    P.dma("sp", msc1[:], msc1_d, writes=[r_m1])
    P.dma("sp", msc2[:], msc2_d, writes=[r_m2])
    P.dma("sp", identf[:], id_d, writes=[r_idf])
    P.dma("sp", nw[:], nw_d.partition_broadcast(128), writes=[r_nw])
    P.op("dve", lambda e: e.tensor_copy(out=identb[:], in_=identf[:]), reads=[r_idf], writes=[r_idb])
    P.op("dve", lambda e: e.tensor_scalar_add(out=msc1[:, :, :, 0:1], in0=msc1[:, :, :, 0:1], scalar1=1.0), reads=[r_m1], writes=[r_m1])
    P.op("dve", lambda e: e.tensor_scalar_add(out=msc2[:, :, :, 0:1], in0=msc2[:, :, :, 0:1], scalar1=1.0), reads=[r_m2], writes=[r_m2])
    for t in range(nt):
        P.dma(["sp", "act"][t % 2], hh[:, t, :], h_d[t * 128:(t + 1) * 128, :], writes=[r_hh[t]])
    stg = P.sbuf([128, T], F32); r_stg = P.res()
    for kc in range(16):
        P.dma("sp", stg[:], hT_d[kc * 128:(kc + 1) * 128, :], writes=[r_stg])
        for t in range(nt):
            P.op("act", lambda e, kc=kc, t=t: e.activation(out=uT[:, kc, t * 128:(t + 1) * 128], in_=stg[:, t * 128:(t + 1) * 128], func=AF.Identity,
                                                           scale=msc1[:, kc, t, 0:1], bias=msc1[:, kc, t, 1:2]), reads=[r_stg, r_m1], writes=[r_uT])
    ps_m = [P.psum([128, 512], F32) for _ in range(2)]; r_psm = [P.res() for _ in range(2)]
    ps_u = [P.psum([128, 512], F32) for _ in range(2)]; r_psu = [P.res() for _ in range(2)]
    ps_tr = P.psum([128, 1024], BF16); r_pstr = P.res()
    ps_tf = [P.psum([128, 512], F32) for _ in range(2)]; r_pstf = [P.res() for _ in range(2)]
    class Rot:
        def __init__(self, n, dt):
            self.items = [(P.sbuf([128, 512], dt), P.res()) for _ in range(n)]
            self.i = 0

        def next(self):
            it = self.items[self.i % len(self.items)]
            self.i += 1
            return it
    rot_sg = Rot(2, F32); rot_of = Rot(2, F32); rot_ob = Rot(2, F32); rot_xn = Rot(2, F32); rot_og = Rot(2, BF16)
    rot_gbc = Rot(3, F32); rot_tmp = Rot(2, F32)
    stats = P.sbuf([128, 4, 6], F32); mv = P.sbuf([128, 2], F32); rstd = P.sbuf([128, 4], F32); r_s = P.res()
    lg_t = P.sbuf([128, 512], F32); lb_t = P.sbuf([128, 512], F32); r_lg = P.res(); r_lb = P.res()
    eps5 = P.sbuf([128, 2], F32)
    P.op("dve", lambda e: e.memset(eps5[:, 0:1], LN_EPS), writes=[r_s])
    P.op("dve", lambda e: e.memset(eps5[:, 1:2], 1e-6), writes=[r_s])
    ln_scr = (stats, mv, rstd, lg_t, lb_t, r_s, r_lg, r_lb, eps5)
    mmi = [0]

    def accum_h(t, cs, ps, r_ps, g_d, first):
        gbc, r_gbc = rot_gbc.next(); tmp, r_tmp = rot_tmp.next()
        P.dma("sp", gbc[:], g_d[t:t + 1, cs].partition_broadcast(128), writes=[r_gbc])
        P.op("dve", lambda e: e.tensor_tensor(out=tmp[:], in0=ps[:], in1=gbc[:], op=ALU.mult), reads=[r_ps, r_gbc], writes=[r_tmp])
        P.op("dve", lambda e: e.scalar_tensor_tensor(out=hh[:, t, cs], in0=hh[:, t, cs], scalar=(ALPHA if first else 1.0), in1=tmp[:],
                                                     op0=ALU.mult, op1=ALU.add), reads=[r_tmp, r_hh[t]], writes=[r_hh[t]])

    for kq in range(4):
        for cgi in range(2):
            cg = kq * 2 + cgi
            cols = slice(cg * 512, (cg + 1) * 512)
            wb = mmi[0] % 2; mmi[0] += 1
            P.dma("pool", W[wb][:], wg_d[:, cols].rearrange("(kc p) n -> p kc n", p=128), writes=[r_W[wb]])
            for t in range(nt):
                pb = t % 2
                ts_ = slice(t * 128, (t + 1) * 128)
                for kc in range(16):
                    P.op("pe", lambda e, kc=kc, pb=pb, wb=wb, ts_=ts_: e.matmul(ps_m[pb][:], lhsT=uT[:, kc, ts_], rhs=W[wb][:, kc, :],
                                                                          start=(kc == 0), stop=(kc == 15)), reads=[r_uT, r_W[wb]], writes=[r_psm[pb]], inc=(kc == 15))
                sg, r_sg = rot_sg.next(); oft, r_of = rot_of.next(); obt, r_ob = rot_ob.next(); xn, r_xn = rot_xn.next(); og, r_og = rot_og.next()
                P.op("act", lambda e, pb=pb, sg=sg, oft=oft, obt=obt, xn=xn, og=og: e.activation(out=sg[:], in_=ps_m[pb][:], func=AF.Silu), reads=[r_psm[pb]], writes=[r_sg])
                P.dma("sp", oft[:], of_d[ts_, cols], writes=[r_of])
                P.dma("act", obt[:], ob_d[ts_, cols], writes=[r_ob])
                P.op("pool", lambda e, sg=sg, oft=oft, obt=obt, xn=xn, og=og: e.tensor_tensor(out=oft[:], in0=oft[:], in1=obt[:], op=ALU.add), reads=[r_of, r_ob], writes=[r_of])
                if layer == 0:
                    P.op("dve", lambda e, sg=sg, oft=oft, obt=obt, xn=xn, og=og: e.bn_stats(out=stats[:, 0, :], in_=oft[:]), reads=[r_of], writes=[r_s])
                    P.op("dve", lambda e, sg=sg, oft=oft, obt=obt, xn=xn, og=og: e.bn_aggr(out=mv[:], in_=stats[:, 0, :]), reads=[r_s], writes=[r_s])
                    emit_rsqrt(P, rstd[:, 0:1], mv[:, 1:2], 1.0, eps5[:, 0:1], r_s)
                    P.op("dve", lambda e, sg=sg, oft=oft, obt=obt, xn=xn, og=og: e.tensor_scalar(out=xn[:], in0=oft[:], scalar1=mv[:, 0:1], scalar2=rstd[:, 0:1], op0=ALU.subtract, op1=ALU.mult),
                         reads=[r_s, r_of], writes=[r_xn])
                else:
                    P.op("pool", lambda e, sg=sg, oft=oft, obt=obt, xn=xn, og=og: e.tensor_tensor(out=xn[:], in0=oft[:], in1=oft[:], op=ALU.mult), reads=[r_of], writes=[r_xn])
                    P.op("dve", lambda e, sg=sg, oft=oft, obt=obt, xn=xn, og=og: e.tensor_reduce(out=rstd[:], in_=xn[:].rearrange("p (a b) -> p a b", a=4), axis=AX.X, op=ALU.add),
                         reads=[r_xn], writes=[r_s])
                    emit_rsqrt(P, rstd[:], rstd[:], 1.0 / 128.0, eps5[:, 1:2], r_s)
                    for j in range(4):
                        P.op("dve", lambda e, j=j, sg=sg, oft=oft, obt=obt, xn=xn, og=og: e.scalar_tensor_tensor(out=xn[:, j * 128:(j + 1) * 128], in0=oft[:, j * 128:(j + 1) * 128],
                                                                          scalar=rstd[:, j:j + 1], in1=nw[:], op0=ALU.mult, op1=ALU.mult),
                             reads=[r_s, r_of, r_nw, r_xn], writes=[r_xn])
                P.op("dve", lambda e, sg=sg, oft=oft, obt=obt, xn=xn, og=og: e.tensor_tensor(out=og[:], in0=xn[:], in1=sg[:], op=ALU.mult), reads=[r_xn, r_sg], writes=[r_og])
                for j in range(4):
                    P.op("pe", lambda e, j=j, sg=sg, oft=oft, obt=obt, xn=xn, og=og: e.transpose(ps_tr[:, j * 128:(j + 1) * 128], og[:, j * 128:(j + 1) * 128], identb[:]),
                         reads=[r_og, r_idb], writes=[r_pstr], inc=(j == 3))
                P.op("act", lambda e, cgi=cgi, ts_=ts_, sg=sg, oft=oft, obt=obt, xn=xn, og=og: e.activation(out=big[:, cgi * 4:(cgi + 1) * 4, ts_],
                                                                    in_=ps_tr[:, :512].rearrange("p (a b) -> p a b", a=4), func=AF.Copy),
                     reads=[r_pstr], writes=[r_big])
        for cg2 in range(4):
            cs = slice(cg2 * 512, (cg2 + 1) * 512)
            wb = mmi[0] % 2; mmi[0] += 1
            P.dma("pool", W[wb][:, 0:8, :], wo_d[kq * 1024:(kq + 1) * 1024, cs].rearrange("(kc p) n -> p kc n", p=128), writes=[r_W[wb]])
            for t in range(nt):
                pb = t % 2
                ts_ = slice(t * 128, (t + 1) * 128)
                for kc in range(8):
                    P.op("pe", lambda e, kc=kc, pb=pb, wb=wb, ts_=ts_: e.matmul(ps_u[pb][:], lhsT=big[:, kc, ts_], rhs=W[wb][:, kc, :],
                                                                          start=(kc == 0), stop=(kc == 7)), reads=[r_big, r_W[wb]], writes=[r_psu[pb]], inc=(kc == 7))
                accum_h(t, cs, ps_u[pb], r_psu[pb], g1_d, kq == 0)
    for t in range(nt):
        emit_ln(P, hh, r_hh, t, ln_d, 0, 1, ln_scr)
        ts_ = slice(t * 128, (t + 1) * 128)
        for kg in range(4):
            pb = kg % 2
            for j in range(4):
                kc = kg * 4 + j
                P.op("pe", lambda e, kc=kc, j=j, pb=pb, t=t: e.transpose(ps_tf[pb][:, j * 128:(j + 1) * 128], hh[:, t, kc * 128:(kc + 1) * 128], identf[:]),
                     reads=[r_hh[t], r_idf], writes=[r_pstf[pb]], inc=(j == 3))
            for j in range(4):
                kc = kg * 4 + j
                P.op("act", lambda e, kc=kc, j=j, pb=pb, t=t, ts_=ts_: e.activation(out=uT[:, kc, ts_], in_=ps_tf[pb][:, j * 128:(j + 1) * 128], func=AF.Identity,
                                                                               scale=msc2[:, kc, t, 0:1], bias=msc2[:, kc, t, 1:2]),
                     reads=[r_pstf[pb], r_m2], writes=[r_uT])
    P.op("dve", lambda e: e.memset(mv[:], 0.0), reads=[r_W[0], r_W[1]], writes=[r_Wg[0], r_Wg[1], r_Wu[0], r_Wu[1], r_s])
    for fq in range(4):
        for fi in range(11):
            f = fq * 11 + fi
            wb = f % 2
            P.dma("pool", Wgu[wb][:, :, 0:128], w1_d[:, f * 128:(f + 1) * 128].rearrange("(kc p) n -> p kc n", p=128), writes=[r_Wg[wb]])
            P.dma("pool", Wgu[wb][:, :, 128:256], w1_d[:, DFF + f * 128:DFF + (f + 1) * 128].rearrange("(kc p) n -> p kc n", p=128), writes=[r_Wu[wb]])
            for gi_, (t0, t1) in enumerate(tgs):
                n = t1 - t0
                pb = gi_ % 2
                for kc in range(16):
                    P.op("pe", lambda e, kc=kc, pb=pb, wb=wb, t0=t0, t1=t1, n=n: e.matmul(ps_m[pb][:, :n], lhsT=Wgu[wb][:, kc, 0:128], rhs=uT[:, kc, t0:t1],
                                                                                    start=(kc == 0), stop=(kc == 15)), reads=[r_Wg[wb], r_uT], writes=[r_psm[pb]], inc=(kc == 15))
                for kc in range(16):
                    P.op("pe", lambda e, kc=kc, pb=pb, wb=wb, t0=t0, t1=t1, n=n: e.matmul(ps_u[pb][:, :n], lhsT=Wgu[wb][:, kc, 128:256], rhs=uT[:, kc, t0:t1],
                                                                                    start=(kc == 0), stop=(kc == 15)), reads=[r_Wu[wb], r_uT], writes=[r_psu[pb]], inc=(kc == 15))
                sg, r_sg = rot_sg.next()
                P.op("act", lambda e, pb=pb, n=n, sg=sg: e.activation(out=sg[:, :n], in_=ps_m[pb][:, :n], func=AF.Silu), reads=[r_psm[pb]], writes=[r_sg])
                P.op("dve", lambda e, pb=pb, n=n, fi=fi, t0=t0, t1=t1, sg=sg: e.tensor_tensor(out=big[:, fi, t0:t1], in0=ps_u[pb][:, :n], in1=sg[:, :n], op=ALU.mult),
                     reads=[r_psu[pb], r_sg], writes=[r_big])
        for cg2 in range(4):
            cs = slice(cg2 * 512, (cg2 + 1) * 512)
            wb = 1
            P.dma("pool", W[wb][:, 0:11, :], w2_d[fq * 1408:(fq + 1) * 1408, cs].rearrange("(kc p) n -> p kc n", p=128), writes=[r_W[wb]])
            for t in range(nt):
                pb = t % 2
                ts_ = slice(t * 128, (t + 1) * 128)
                for kc in range(11):
                    P.op("pe", lambda e, kc=kc, pb=pb, wb=wb, ts_=ts_: e.matmul(ps_tf[pb][:], lhsT=big[:, kc, ts_], rhs=W[wb][:, kc, :],
                                                                          start=(kc == 0), stop=(kc == 10)), reads=[r_big, r_W[wb]], writes=[r_pstf[pb]], inc=(kc == 10))
                accum_h(t, cs, ps_tf[pb], r_pstf[pb], g2_d, fq == 0)
    for t in range(nt):
        emit_ln(P, hh, r_hh, t, ln_d, 2, 3, ln_scr)
        P.dma("sp", out_d[t * 128:(t + 1) * 128, :], hh[:, t, :], reads=[r_hh[t]])
    P.finish()
    return nc


def run_blk(inputs, mod, layer, o, hcat):
    nc = build_blk(layer)
    T = 1152 if layer == 0 else 1024
    nt = T // 128
    m = mod[layer].reshape(5, 6, D)
    if layer == 0:
        wg = np.ascontiguousarray(inputs["ret_w_in"][0][:, 8192:12288]); wo = inputs["ret_w_out"][0]
        nw = np.ones((1, 128), np.float32)
    else:
        wg = np.ascontiguousarray(inputs["gdn_w_in"][0][:, 8192:12288]); wo = inputs["gdn_w_out"][0]
        nw = np.ascontiguousarray(inputs["gdn_norm"][0][None, :])
    lnrows = np.stack([inputs["ln_g"][layer, 0], inputs["ln_b"][layer, 0], inputs["ln_g"][layer, 1], inputs["ln_b"][layer, 1]], 0)
    in_maps = []
    for core in range(8):
        b, s = core // 2, core % 2
        t0 = s * T + (0 if layer == 0 else NCTX)
        rows = [(4 if (t0 + t * 128) < NCTX else b) for t in range(nt)]

        def msc(js, jh):
            a = np.stack([np.stack([m[r, js], m[r, jh]], -1) for r in rows], 1)
            return np.ascontiguousarray(a.reshape(16, 128, nt, 2).transpose(1, 0, 2, 3))
        h = hcat[b, t0:t0 + T]
        in_maps.append({
            "of": np.ascontiguousarray(o[b, 0, t0:t0 + T].reshape(T, 4096)), "ob": np.ascontiguousarray(o[b, 1, t0:t0 + T].reshape(T, 4096)),
            "h": np.ascontiguousarray(h), "hT": np.ascontiguousarray(h.T),
            "msc1": msc(1, 0), "msc2": msc(4, 3),
            "g1": np.ascontiguousarray(np.stack([m[r, 2] for r in rows], 0)), "g2": np.ascontiguousarray(np.stack([m[r, 5] for r in rows], 0)),
            "lnrows": np.ascontiguousarray(lnrows), "wg": wg, "wo": wo, "w1": inputs["ffn_w_in"][layer], "w2": inputs["ffn_w_out"][layer],
            "nw": nw, "identf": np.eye(128, dtype=np.float32)})
    res = run_bass_kernel_spmd(nc, in_maps, core_ids=list(range(8)))
    if layer == 0:
        out = np.zeros((4, NTOT, D), np.float32)
        for core in range(8):
            b, s = core // 2, core % 2
            out[b, s * T:(s + 1) * T] = res.results[core]["out"]
    else:
        out = np.zeros((4, NLAT, D), np.float32)
        for core in range(8):
            b, s = core // 2, core % 2
            out[b, s * T:(s + 1) * T] = res.results[core]["out"]
    return out


def build_gdn(n_heads=16):
    nc = new_nc()
    T = NTOT
    NCH = T // 128
    hT = nc.dram_tensor("hT", [D, T], F32, kind="ExternalInput").ap()
    msc_d = nc.dram_tensor("msc", [128, 16, 4], F32, kind="ExternalInput").ap()
    wqkv = nc.dram_tensor("wqkv", [D, 8192], F32, kind="ExternalInput").ap()
    wba = nc.dram_tensor("wba", [D, 64], F32, kind="ExternalInput").ap()
    convw_d = nc.dram_tensor("convw", [128, 64, 5], F32, kind="ExternalInput").ap()
    alog_d = nc.dram_tensor("alog", [1, 32], F32, kind="ExternalInput").ap()
    dtb_d = nc.dram_tensor("dtb", [1, 32], F32, kind="ExternalInput").ap()
    id_d = nc.dram_tensor("identf", [128, 128], F32, kind="ExternalInput").ap()
    U_d = nc.dram_tensor("U", [128, 128], F32, kind="ExternalInput").ap()
    msl_d = nc.dram_tensor("msl", [128, 128], F32, kind="ExternalInput").ap()
    o_d = nc.dram_tensor("o", [T, 32, 128], F32, kind="ExternalOutput").ap()
    P = Prog(nc)
    uT = P.sbuf([128, 16, T], BF16); r_uT = P.res()
    emit_uT(P, hT, msc_d, uT, r_uT, T, [(0, NCTX, 0), (NCTX, T, 1)])
    identf = P.sbuf([128, 128], F32); r_idf = P.res()
    identb = P.sbuf([128, 128], BF16); r_idb = P.res()
    U = P.sbuf([128, 128], F32); r_U = P.res()
    msl = P.sbuf([128, 128], F32); r_msl = P.res()
    onesf = P.sbuf([128, 128], F32); r_ones = P.res()
    cst = P.sbuf([128, 2], F32); r_cst = P.res()
    convw = P.sbuf([128, 64, 5], F32); r_cw = P.res()
    alog = P.sbuf([128, 32], F32); r_al = P.res()
    dtb = P.sbuf([128, 32], F32); r_dtb = P.res()
    P.dma("sp", identf[:], id_d, writes=[r_idf])
    P.dma("sp", U[:], U_d, writes=[r_U])
    P.dma("sp", msl[:], msl_d, writes=[r_msl])
    P.dma("sp", convw[:], convw_d, writes=[r_cw])
    P.dma("sp", alog[:], alog_d.partition_broadcast(128), writes=[r_al])
    P.dma("sp", dtb[:], dtb_d.partition_broadcast(128), writes=[r_dtb])
    P.op("dve", lambda e: e.tensor_copy(out=identb[:], in_=identf[:]), reads=[r_idf], writes=[r_idb])
    P.op("dve", lambda e: e.memset(onesf[:], 1.0), writes=[r_ones])
    P.op("dve", lambda e: e.memset(cst[:, 0:1], 1e-6), writes=[r_cst])
    P.op("dve", lambda e: e.memset(cst[:, 1:2], 1.0), writes=[r_cst])
    P.op("act", lambda e: e.activation(out=alog[:], in_=alog[:], func=AF.Exp), reads=[r_al], writes=[r_al])
    bank = [P.psum([128, 512], F32) for _ in range(8)]; r_bk = [P.res(excl=True) for _ in range(8)]
    ps_p = [bank[6], bank[7]]; r_pp = [r_bk[6], r_bk[7]]
    bank_bf = bank[3]
    beta = P.sbuf([128, NCH, 32], F32); r_beta = P.res()
    negb = P.sbuf([128, NCH, 32], F32); r_negb = P.res()
    la = P.sbuf([128, NCH, 32], F32); r_la = P.res()
    wba_t = P.sbuf([128, 16, 64], BF16); r_wba = P.res()
    t32 = P.sbuf([128, 32], F32); r_t32 = P.res()
    P.dma("pool", wba_t[:], wba.rearrange("(kc p) n -> p kc n", p=128), writes=[r_wba])
    for c in range(NCH):
        pb = c % 2
        for kc in range(16):
            P.op("pe", lambda e, c=c, kc=kc, pb=pb: e.matmul(ps_p[pb][:, :64], lhsT=uT[:, kc, c * 128:(c + 1) * 128], rhs=wba_t[:, kc, :],
                                                       start=(kc == 0), stop=(kc == 15)), reads=[r_uT, r_wba], writes=[r_pp[pb]], inc=(kc == 15))
        P.op("act", lambda e, c=c, pb=pb: e.activation(out=beta[:, c, :], in_=ps_p[pb][:, 0:32], func=AF.Sigmoid), reads=[r_pp[pb]], writes=[r_beta])
        P.op("dve", lambda e, pb=pb: e.tensor_tensor(out=t32[:], in0=ps_p[pb][:, 32:64], in1=dtb[:], op=ALU.add), reads=[r_pp[pb], r_dtb], writes=[r_t32])
        P.op("act", lambda e: e.activation(out=t32[:], in_=t32[:], func=AF.Exp), reads=[r_t32], writes=[r_t32])
        P.op("act", lambda e: e.activation(out=t32[:], in_=t32[:], func=AF.Ln, bias=cst[:, 1:2], scale=1.0), reads=[r_t32, r_cst], writes=[r_t32])
        P.op("dve", lambda e, c=c: e.scalar_tensor_tensor(out=la[:, c, :], in0=t32[:], scalar=-1.0, in1=alog[:], op0=ALU.mult, op1=ALU.mult),
             reads=[r_t32, r_al], writes=[r_la])
    P.op("dve", lambda e: e.tensor_scalar_mul(out=negb[:], in0=beta[:], scalar1=-1.0), reads=[r_beta], writes=[r_negb])
    LP = 2 + NCTX + 2 + 2 + NLAT + 2
    xp = P.sbuf([128, LP], F32); r_xp = P.res()
    acc = P.sbuf([128, T], F32); r_acc = P.res()
    ys = P.sbuf([128, T], F32); r_ys = P.res()
    sq = P.sbuf([128, 512], F32); r_sq = P.res()
    rn = P.sbuf([128, 512], F32); r_rn = P.res()
    HS = []
    for _ in range(2):
        HS.append((P.sbuf([128, T], BF16), P.res(), P.sbuf([128, T], BF16), P.res(), P.sbuf([128, NCH, 128], BF16), P.res(),
                   [P.sbuf([128, NCH, 128], BF16) for _ in range(2)], [P.res() for _ in range(2)]))
    Wp = [P.sbuf([128, 16, 128], BF16) for _ in range(2)]; r_Wp = [P.res() for _ in range(2)]
    P.op("pool", lambda e: e.memset(xp[:], 0.0), writes=[r_xp])
    ps_trb = P.psum([128, 1024], BF16) if False else None
    class NS:
        pass

    def f32t():
        return P.sbuf([128, 128], F32), P.res()

    def bf16t():
        return P.sbuf([128, 128], BF16), P.res()
    ST = []
    for s_ in range(2):
        S = NS()
        S.LAU, S.r_LAU = f32t(); S.Zabs, S.r_Z = f32t(); S.decA, S.r_decA = f32t(); S.egR, S.r_egR = f32t()
        S.t1, S.r_t1 = f32t(); S.t2, S.r_t2 = f32t()
        S.Xs = [f32t() for _ in range(2)]; S.Ys = [f32t() for _ in range(2)]; S.Ts = [f32t() for _ in range(2)]
        S.Ttb, S.r_Ttb = bf16t()
        S.cols = P.sbuf([128, 8], F32); S.r_cols = P.res()
        S.qdT, S.r_qdT = bf16t(); S.kbg, S.r_kbg = bf16t(); S.kd, S.r_kd = bf16t(); S.nwT, S.r_nwT = bf16t()
        S.u_sb, S.r_u = bf16t(); S.QKT, S.r_QKT = bf16t()
        S.o_sb = [P.sbuf([128, 128], F32) for _ in range(2)]; S.r_osb = [P.res() for _ in range(2)]
        S.m, S.r_m = f32t(); S.m_bf, S.r_mbf = bf16t()
        S.b = bank[3 * s_:3 * s_ + 3]; S.rb = r_bk[3 * s_:3 * s_ + 3]
        ST.append(S)
    groups = [(0, 256), (256, 768), (768, 1280), (1280, 1792), (1792, 2304)]
    segs = [(2, 0, NCTX), (2 + NCTX + 4, NCTX, NLAT)]
    B = ps_p; RB = r_pp
    wi = [0]

    def project_conv_silu(g):
        wb = wi[0] % 2; wi[0] += 1
        P.dma("pool", Wp[wb][:], wqkv[:, g * 128:(g + 1) * 128].rearrange("(kc p) n -> p kc n", p=128), writes=[r_Wp[wb]])
        for gi, (t0, t1) in enumerate(groups):
            n = t1 - t0; pb = gi % 2
            for kc in range(16):
                P.op("pe", lambda e, kc=kc, pb=pb, wb=wb, t0=t0, t1=t1, n=n: e.matmul(ps_p[pb][:, :n], lhsT=Wp[wb][:, kc, :], rhs=uT[:, kc, t0:t1],
                                                                                start=(kc == 0), stop=(kc == 15)), reads=[r_Wp[wb], r_uT], writes=[r_pp[pb]], inc=(kc == 15))
            yield
            off = 2 + t0 if t0 < NCTX else 2 + NCTX + 4 + (t0 - NCTX)
            P.op("act", lambda e, pb=pb, n=n, off=off: e.activation(out=xp[:, off:off + n], in_=ps_p[pb][:, :n], func=AF.Copy), reads=[r_pp[pb]], writes=[r_xp])
            yield
        for (xo, to, L) in segs:
            P.op("dve", lambda e, xo=xo, to=to, L=L, g=g: e.tensor_scalar_mul(out=acc[:, to:to + L], in0=xp[:, xo - 2:xo - 2 + L], scalar1=convw[:, g, 0:1]),
                 reads=[r_xp, r_cw], writes=[r_acc])
            yield
            for j in range(1, 5):
                P.op("dve", lambda e, xo=xo, to=to, L=L, g=g, j=j: e.scalar_tensor_tensor(out=acc[:, to:to + L], in0=xp[:, xo - 2 + j:xo - 2 + j + L], scalar=convw[:, g, j:j + 1],
                                                                                        in1=acc[:, to:to + L], op0=ALU.mult, op1=ALU.add), reads=[r_xp, r_cw, r_acc], writes=[r_acc])
                yield
        P.op("act", lambda e: e.activation(out=ys[:], in_=acc[:], func=AF.Silu), reads=[r_acc], writes=[r_ys])
        yield

    def l2norm_to(dst, r_dst):
        for gi, (t0, t1) in enumerate(groups):
            n = t1 - t0; pb = gi % 2
            P.op("pool", lambda e, t0=t0, t1=t1, n=n: e.tensor_tensor(out=sq[:, :n], in0=ys[:, t0:t1], in1=ys[:, t0:t1], op=ALU.mult), reads=[r_ys], writes=[r_sq])
            P.op("pe", lambda e, pb=pb, n=n: e.matmul(ps_p[pb][:, :n], lhsT=onesf[:], rhs=sq[:, :n], start=True, stop=True), reads=[r_ones, r_sq], writes=[r_pp[pb]])
            P.op("act", lambda e, pb=pb, n=n: e.activation(out=rn[:, :n], in_=ps_p[pb][:, :n], func=AF.Sqrt, bias=cst[:, 0:1], scale=1.0), reads=[r_pp[pb], r_cst], writes=[r_rn])
            P.op("dve", lambda e, n=n: e.reciprocal(out=rn[:, :n], in_=rn[:, :n]), reads=[r_rn], writes=[r_rn])
            P.op("dve", lambda e, t0=t0, t1=t1, n=n: e.tensor_tensor(out=dst[:, t0:t1], in0=ys[:, t0:t1], in1=rn[:, :n], op=ALU.mult), reads=[r_ys, r_rn], writes=[r_dst])
            yield

    def prep(hq):
        qT, r_qT, kT, r_kT, kTok, r_kTok, vb, r_vb = HS[hq % 2]
        yield from project_conv_silu(hq); yield from l2norm_to(qT, r_qT)
        yield from project_conv_silu(16 + hq); yield from l2norm_to(kT, r_kT)
        for c in range(NCH):
            pb = c % 2
            P.op("pe", lambda e, c=c, pb=pb: e.matmul(B[pb][:, :128], lhsT=kT[:, c * 128:(c + 1) * 128], rhs=identb[:], start=True, stop=True), reads=[r_kT, r_idb], writes=[RB[pb]])
            P.op("act", lambda e, c=c, pb=pb: e.activation(out=kTok[:, c, :], in_=B[pb][:, :128], func=AF.Copy), reads=[RB[pb]], writes=[r_kTok])
            yield
        for e_ in range(2):
            hv = 2 * hq + e_
            yield from project_conv_silu(32 + hv)
            for c in range(NCH):
                pb = c % 2
                P.op("pe", lambda e, c=c, pb=pb: e.matmul(B[pb][:, :128], lhsT=ys[:, c * 128:(c + 1) * 128], rhs=identf[:], start=True, stop=True), reads=[r_ys, r_idf], writes=[RB[pb]])
                P.op("act", lambda e, c=c, pb=pb, e_=e_, hv=hv: e.activation(out=vb[e_][:, c, :], in_=B[pb][:, :128], func=AF.Identity, scale=beta[:, c, hv:hv + 1]),
                     reads=[RB[pb], r_beta], writes=[r_vb[e_]])
                yield

    for _ in prep(0):
        pass
    for hq in range(n_heads):
        gp = prep(hq + 1) if hq + 1 < n_heads else iter(())
        def unit(e_, c, S):
            hv = 2 * hq + e_
            cs = slice(c * 128, (c + 1) * 128)
            ob = c % 2
            b, rb, cols, r_cols = S.b, S.rb, S.cols, S.r_cols
            R_ = b[0][:, 0:128]; G_ = b[1][:, 128:256]; Y2_ = b[1][:, 256:384]; M_ = b[1][:, 384:512]
            TT_ = b[2][:, 0:128]; UU_ = b[2][:, 128:256]; X2_ = b[0][:, 128:256]; W_ = b[0][:, 256:384]
            qT, r_qT, kT, r_kT, kTok, r_kTok, vb, r_vb = HS[hq % 2]
            P.op("dve", lambda e: e.tensor_scalar_mul(out=S.LAU[:], in0=U[:], scalar1=la[:, c, hv:hv + 1]), reads=[r_U, r_la], writes=[S.r_LAU]); yield
            P.op("pe", lambda e: e.matmul(R_, lhsT=onesf[:], rhs=S.LAU[:], start=True, stop=True), reads=[r_ones, S.r_LAU], writes=[rb[0]]); yield
            P.op("pe", lambda e: e.matmul(b[1][:, 0:128], lhsT=S.LAU[:], rhs=onesf[:], start=True, stop=True), reads=[r_ones, S.r_LAU], writes=[rb[1]]); yield
            P.op("act", lambda e: e.activation(out=cols[:, 0:1], in_=b[1][:, 0:1], func=AF.Copy), reads=[rb[1]], writes=[r_cols]); yield
            P.op("act", lambda e: e.activation(out=cols[:, 5:6], in_=b[1][:, 0:1], func=AF.Identity, scale=-1.0), reads=[rb[1], r_cols], writes=[r_cols]); yield
            P.op("act", lambda e: e.activation(out=S.Zabs[:], in_=R_, func=AF.Abs, bias=cols[:, 5:6], scale=1.0), reads=[rb[0], r_cols], writes=[S.r_Z]); yield
            P.op("act", lambda e: e.activation(out=S.decA[:], in_=S.Zabs[:], func=AF.Exp, scale=-1.0), reads=[S.r_Z], writes=[S.r_decA]); yield
            P.op("dve", lambda e: e.tensor_scalar(out=cols[:, 3:4], in0=b[0][:, 127:128], scalar1=cols[:, 0:1], scalar2=None, op0=ALU.subtract),
                 reads=[rb[0], r_cols], writes=[r_cols]); yield
            P.op("act", lambda e: e.activation(out=cols[:, 1:2], in_=cols[:, 0:1], func=AF.Exp), reads=[r_cols], writes=[r_cols]); yield
            P.op("act", lambda e: e.activation(out=cols[:, 2:3], in_=b[0][:, 127:128], func=AF.Exp), reads=[rb[0], r_cols], writes=[r_cols]); yield
            P.op("act", lambda e: e.activation(out=cols[:, 3:4], in_=cols[:, 3:4], func=AF.Exp), reads=[r_cols], writes=[r_cols]); yield
            P.op("act", lambda e: e.activation(out=S.egR[:], in_=R_, func=AF.Exp), reads=[rb[0]], writes=[S.r_egR]); yield
            P.op("dve", lambda e: e.tensor_tensor(out=S.qdT[:], in0=qT[:, cs], in1=S.egR[:], op=ALU.mult), reads=[r_qT, S.r_egR], writes=[S.r_qdT]); yield
            P.op("dve", lambda e: e.tensor_tensor(out=cols[:, 4:5], in0=cols[:, 1:2], in1=beta[:, c, hv:hv + 1], op=ALU.mult), reads=[r_cols, r_beta], writes=[r_cols]); yield
            P.op("pool", lambda e: e.tensor_tensor(out=S.t1[:], in0=S.decA[:], in1=msl[:], op=ALU.mult), reads=[S.r_decA, r_msl], writes=[S.r_t1]); yield
            P.op("pool", lambda e: e.tensor_tensor(out=S.t2[:], in0=S.decA[:], in1=U[:], op=ALU.mult), reads=[S.r_decA, r_U], writes=[S.r_t2]); yield
            X0, rX0 = S.Xs[0]; Y0, rY0 = S.Ys[0]; T0, rT0 = S.Ts[0]
            P.op("pe", lambda e: e.matmul(G_, lhsT=kT[:, cs], rhs=kT[:, cs], start=True, stop=True), reads=[r_kT], writes=[rb[1]]); yield
            P.op("dve", lambda e: e.scalar_tensor_tensor(out=X0[:], in0=G_, scalar=negb[:, c, hv:hv + 1], in1=S.t1[:], op0=ALU.mult, op1=ALU.mult),
                 reads=[rb[1], r_negb, S.r_t1], writes=[rX0]); yield
            P.op("pe", lambda e: e.matmul(TT_, lhsT=X0[:], rhs=identf[:], start=True, stop=True), reads=[rX0, r_idf], writes=[rb[2]]); yield
            P.op("act", lambda e: e.activation(out=Y0[:], in_=TT_, func=AF.Copy), reads=[rb[2]], writes=[rY0]); yield
            P.op("dve", lambda e: e.tensor_tensor(out=T0[:], in0=TT_, in1=identf[:], op=ALU.add), reads=[rb[2], r_idf], writes=[rT0]); yield
            cur = 0
            for lvl in range(1, 7):
                Xc, rXc = S.Xs[cur]; Yc, rYc = S.Ys[cur]; Tc, rTc = S.Ts[cur]
                Xn, rXn = S.Xs[1 - cur]; Yn, rYn = S.Ys[1 - cur]; Tn, rTn = S.Ts[1 - cur]
                last = (lvl == 6)
                P.op("pe", lambda e, Xc=Xc, Yc=Yc: e.matmul(X2_, lhsT=Yc[:], rhs=Xc[:], start=True, stop=True), reads=[rXc, rYc], writes=[rb[0]]); yield
                if not last:
                    P.op("pe", lambda e, Xc=Xc, Yc=Yc: e.matmul(Y2_, lhsT=Xc[:], rhs=Yc[:], start=True, stop=True), reads=[rXc, rYc], writes=[rb[1]]); yield
                P.op("act", lambda e, Xn=Xn: e.activation(out=Xn[:], in_=X2_, func=AF.Copy), reads=[rb[0]], writes=[rXn]); yield
                if not last:
                    P.op("dve", lambda e, Yn=Yn: e.tensor_copy(out=Yn[:], in_=Y2_), reads=[rb[1]], writes=[rYn]); yield
                P.op("pe", lambda e, Xn=Xn, Tc=Tc: e.matmul(TT_, lhsT=Xn[:], rhs=Tc[:], start=True, stop=False), reads=[rXn, rTc], writes=[rb[2]], inc=False)
                P.op("pe", lambda e, Tc=Tc: e.matmul(TT_, lhsT=identf[:], rhs=Tc[:], start=False, stop=True), reads=[r_idf, rTc], writes=[rb[2]]); yield
                if not last:
                    P.op("dve", lambda e, Tn=Tn: e.tensor_copy(out=Tn[:], in_=TT_), reads=[rb[2]], writes=[rTn]); yield
                else:
                    P.op("dve", lambda e: e.tensor_copy(out=S.Ttb[:], in_=TT_), reads=[rb[2]], writes=[S.r_Ttb]); yield
                cur = 1 - cur
            P.op("act", lambda e: e.activation(out=S.kbg[:], in_=kTok[:, c, :], func=AF.Identity, scale=cols[:, 4:5]), reads=[r_kTok, r_cols], writes=[S.r_kbg]); yield
            P.op("act", lambda e: e.activation(out=S.kd[:], in_=kTok[:, c, :], func=AF.Identity, scale=cols[:, 3:4]), reads=[r_kTok, r_cols], writes=[S.r_kd]); yield
            P.op("pe", lambda e: e.matmul(W_, lhsT=S.kbg[:], rhs=S.Ttb[:], start=True, stop=True), reads=[S.r_kbg, S.r_Ttb], writes=[rb[0]]); yield
            P.op("act", lambda e: e.activation(out=S.nwT[:], in_=W_, func=AF.Identity, scale=-1.0), reads=[rb[0]], writes=[S.r_nwT]); yield
            if c > 0:
                P.op("pe", lambda e: e.matmul(UU_, lhsT=S.Ttb[:], rhs=vb[e_][:, c, :], start=True, stop=False), reads=[S.r_Ttb, r_vb[e_]], writes=[rb[2]], inc=False)
                P.op("pe", lambda e: e.matmul(UU_, lhsT=S.nwT[:], rhs=S.m_bf[:], start=False, stop=True), reads=[S.r_nwT, S.r_mbf], writes=[rb[2]]); yield
            else:
                P.op("pe", lambda e: e.matmul(UU_, lhsT=S.Ttb[:], rhs=vb[e_][:, c, :], start=True, stop=True), reads=[S.r_Ttb, r_vb[e_]], writes=[rb[2]]); yield
            P.op("act", lambda e: e.activation(out=S.u_sb[:], in_=UU_, func=AF.Copy), reads=[rb[2]], writes=[S.r_u]); yield
            P.op("pe", lambda e: e.matmul(G_, lhsT=kT[:, cs], rhs=qT[:, cs], start=True, stop=True), reads=[r_kT, r_qT], writes=[rb[1]]); yield
            P.op("dve", lambda e: e.tensor_tensor(out=S.QKT[:], in0=G_, in1=S.t2[:], op=ALU.mult), reads=[rb[1], S.r_t2], writes=[S.r_QKT]); yield
            if c > 0:
                P.op("pe", lambda e: e.matmul(R_, lhsT=S.qdT[:], rhs=S.m_bf[:], start=True, stop=False), reads=[S.r_qdT, S.r_mbf], writes=[rb[0]], inc=False)
            P.op("pe", lambda e: e.matmul(R_, lhsT=S.QKT[:], rhs=S.u_sb[:], start=(c == 0), stop=True), reads=[S.r_QKT, S.r_u], writes=[rb[0]]); yield
            P.op("act", lambda e: e.activation(out=S.o_sb[ob][:], in_=R_, func=AF.Identity, scale=128.0 ** -0.5), reads=[rb[0]], writes=[S.r_osb[ob]]); yield
            P.dma("sp", o_d[c * 128:(c + 1) * 128, hv, :], S.o_sb[ob][:], reads=[S.r_osb[ob]]); yield
            if c < NCH - 1:
                P.op("pe", lambda e: e.matmul(M_, lhsT=S.kd[:], rhs=S.u_sb[:], start=True, stop=True), reads=[S.r_kd, S.r_u], writes=[rb[1]]); yield
                if c == 0:
                    P.op("dve", lambda e: e.tensor_copy(out=S.m[:], in_=M_), reads=[rb[1]], writes=[S.r_m]); yield
                else:
                    P.op("dve", lambda e: e.scalar_tensor_tensor(out=S.m[:], in0=S.m[:], scalar=cols[:, 2:3], in1=M_, op0=ALU.mult, op1=ALU.add),
                         reads=[S.r_m, r_cols, rb[1]], writes=[S.r_m]); yield
                P.op("pool", lambda e: e.tensor_copy(out=S.m_bf[:], in_=S.m[:]), reads=[S.r_m], writes=[S.r_mbf]); yield

        for c in range(NCH):
            g0 = unit(0, c, ST[0]); g1 = unit(1, c, ST[1])
            d0 = d1 = False
            while not (d0 and d1):
                if not d0:
                    try:
                        next(g0)
                    except StopIteration:
                        d0 = True
                if not d1:
                    try:
                        next(g1)
                    except StopIteration:
                        d1 = True
                next(gp, None)
        for _ in gp:
            pass
    P.finish()
    return nc


def run_gdn(inputs, mod, hcat, n_heads=16):
    nc = build_gdn(n_heads)
    w = inputs["gdn_w_in"][0]
    wqkv = np.ascontiguousarray(w[:, 0:8192])
    kk = np.arange(128)[:, None]; ii = np.arange(128)[None, :]
    U = (kk <= ii).astype(np.float32)
    msl = (kk > ii).astype(np.float32)
    in_maps = []
    for core in range(8):
        b, d = core // 2, core % 2
        h = hcat[b]
        if d:
            h = _seg_flip_np(h)
        cw = inputs["gdn_conv"][0]
        if d:
            cw = cw[::-1]
        convw = np.ascontiguousarray(cw.T.reshape(64, 128, 5).transpose(1, 0, 2))
        wba = np.ascontiguousarray(np.concatenate([w[:, 12288 + d * 32:12288 + (d + 1) * 32], w[:, 12352 + d * 32:12352 + (d + 1) * 32]], 1))
        in_maps.append({
            "hT": np.ascontiguousarray(h.T), "msc": _msc(mod, 1, b, 1, 0), "wqkv": wqkv, "wba": wba, "convw": convw,
            "alog": np.ascontiguousarray(inputs["gdn_a_log"][0, d][None, :]), "dtb": np.ascontiguousarray(inputs["gdn_dt_bias"][0, d][None, :]),
            "identf": np.eye(128, dtype=np.float32), "U": U, "msl": msl})
    res = run_bass_kernel_spmd(nc, in_maps, core_ids=list(range(8)))
    out = np.zeros((4, 2, NTOT, 32, 128), np.float32)
    for core in range(8):
        b, d = core // 2, core % 2
        o = res.results[core]["o"]
        out[b, d] = _seg_flip_np(o) if d else o
    return out


def kernel(**inputs):
    inputs = {k: np.asarray(v, dtype=np.float32) for k, v in inputs.items()}
    mod = run_ada(inputs)
    hcat = np.concatenate([inputs["ctx"], inputs["x"]], axis=1)
    o = run_ret(inputs, mod, hcat)
    h1 = run_blk(inputs, mod, 0, o.reshape(4, 2, NTOT, 4096), hcat)
    o2 = run_gdn(inputs, mod, h1)
    out = run_blk(inputs, mod, 1, o2.reshape(4, 2, NTOT, 4096), h1)
    return out.astype(np.float32)
```

```python
import contextlib
import numpy as np
import concourse.bass as bass
import concourse.mybir as mybir
from concourse.bass_utils import run_bass_kernel_spmd

F32 = mybir.dt.float32
BF16 = mybir.dt.bfloat16
AF = mybir.ActivationFunctionType
ALU = mybir.AluOpType
AX = mybir.AxisListType

D = 2048
NCTX = 256
NLAT = 2048
NTOT = NCTX + NLAT
DFF = 5632
ALPHA = 4.0 ** 0.25
LN_EPS = 1e-5


class Res:
    __slots__ = ("w", "rd", "name", "excl")

    def __init__(self, name="", excl=False):
        self.w = []
        self.rd = {}
        self.name = name
        self.excl = excl


class Prog:
    ENG = ("pe", "act", "dve", "pool", "sp")

    def __init__(self, nc, n_slots=32):
        self.nc = nc
        self.stack = contextlib.ExitStack()
        self.sem = {}
        self.cnt = {}
        self.pending = {e: False for e in self.ENG}
        self.seen = {e: {} for e in self.ENG}
        self.streams = {e: [] for e in self.ENG}
        for e in self.ENG:
            if e == "sp":
                continue
            self.sem[e] = self.stack.enter_context(nc.semaphore("s_" + e))
            self.cnt[e] = 0
        self.n_slots = n_slots
        for i in range(n_slots):
            n = "d%d" % i
            self.sem[n] = self.stack.enter_context(nc.semaphore("s_" + n))
            self.cnt[n] = 0
        self.slot_i = 0
        self.slot_p = 0
        self.n_inst = {e: 0 for e in self.ENG}
        self._uid = 0

    def sbuf(self, shape, dtype, name=None):
        self._uid += 1
        return self.stack.enter_context(self.nc.sbuf_tensor(name or ("t%d" % self._uid), list(shape), dtype))

    def psum(self, shape, dtype, name=None):
        self._uid += 1
        return self.stack.enter_context(self.nc.psum_tensor(name or ("p%d" % self._uid), list(shape), dtype))

    def res(self, name="", excl=False):
        return Res(name, excl)

    def _need(self, eng, reads, writes):
        need = {}
        for r in reads:
            for (s, v) in r.w:
                if need.get(s, 0) < v:
                    need[s] = v
        for r in writes:
            for (s, v) in r.w:
                if need.get(s, 0) < v:
                    need[s] = v
            for s, v in r.rd.items():
                if need.get(s, 0) < v:
                    need[s] = v
        seen = self.seen[eng]
        for s, v in need.items():
            if s == "pe" and eng == "pe":
                continue
            if seen.get(s, 0) >= v:
                continue
            seen[s] = v
            sem = self.sem[s]
            self.streams[eng].append(lambda e, sem=sem, v=v: e.wait_ge(sem, v))

    def op(self, eng, fn, reads=(), writes=(), inc=True):
        if eng != "pe":
            ex = [r for r in reads if r.excl]
            if ex:
                reads = [r for r in reads if not r.excl]
                writes = list(writes) + ex
        self._need(eng, reads, writes)
        self.n_inst[eng] += 1
        if inc:
            self.cnt[eng] += 1
            val = self.cnt[eng]
            sem = self.sem[eng]
            self.streams[eng].append(lambda e, fn=fn, sem=sem: fn(e).then_inc(sem, 1))
            self.pending[eng] = False
        else:
            val = self.cnt[eng] + 1
            self.streams[eng].append(lambda e, fn=fn: fn(e))
            self.pending[eng] = True
        for r in reads:
            if r.rd.get(eng, 0) < val:
                r.rd[eng] = val
        for r in writes:
            r.w = [(eng, val)]
            r.rd = {}
        return val

    def dma(self, q, out, in_, reads=(), writes=(), **kw):
        half = self.n_slots // 2
        if q == "pool":
            s = "d%d" % (half + self.slot_p)
            self.slot_p = (self.slot_p + 1) % half
        else:
            s = "d%d" % self.slot_i
            self.slot_i = (self.slot_i + 1) % half
        self._need(q, reads, writes)
        seen = self.seen[q]
        if seen.get(s, 0) < self.cnt[s]:
            seen[s] = self.cnt[s]
            self.streams[q].append(lambda e, sem=self.sem[s], v=self.cnt[s]: e.wait_ge(sem, v))
        self.cnt[s] += 16
        val = self.cnt[s]
        sem = self.sem[s]
        self.streams[q].append(lambda e, sem=sem, out=out, in_=in_, kw=kw: e.dma_start(out=out, in_=in_, **kw).then_inc(sem, 16))
        self.n_inst[q] += 1
        for r in reads:
            if r.rd.get(s, 0) < val:
                r.rd[s] = val
        for r in writes:
            r.w = [(s, val)]
            r.rd = {}
        return (s, val)

    def finish(self):
        for e in self.ENG:
            assert not self.pending[e], "engine %s has pending un-signalled ops" % e
        for i in range(self.n_slots):
            s = "d%d" % i
            if self.cnt[s] > 0:
                self.streams["sp"].append(lambda e, sem=self.sem[s], v=self.cnt[s]: e.wait_ge(sem, v))
        for en in ("pe", "act", "dve", "pool"):
            if self.cnt[en] > 0:
                self.streams["sp"].append(lambda e, sem=self.sem[en], v=self.cnt[en]: e.wait_ge(sem, v))
        nc = self.nc
        streams = self.streams
        with nc.Block() as block:
            @block.sync
            def _(e):
                for f in streams["sp"]:
                    f(e)

            @block.tensor
            def _(e):
                for f in streams["pe"]:
                    f(e)

            @block.scalar
            def _(e):
                for f in streams["act"]:
                    f(e)

            @block.vector
            def _(e):
                for f in streams["dve"]:
                    f(e)

            @block.gpsimd
            def _(e):
                for f in streams["pool"]:
                    f(e)
        self.stack.close()


def new_nc():
    return bass.Bass("TRN2", target_bir_lowering=False)


def build_ada():
    nc = new_nc()
    cc = nc.dram_tensor("cc", [5, 2048], F32, kind="ExternalInput").ap()
    adaw = nc.dram_tensor("adaw", [2, 2048, 1536], F32, kind="ExternalInput").ap()
    adab = nc.dram_tensor("adab", [2, 1, 1536], F32, kind="ExternalInput").ap()
    ident_d = nc.dram_tensor("ident", [128, 128], F32, kind="ExternalInput").ap()
    mod = nc.dram_tensor("mod", [2, 5, 1536], F32, kind="ExternalOutput").ap()
    P = Prog(nc)
    cc_t = P.sbuf([5, 2048], F32); r_cc = P.res()
    sc = P.sbuf([5, 2048], F32); r_sc = P.res()
    ident = P.sbuf([128, 128], F32); r_id = P.res()
    ones = P.sbuf([1, 8], F32); r_ones = P.res()
    scT = P.sbuf([128, 16, 5], F32); r_scT = P.res()
    bias = P.sbuf([1, 2, 1536], F32); r_bias = P.res()
    P.dma("sp", cc_t[:], cc, writes=[r_cc])
    P.dma("sp", ident[:], ident_d, writes=[r_id])
    P.dma("sp", bias[:], adab.rearrange("l o n -> o l n"), writes=[r_bias])
    P.op("dve", lambda e: e.memset(ones[:], 1.0), writes=[r_ones])
    P.op("act", lambda e: e.activation(out=sc[:], in_=cc_t[:], func=AF.Silu), reads=[r_cc], writes=[r_sc])
    ps_t = P.psum([128, 16, 5], F32); r_pst = P.res()
    for kc in range(16):
        P.op("pe", lambda e, kc=kc: e.transpose(ps_t[:, kc, :], sc[:, kc * 128:(kc + 1) * 128], ident[:5, :5]),
             reads=[r_sc, r_id], writes=[r_pst], inc=(kc == 15))
    P.op("dve", lambda e: e.tensor_copy(out=scT[:], in_=ps_t[:]), reads=[r_pst], writes=[r_scT])
    wt = [P.sbuf([128, 16, 512], F32) for _ in range(3)]; r_wt = [P.res() for _ in range(3)]
    ps = [P.psum([5, 512], F32) for _ in range(2)]; r_ps = [P.res() for _ in range(2)]
    ot = [P.sbuf([5, 512], F32) for _ in range(2)]; r_ot = [P.res() for _ in range(2)]
    it = 0
    for l in range(2):
        for g in range(3):
            b = it % 3; pb = it % 2
            q = ["sp", "pool", "act"][it % 3]
            P.dma(q, wt[b][:], adaw[l, :, g * 512:(g + 1) * 512].rearrange("(kc p) n -> p kc n", p=128), writes=[r_wt[b]])
            for kc in range(16):
                P.op("pe", lambda e, kc=kc, b=b, pb=pb: e.matmul(ps[pb][:], lhsT=scT[:, kc, :], rhs=wt[b][:, kc, :], start=(kc == 0), stop=False),
                     reads=[r_scT, r_wt[b]], writes=[r_ps[pb]], inc=False)
            P.op("pe", lambda e, l=l, g=g, pb=pb: e.matmul(ps[pb][:], lhsT=ones[:, :5], rhs=bias[:, l, g * 512:(g + 1) * 512], start=False, stop=True),
                 reads=[r_ones, r_bias], writes=[r_ps[pb]], inc=True)
            P.op("dve", lambda e, pb=pb: e.tensor_copy(out=ot[pb][:], in_=ps[pb][:]), reads=[r_ps[pb]], writes=[r_ot[pb]])
            P.dma("sp", mod[l, :, g * 512:(g + 1) * 512], ot[pb][:], reads=[r_ot[pb]])
            it += 1
    P.finish()
    return nc


def emit_uT(P, hT, msc_d, uT, r_uT, T, segs):
    msc = P.sbuf([128, 16, 4], F32); r_msc = P.res()
    P.dma("sp", msc[:], msc_d, writes=[r_msc])
    P.op("dve", lambda e: e.tensor_scalar_add(out=msc[:, :, 0:1], in0=msc[:, :, 0:1], scalar1=1.0), reads=[r_msc], writes=[r_msc])
    P.op("dve", lambda e: e.tensor_scalar_add(out=msc[:, :, 2:3], in0=msc[:, :, 2:3], scalar1=1.0), reads=[r_msc], writes=[r_msc])
    stg = [P.sbuf([128, T], F32) for _ in range(2)]; r_stg = [P.res() for _ in range(2)]
    for kc in range(16):
        b = kc % 2
        P.dma(["sp", "act"][kc % 2], stg[b][:], hT[kc * 128:(kc + 1) * 128, :], writes=[r_stg[b]])
        for (t0, t1, w) in segs:
            P.op("act", lambda e, kc=kc, b=b, t0=t0, t1=t1, w=w: e.activation(
                out=uT[:, kc, t0:t1], in_=stg[b][:, t0:t1], func=AF.Identity,
                scale=msc[:, kc, 2 * w:2 * w + 1], bias=msc[:, kc, 2 * w + 1:2 * w + 2]),
                reads=[r_stg[b], r_msc], writes=[r_uT])


def build_ret():
    nc = new_nc()
    T = NTOT
    hT = nc.dram_tensor("hT", [D, T], F32, kind="ExternalInput").ap()
    msc_d = nc.dram_tensor("msc", [128, 16, 4], F32, kind="ExternalInput").ap()
    wq = nc.dram_tensor("wq", [D, 2048], F32, kind="ExternalInput").ap()
    wk = nc.dram_tensor("wk", [D, 2048], F32, kind="ExternalInput").ap()
    wv = nc.dram_tensor("wv", [D, 4096], F32, kind="ExternalInput").ap()
    cos_d = nc.dram_tensor("cosT", [128, NLAT], F32, kind="ExternalInput").ap()
    sin_d = nc.dram_tensor("sinT", [128, NLAT], F32, kind="ExternalInput").ap()
    mask_d = nc.dram_tensor("mask01", [128, 128], F32, kind="ExternalInput").ap()
    expo_d = nc.dram_tensor("expo", [128, 128], F32, kind="ExternalInput").ap()
    idx_d = nc.dram_tensor("idxs", [128, 4], F32, kind="ExternalInput").ap()
    rdec_d = nc.dram_tensor("rdec", [1, 8], F32, kind="ExternalInput").ap()
    identb_d = nc.dram_tensor("identb", [128, 128], F32, kind="ExternalInput").ap()
    o_d = nc.dram_tensor("o", [T, 8, 512], F32, kind="ExternalOutput").ap()
    P = Prog(nc)
    uT = P.sbuf([128, 16, T], BF16); r_uT = P.res()
    emit_uT(P, hT, msc_d, uT, r_uT, T, [(0, NCTX, 0), (NCTX, T, 1)])
    cosT = P.sbuf([128, NLAT], F32); r_cos = P.res()
    sinT = P.sbuf([128, NLAT], F32); r_sin = P.res()
    mask01 = P.sbuf([128, 128], F32); expo = P.sbuf([128, 128], F32); idxs = P.sbuf([128, 4], F32)
    rdec = P.sbuf([128, 8], F32); identf = P.sbuf([128, 128], F32); identb = P.sbuf([128, 128], BF16)
    r_c = P.res()
    P.dma("sp", cosT[:], cos_d, writes=[r_cos])
    P.dma("sp", sinT[:], sin_d, writes=[r_sin])
    P.dma("sp", mask01[:], mask_d, writes=[r_c])
    r_c2 = P.res(); r_c3 = P.res(); r_c4 = P.res(); r_c5 = P.res()
    P.dma("sp", expo[:], expo_d, writes=[r_c2])
    P.dma("sp", idxs[:], idx_d, writes=[r_c3])
    P.dma("sp", rdec[:], rdec_d.partition_broadcast(128), writes=[r_c4])
    P.dma("sp", identf[:], identb_d, writes=[r_c5])
    r_idb = P.res()
    P.op("dve", lambda e: e.tensor_copy(out=identb[:], in_=identf[:]), reads=[r_c5], writes=[r_idb])
    lg = P.sbuf([128, 8], F32); r_lg = P.res()
    P.op("act", lambda e: e.activation(out=lg[:], in_=rdec[:], func=AF.Exp), reads=[r_c4], writes=[r_lg])
    P.op("dve", lambda e: e.tensor_scalar_mul(out=lg[:], in0=lg[:], scalar1=-1.0), reads=[r_lg], writes=[r_lg])
    maskT = P.sbuf([128, 8, 128], F32); r_mask = P.res()
    decs = P.sbuf([128, 8, 4], F32); r_decs = P.res()
    for h in range(8):
        P.op("act", lambda e, h=h: e.activation(out=maskT[:, h, :], in_=expo[:], func=AF.Exp, scale=lg[:, h:h + 1]),
             reads=[r_c2, r_lg], writes=[r_mask])
        P.op("dve", lambda e, h=h: e.scalar_tensor_tensor(out=maskT[:, h, :], in0=maskT[:, h, :], scalar=1.0 / 16.0, in1=mask01[:],
                                                          op0=ALU.mult, op1=ALU.mult), reads=[r_mask, r_c], writes=[r_mask])
        P.op("act", lambda e, h=h: e.activation(out=decs[:, h, 0:3], in_=idxs[:, 0:3], func=AF.Exp, scale=lg[:, h:h + 1]),
             reads=[r_c3, r_lg], writes=[r_decs])
        P.op("dve", lambda e, h=h: e.tensor_scalar_mul(out=decs[:, h, 0:1], in0=decs[:, h, 0:1], scalar1=1.0 / 16.0),
             reads=[r_decs], writes=[r_decs])
    qT = P.sbuf([128, 2, T], BF16); r_qT = P.res()
    kT = P.sbuf([128, 2, T], BF16); r_kT = P.res()
    vt = P.sbuf([128, 18, 512], BF16); r_v = P.res()
    wq_t = [P.sbuf([128, 16, 256], BF16)] * 2; r_wq = [P.res()] * 2
    wk_t = [P.sbuf([128, 16, 256], BF16)] * 2; r_wk = [P.res()] * 2
    wv_t = [P.sbuf([128, 16, 512], BF16)] * 2; r_wv = [P.res()] * 2
    ps_a = [P.psum([128, 512], F32) for _ in range(2)]; r_psa = [P.res() for _ in range(2)]
    ps_S = P.psum([128, 512], F32); r_psS = P.res()
    ps_kt = P.psum([128, 1024], BF16); r_pskt = P.res()
    ps_o = P.psum([128, 512], F32); r_pso = P.res()
    ps_i = P.psum([128, 512], F32); r_psi = P.res()
    ps_s = [P.psum([128, 512], F32) for _ in range(2)]; r_pss = [P.res() for _ in range(2)]
    tmp = [P.sbuf([128, 512], F32) for _ in range(4)]; r_tmp = [P.res() for _ in range(4)]
    AT = P.sbuf([128, 128], BF16); r_AT = P.res()
    kd = P.sbuf([128, 256], BF16); r_kd = P.res()
    o_sb = [P.sbuf([128, 512], F32) for _ in range(2)]; r_osb = [P.res() for _ in range(2)]
    st = P.sbuf([128, 2, 512], F32); r_st = P.res()
    st_bf = P.sbuf([128, 2, 512], BF16); r_stbf = P.res()
    groups = [(0, 256, 0), (256, 768, 1), (768, 1280, 1), (1280, 1792, 1), (1792, 2304, 1)]

    def load_w(h):
        b = h % 2
        P.dma("pool", wq_t[b][:], wq[:, h * 256:(h + 1) * 256].rearrange("(kc p) n -> p kc n", p=128), writes=[r_wq[b]])
        P.dma("pool", wk_t[b][:], wk[:, h * 256:(h + 1) * 256].rearrange("(kc p) n -> p kc n", p=128), writes=[r_wk[b]])
        P.dma("pool", wv_t[b][:], wv[:, h * 512:(h + 1) * 512].rearrange("(kc p) n -> p kc n", p=128), writes=[r_wv[b]])

    for h in range(8):
        b = h % 2
        load_w(h)
        for (w_t, r_w, dst, r_dst) in ((wq_t[b], r_wq[b], qT, r_qT), (wk_t[b], r_wk[b], kT, r_kT)):
            for (t0, t1, lat) in groups:
                n = t1 - t0
                for dc in range(2):
                    for kc in range(16):
                        P.op("pe", lambda e, dc=dc, kc=kc, w_t=w_t, t0=t0, t1=t1, n=n: e.matmul(
                            ps_a[dc][:, :n], lhsT=w_t[:, kc, dc * 128:(dc + 1) * 128], rhs=uT[:, kc, t0:t1],
                            start=(kc == 0), stop=(kc == 15)), reads=[r_w, r_uT], writes=[r_psa[dc]], inc=(kc == 15))
                if not lat:
                    for dc in range(2):
                        P.op("act", lambda e, dc=dc, dst=dst, t0=t0, t1=t1, n=n: e.activation(out=dst[:, dc, t0:t1], in_=ps_a[dc][:, :n], func=AF.Copy),
                             reads=[r_psa[dc]], writes=[r_dst])
                else:
                    l0 = t0 - NCTX; l1 = t1 - NCTX
                    P.op("dve", lambda e, l0=l0, l1=l1: e.tensor_tensor(out=tmp[0][:], in0=ps_a[0][:], in1=cosT[:, l0:l1], op=ALU.mult),
                         reads=[r_psa[0], r_cos], writes=[r_tmp[0]])
                    P.op("dve", lambda e, l0=l0, l1=l1: e.tensor_tensor(out=tmp[1][:], in0=ps_a[1][:], in1=sinT[:, l0:l1], op=ALU.mult),
                         reads=[r_psa[1], r_sin], writes=[r_tmp[1]])
                    P.op("dve", lambda e, l0=l0, l1=l1: e.tensor_tensor(out=tmp[2][:], in0=ps_a[0][:], in1=sinT[:, l0:l1], op=ALU.mult),
                         reads=[r_psa[0], r_sin], writes=[r_tmp[2]])
                    P.op("dve", lambda e, l0=l0, l1=l1: e.tensor_tensor(out=tmp[3][:], in0=ps_a[1][:], in1=cosT[:, l0:l1], op=ALU.mult),
                         reads=[r_psa[1], r_cos], writes=[r_tmp[3]])
                    P.op("pool", lambda e, dst=dst, t0=t0, t1=t1: e.tensor_tensor(out=dst[:, 0, t0:t1], in0=tmp[0][:], in1=tmp[1][:], op=ALU.subtract),
                         reads=[r_tmp[0], r_tmp[1]], writes=[r_dst])
                    P.op("pool", lambda e, dst=dst, t0=t0, t1=t1: e.tensor_tensor(out=dst[:, 1, t0:t1], in0=tmp[2][:], in1=tmp[3][:], op=ALU.add),
                         reads=[r_tmp[2], r_tmp[3]], writes=[r_dst])
        for c in range(18):
            pb = c % 2
            for kc in range(16):
                P.op("pe", lambda e, c=c, kc=kc, pb=pb, b=b: e.matmul(ps_a[pb][:], lhsT=uT[:, kc, c * 128:(c + 1) * 128], rhs=wv_t[b][:, kc, :],
                                                                 start=(kc == 0), stop=(kc == 15)), reads=[r_wv[b], r_uT], writes=[r_psa[pb]], inc=(kc == 15))
            P.op("act", lambda e, c=c, pb=pb: e.activation(out=vt[:, c, :], in_=ps_a[pb][:], func=AF.Copy), reads=[r_psa[pb]], writes=[r_v])
        for c in range(18):
            cs = slice(c * 128, (c + 1) * 128)
            ob = c % 2
            for dc in range(2):
                P.op("pe", lambda e, dc=dc, cs=cs: e.matmul(ps_S[:, :128], lhsT=kT[:, dc, cs], rhs=qT[:, dc, cs], start=(dc == 0), stop=(dc == 1)),
                     reads=[r_kT, r_qT], writes=[r_psS], inc=(dc == 1))
            P.op("dve", lambda e, h=h: e.tensor_tensor(out=AT[:], in0=ps_S[:, :128], in1=maskT[:, h, :], op=ALU.mult),
                 reads=[r_psS, r_mask], writes=[r_AT])
            P.op("pe", lambda e, c=c: e.matmul(ps_o[:], lhsT=AT[:], rhs=vt[:, c, :], start=True, stop=True), reads=[r_AT, r_v], writes=[r_pso])
            P.op("act", lambda e, ob=ob: e.activation(out=o_sb[ob][:], in_=ps_o[:], func=AF.Copy), reads=[r_pso], writes=[r_osb[ob]])
            if c > 0:
                for dc in range(2):
                    P.op("pe", lambda e, dc=dc, cs=cs: e.matmul(ps_i[:], lhsT=qT[:, dc, cs], rhs=st_bf[:, dc, :], start=(dc == 0), stop=(dc == 1)),
                         reads=[r_qT, r_stbf], writes=[r_psi], inc=(dc == 1))
                P.op("dve", lambda e, ob=ob, h=h: e.scalar_tensor_tensor(out=o_sb[ob][:], in0=ps_i[:], scalar=decs[:, h, 0:1], in1=o_sb[ob][:],
                                                                         op0=ALU.mult, op1=ALU.add), reads=[r_psi, r_decs, r_osb[ob]], writes=[r_osb[ob]])
            P.dma("sp", o_d[c * 128:(c + 1) * 128, h, :], o_sb[ob][:], reads=[r_osb[ob]])
            if c < 17:
                for dc in range(2):
                    P.op("pe", lambda e, dc=dc, cs=cs: e.transpose(ps_kt[:, dc * 128:(dc + 1) * 128], kT[:, dc, cs], identb[:]),
                         reads=[r_kT, r_idb], writes=[r_pskt], inc=(dc == 1))
                P.op("act", lambda e, h=h: e.activation(out=kd[:], in_=ps_kt[:, :256], func=AF.Identity, scale=decs[:, h, 1:2]),
                     reads=[r_pskt, r_decs], writes=[r_kd])
                for dc in range(2):
                    P.op("pe", lambda e, dc=dc, c=c: e.matmul(ps_s[dc][:], lhsT=kd[:, dc * 128:(dc + 1) * 128], rhs=vt[:, c, :], start=True, stop=True),
                         reads=[r_kd, r_v], writes=[r_pss[dc]])
                    if c == 0:
                        P.op("dve", lambda e, dc=dc: e.tensor_copy(out=st[:, dc, :], in_=ps_s[dc][:]), reads=[r_pss[dc]], writes=[r_st])
                    else:
                        P.op("dve", lambda e, dc=dc, h=h: e.scalar_tensor_tensor(out=st[:, dc, :], in0=st[:, dc, :], scalar=decs[:, h, 2:3], in1=ps_s[dc][:],
                                                                                 op0=ALU.mult, op1=ALU.add), reads=[r_st, r_decs, r_pss[dc]], writes=[r_st])
                P.op("pool", lambda e: e.tensor_copy(out=st_bf[:], in_=st[:]), reads=[r_st], writes=[r_stbf])
    P.finish()
    return nc


def _seg_flip_np(t):
    return np.concatenate([t[:NCTX][::-1], t[NCTX:][::-1]], axis=0)


def _rope_tables():
    rows = NLAT // 64
    row = np.repeat(np.arange(rows, dtype=np.float32), 64)
    col = np.tile(np.arange(64, dtype=np.float32), rows)
    n_freq = 64
    inv_freq = (np.float32(10000.0) ** (-np.arange(n_freq, dtype=np.float32) / np.float32(n_freq))).astype(np.float32)
    ang = np.concatenate([row[:, None] * inv_freq, col[:, None] * inv_freq], axis=-1).astype(np.float32)
    return np.cos(ang).astype(np.float32), np.sin(ang).astype(np.float32)


def _msc(mod, layer, b, j_scale, j_shift):
    m = mod[layer].reshape(5, 6, D)
    cols = [m[4, j_scale], m[4, j_shift], m[b, j_scale], m[b, j_shift]]
    a = np.stack(cols, axis=-1)
    return np.ascontiguousarray(a.reshape(16, 128, 4).transpose(1, 0, 2))


def run_ada(inputs):
    nc = build_ada()
    cc = np.concatenate([inputs["c"], inputs["c_ctx"][None]], 0).astype(np.float32)
    in_maps = []
    for i in range(8):
        in_maps.append({"cc": cc, "adaw": np.ascontiguousarray(inputs["ada_w"][:, :, i * 1536:(i + 1) * 1536]),
                        "adab": np.ascontiguousarray(inputs["ada_b"][:, None, i * 1536:(i + 1) * 1536]),
                        "ident": np.eye(128, dtype=np.float32)})
    res = run_bass_kernel_spmd(nc, in_maps, core_ids=list(range(8)))
    return np.concatenate([r["mod"] for r in res.results], axis=2)


def run_ret(inputs, mod, hcat):
    nc = build_ret()
    cos, sin = _rope_tables()
    w = inputs["ret_w_in"][0]
    jj = np.arange(128)[:, None]; ii = np.arange(128)[None, :]
    expo = np.maximum(ii - jj, 0).astype(np.float32)
    idxs = np.stack([np.arange(128) + 1.0, 127.0 - np.arange(128), np.full(128, 128.0), np.zeros(128)], -1).astype(np.float32)
    in_maps = []
    for core in range(8):
        b, d = core // 2, core % 2
        h = hcat[b]
        if d:
            h = _seg_flip_np(h)
        cs, sn = (cos[::-1], sin[::-1]) if d else (cos, sin)
        mask01 = (ii > jj) if d else (ii >= jj)
        in_maps.append({
            "hT": np.ascontiguousarray(h.T), "msc": _msc(mod, 0, b, 1, 0),
            "wq": np.ascontiguousarray(w[:, 0:2048]), "wk": np.ascontiguousarray(w[:, 2048:4096]),
            "wv": np.ascontiguousarray(w[:, 4096:8192]),
            "cosT": np.ascontiguousarray(cs.T), "sinT": np.ascontiguousarray(sn.T),
            "mask01": mask01.astype(np.float32), "expo": expo, "idxs": idxs,
            "rdec": np.ascontiguousarray(inputs["ret_decay"][0, d][None, :]),
            "identb": np.eye(128, dtype=np.float32)})
    res = run_bass_kernel_spmd(nc, in_maps, core_ids=list(range(8)))
    out = np.zeros((4, 2, NTOT, 8, 512), np.float32)
    for core in range(8):
        b, d = core // 2, core % 2
        o = res.results[core]["o"]
        out[b, d] = _seg_flip_np(o) if d else o
    return out


def emit_rsqrt(P, out_ap, in_ap, scale, eps_ap, r_s):
    P.op("act", lambda e: e.activation(out=out_ap, in_=in_ap, func=AF.Sqrt, bias=eps_ap, scale=scale), reads=[r_s], writes=[r_s])
    P.op("dve", lambda e: e.reciprocal(out=out_ap, in_=out_ap), reads=[r_s], writes=[r_s])


def emit_ln(P, hh, r_hh, t, lnrow_d, gi, bi, scr):
    stats, mv, rstd, gbc, bbc, r_s, r_g, r_b, eps5 = scr
    for j in range(4):
        P.op("dve", lambda e, j=j: e.bn_stats(out=stats[:, j, :], in_=hh[:, t, j * 512:(j + 1) * 512]), reads=[r_hh[t]], writes=[r_s])
    P.op("dve", lambda e: e.bn_aggr(out=mv[:], in_=stats[:].rearrange("p a b -> p (a b)")), reads=[r_s], writes=[r_s])
    emit_rsqrt(P, rstd[:, 0:1], mv[:, 1:2], 1.0, eps5[:, 0:1], r_s)
    P.op("dve", lambda e: e.tensor_scalar(out=hh[:, t, :], in0=hh[:, t, :], scalar1=mv[:, 0:1], scalar2=rstd[:, 0:1], op0=ALU.subtract, op1=ALU.mult),
         reads=[r_s, r_hh[t]], writes=[r_hh[t]])
    for j in range(4):
        cs = slice(j * 512, (j + 1) * 512)
        P.dma("sp", gbc[:], lnrow_d[gi:gi + 1, cs].partition_broadcast(128), writes=[r_g])
        P.dma("sp", bbc[:], lnrow_d[bi:bi + 1, cs].partition_broadcast(128), writes=[r_b])
        P.op("pool", lambda e, cs=cs: e.tensor_tensor(out=hh[:, t, cs], in0=hh[:, t, cs], in1=gbc[:], op=ALU.mult), reads=[r_g, r_hh[t]], writes=[r_hh[t]])
        P.op("pool", lambda e, cs=cs: e.tensor_tensor(out=hh[:, t, cs], in0=hh[:, t, cs], in1=bbc[:], op=ALU.add), reads=[r_b, r_hh[t]], writes=[r_hh[t]])


def build_blk(layer):
    nc = new_nc()
    T = 1152 if layer == 0 else 1024
    nt = T // 128
    tgs = [(0, 384), (384, 768), (768, 1152)] if layer == 0 else [(0, 512), (512, 1024)]
    of_d = nc.dram_tensor("of", [T, 4096], F32, kind="ExternalInput").ap()
    ob_d = nc.dram_tensor("ob", [T, 4096], F32, kind="ExternalInput").ap()
    h_d = nc.dram_tensor("h", [T, D], F32, kind="ExternalInput").ap()
    hT_d = nc.dram_tensor("hT", [D, T], F32, kind="ExternalInput").ap()
    msc1_d = nc.dram_tensor("msc1", [128, 16, nt, 2], F32, kind="ExternalInput").ap()
    msc2_d = nc.dram_tensor("msc2", [128, 16, nt, 2], F32, kind="ExternalInput").ap()
    g1_d = nc.dram_tensor("g1", [nt, D], F32, kind="ExternalInput").ap()
    g2_d = nc.dram_tensor("g2", [nt, D], F32, kind="ExternalInput").ap()
    ln_d = nc.dram_tensor("lnrows", [4, D], F32, kind="ExternalInput").ap()
    wg_d = nc.dram_tensor("wg", [D, 4096], F32, kind="ExternalInput").ap()
    wo_d = nc.dram_tensor("wo", [4096, D], F32, kind="ExternalInput").ap()
    w1_d = nc.dram_tensor("w1", [D, 2 * DFF], F32, kind="ExternalInput").ap()
    w2_d = nc.dram_tensor("w2", [DFF, D], F32, kind="ExternalInput").ap()
    nw_d = nc.dram_tensor("nw", [1, 128], F32, kind="ExternalInput").ap()
    id_d = nc.dram_tensor("identf", [128, 128], F32, kind="ExternalInput").ap()
    out_d = nc.dram_tensor("out", [T, D], F32, kind="ExternalOutput").ap()
    P = Prog(nc)
    uT = P.sbuf([128, 16, T], BF16); r_uT = P.res()
    big = P.sbuf([128, 11, T], BF16); r_big = P.res()
    hh = P.sbuf([128, nt, D], F32); r_hh = [P.res() for _ in range(nt)]
    W = [P.sbuf([128, 16, 512], BF16) for _ in range(2)]; r_W = [P.res() for _ in range(2)]
    Wgu = [W[0][:, :, 0:256], W[0][:, :, 256:512]]; r_Wg = [P.res() for _ in range(2)]; r_Wu = [P.res() for _ in range(2)]
    msc1 = P.sbuf([128, 16, nt, 2], F32); r_m1 = P.res()
    msc2 = P.sbuf([128, 16, nt, 2], F32); r_m2 = P.res()
    identf = P.sbuf([128, 128], F32); r_idf = P.res()
    identb = P.sbuf([128, 128], BF16); r_idb = P.res()
    nw = P.sbuf([128, 128], F32); r_nw = P.res()
    P.dma("sp", msc1[:], msc1_d, writes=[r_m1])
    P.dma("sp", msc2[:], msc2_d, writes=[r_m2])
    P.dma("sp", identf[:], id_d, writes=[r_idf])
    P.dma("sp", nw[:], nw_d.partition_broadcast(128), writes=[r_nw])
    P.op("dve", lambda e: e.tensor_copy(out=identb[:], in_=identf[:]), reads=[r_idf], writes=[r_idb])
    P.op("dve", lambda e: e.tensor_scalar_add(out=msc1[:, :, :, 0:1], in0=msc1[:, :, :, 0:1], scalar1=1.0), reads=[r_m1], writes=[r_m1])
    P.op("dve", lambda e: e.tensor_scalar_add(out=msc2[:, :, :, 0:1], in0=msc2[:, :, :, 0:1], scalar1=1.0), reads=[r_m2], writes=[r_m2])
    for t in range(nt):
        P.dma(["sp", "act"][t % 2], hh[:, t, :], h_d[t * 128:(t + 1) * 128, :], writes=[r_hh[t]])
    stg = P.sbuf([128, T], F32); r_stg = P.res()
    for kc in range(16):
        P.dma("sp", stg[:], hT_d[kc * 128:(kc + 1) * 128, :], writes=[r_stg])
        for t in range(nt):
            P.op("act", lambda e, kc=kc, t=t: e.activation(out=uT[:, kc, t * 128:(t + 1) * 128], in_=stg[:, t * 128:(t + 1) * 128], func=AF.Identity,
                                                           scale=msc1[:, kc, t, 0:1], bias=msc1[:, kc, t, 1:2]), reads=[r_stg, r_m1], writes=[r_uT])
    ps_m = [P.psum([128, 512], F32) for _ in range(2)]; r_psm = [P.res() for _ in range(2)]
    ps_u = [P.psum([128, 512], F32) for _ in range(2)]; r_psu = [P.res() for _ in range(2)]
    ps_tr = P.psum([128, 1024], BF16); r_pstr = P.res()
    ps_tf = [P.psum([128, 512], F32) for _ in range(2)]; r_pstf = [P.res() for _ in range(2)]
    class Rot:
        def __init__(self, n, dt):
            self.items = [(P.sbuf([128, 512], dt), P.res()) for _ in range(n)]
            self.i = 0

        def next(self):
            it = self.items[self.i % len(self.items)]
            self.i += 1
            return it
    rot_sg = Rot(2, F32); rot_of = Rot(2, F32); rot_ob = Rot(2, F32); rot_xn = Rot(2, F32); rot_og = Rot(2, BF16)
    rot_gbc = Rot(3, F32); rot_tmp = Rot(2, F32)
    stats = P.sbuf([128, 4, 6], F32); mv = P.sbuf([128, 2], F32); rstd = P.sbuf([128, 4], F32); r_s = P.res()
    lg_t = P.sbuf([128, 512], F32); lb_t = P.sbuf([128, 512], F32); r_lg = P.res(); r_lb = P.res()
    eps5 = P.sbuf([128, 2], F32)
    P.op("dve", lambda e: e.memset(eps5[:, 0:1], LN_EPS), writes=[r_s])
    P.op("dve", lambda e: e.memset(eps5[:, 1:2], 1e-6), writes=[r_s])
    ln_scr = (stats, mv, rstd, lg_t, lb_t, r_s, r_lg, r_lb, eps5)
    mmi = [0]

    def accum_h(t, cs, ps, r_ps, g_d, first):
        gbc, r_gbc = rot_gbc.next(); tmp, r_tmp = rot_tmp.next()
        P.dma("sp", gbc[:], g_d[t:t + 1, cs].partition_broadcast(128), writes=[r_gbc])
        P.op("dve", lambda e: e.tensor_tensor(out=tmp[:], in0=ps[:], in1=gbc[:], op=ALU.mult), reads=[r_ps, r_gbc], writes=[r_tmp])
        P.op("dve", lambda e: e.scalar_tensor_tensor(out=hh[:, t, cs], in0=hh[:, t, cs], scalar=(ALPHA if first else 1.0), in1=tmp[:],
                                                     op0=ALU.mult, op1=ALU.add), reads=[r_tmp, r_hh[t]], writes=[r_hh[t]])

    for kq in range(4):
        for cgi in range(2):
            cg = kq * 2 + cgi
            cols = slice(cg * 512, (cg + 1) * 512)
            wb = mmi[0] % 2; mmi[0] += 1
            P.dma("pool", W[wb][:], wg_d[:, cols].rearrange("(kc p) n -> p kc n", p=128), writes=[r_W[wb]])
            for t in range(nt):
                pb = t % 2
                ts_ = slice(t * 128, (t + 1) * 128)
                for kc in range(16):
                    P.op("pe", lambda e, kc=kc, pb=pb, wb=wb, ts_=ts_: e.matmul(ps_m[pb][:], lhsT=uT[:, kc, ts_], rhs=W[wb][:, kc, :],
                                                                          start=(kc == 0), stop=(kc == 15)), reads=[r_uT, r_W[wb]], writes=[r_psm[pb]], inc=(kc == 15))
                sg, r_sg = rot_sg.next(); oft, r_of = rot_of.next(); obt, r_ob = rot_ob.next(); xn, r_xn = rot_xn.next(); og, r_og = rot_og.next()
                P.op("act", lambda e, pb=pb, sg=sg, oft=oft, obt=obt, xn=xn, og=og: e.activation(out=sg[:], in_=ps_m[pb][:], func=AF.Silu), reads=[r_psm[pb]], writes=[r_sg])
                P.dma("sp", oft[:], of_d[ts_, cols], writes=[r_of])
                P.dma("act", obt[:], ob_d[ts_, cols], writes=[r_ob])
                P.op("pool", lambda e, sg=sg, oft=oft, obt=obt, xn=xn, og=og: e.tensor_tensor(out=oft[:], in0=oft[:], in1=obt[:], op=ALU.add), reads=[r_of, r_ob], writes=[r_of])
                if layer == 0:
                    P.op("dve", lambda e, sg=sg, oft=oft, obt=obt, xn=xn, og=og: e.bn_stats(out=stats[:, 0, :], in_=oft[:]), reads=[r_of], writes=[r_s])
                    P.op("dve", lambda e, sg=sg, oft=oft, obt=obt, xn=xn, og=og: e.bn_aggr(out=mv[:], in_=stats[:, 0, :]), reads=[r_s], writes=[r_s])
                    emit_rsqrt(P, rstd[:, 0:1], mv[:, 1:2], 1.0, eps5[:, 0:1], r_s)
                    P.op("dve", lambda e, sg=sg, oft=oft, obt=obt, xn=xn, og=og: e.tensor_scalar(out=xn[:], in0=oft[:], scalar1=mv[:, 0:1], scalar2=rstd[:, 0:1], op0=ALU.subtract, op1=ALU.mult),
                         reads=[r_s, r_of], writes=[r_xn])
                else:
                    P.op("pool", lambda e, sg=sg, oft=oft, obt=obt, xn=xn, og=og: e.tensor_tensor(out=xn[:], in0=oft[:], in1=oft[:], op=ALU.mult), reads=[r_of], writes=[r_xn])
                    P.op("dve", lambda e, sg=sg, oft=oft, obt=obt, xn=xn, og=og: e.tensor_reduce(out=rstd[:], in_=xn[:].rearrange("p (a b) -> p a b", a=4), axis=AX.X, op=ALU.add),
                         reads=[r_xn], writes=[r_s])
                    emit_rsqrt(P, rstd[:], rstd[:], 1.0 / 128.0, eps5[:, 1:2], r_s)
                    for j in range(4):
                        P.op("dve", lambda e, j=j, sg=sg, oft=oft, obt=obt, xn=xn, og=og: e.scalar_tensor_tensor(out=xn[:, j * 128:(j + 1) * 128], in0=oft[:, j * 128:(j + 1) * 128],
                                                                          scalar=rstd[:, j:j + 1], in1=nw[:], op0=ALU.mult, op1=ALU.mult),
                             reads=[r_s, r_of, r_nw, r_xn], writes=[r_xn])
                P.op("dve", lambda e, sg=sg, oft=oft, obt=obt, xn=xn, og=og: e.tensor_tensor(out=og[:], in0=xn[:], in1=sg[:], op=ALU.mult), reads=[r_xn, r_sg], writes=[r_og])
                for j in range(4):
                    P.op("pe", lambda e, j=j, sg=sg, oft=oft, obt=obt, xn=xn, og=og: e.transpose(ps_tr[:, j * 128:(j + 1) * 128], og[:, j * 128:(j + 1) * 128], identb[:]),
                         reads=[r_og, r_idb], writes=[r_pstr], inc=(j == 3))
                P.op("act", lambda e, cgi=cgi, ts_=ts_, sg=sg, oft=oft, obt=obt, xn=xn, og=og: e.activation(out=big[:, cgi * 4:(cgi + 1) * 4, ts_],
                                                                    in_=ps_tr[:, :512].rearrange("p (a b) -> p a b", a=4), func=AF.Copy),
                     reads=[r_pstr], writes=[r_big])
        for cg2 in range(4):
            cs = slice(cg2 * 512, (cg2 + 1) * 512)
            wb = mmi[0] % 2; mmi[0] += 1
            P.dma("pool", W[wb][:, 0:8, :], wo_d[kq * 1024:(kq + 1) * 1024, cs].rearrange("(kc p) n -> p kc n", p=128), writes=[r_W[wb]])
            for t in range(nt):
                pb = t % 2
                ts_ = slice(t * 128, (t + 1) * 128)
                for kc in range(8):
                    P.op("pe", lambda e, kc=kc, pb=pb, wb=wb, ts_=ts_: e.matmul(ps_u[pb][:], lhsT=big[:, kc, ts_], rhs=W[wb][:, kc, :],
                                                                          start=(kc == 0), stop=(kc == 7)), reads=[r_big, r_W[wb]], writes=[r_psu[pb]], inc=(kc == 7))
                accum_h(t, cs, ps_u[pb], r_psu[pb], g1_d, kq == 0)
    for t in range(nt):
        emit_ln(P, hh, r_hh, t, ln_d, 0, 1, ln_scr)
        ts_ = slice(t * 128, (t + 1) * 128)
        for kg in range(4):
            pb = kg % 2
            for j in range(4):
                kc = kg * 4 + j
                P.op("pe", lambda e, kc=kc, j=j, pb=pb, t=t: e.transpose(ps_tf[pb][:, j * 128:(j + 1) * 128], hh[:, t, kc * 128:(kc + 1) * 128], identf[:]),
                     reads=[r_hh[t], r_idf], writes=[r_pstf[pb]], inc=(j == 3))
            for j in range(4):
                kc = kg * 4 + j
                P.op("act", lambda e, kc=kc, j=j, pb=pb, t=t, ts_=ts_: e.activation(out=uT[:, kc, ts_], in_=ps_tf[pb][:, j * 128:(j + 1) * 128], func=AF.Identity,
                                                                               scale=msc2[:, kc, t, 0:1], bias=msc2[:, kc, t, 1:2]),
                     reads=[r_pstf[pb], r_m2], writes=[r_uT])
    P.op("dve", lambda e: e.memset(mv[:], 0.0), reads=[r_W[0], r_W[1]], writes=[r_Wg[0], r_Wg[1], r_Wu[0], r_Wu[1], r_s])
    for fq in range(4):
        for fi in range(11):
            f = fq * 11 + fi
            wb = f % 2
            P.dma("pool", Wgu[wb][:, :, 0:128], w1_d[:, f * 128:(f + 1) * 128].rearrange("(kc p) n -> p kc n", p=128), writes=[r_Wg[wb]])
            P.dma("pool", Wgu[wb][:, :, 128:256], w1_d[:, DFF + f * 128:DFF + (f + 1) * 128].rearrange("(kc p) n -> p kc n", p=128), writes=[r_Wu[wb]])
            for gi_, (t0, t1) in enumerate(tgs):
                n = t1 - t0
                pb = gi_ % 2
                for kc in range(16):
                    P.op("pe", lambda e, kc=kc, pb=pb, wb=wb, t0=t0, t1=t1, n=n: e.matmul(ps_m[pb][:, :n], lhsT=Wgu[wb][:, kc, 0:128], rhs=uT[:, kc, t0:t1],
                                                                                    start=(kc == 0), stop=(kc == 15)), reads=[r_Wg[wb], r_uT], writes=[r_psm[pb]], inc=(kc == 15))
                for kc in range(16):
                    P.op("pe", lambda e, kc=kc, pb=pb, wb=wb, t0=t0, t1=t1, n=n: e.matmul(ps_u[pb][:, :n], lhsT=Wgu[wb][:, kc, 128:256], rhs=uT[:, kc, t0:t1],
                                                                                    start=(kc == 0), stop=(kc == 15)), reads=[r_Wu[wb], r_uT], writes=[r_psu[pb]], inc=(kc == 15))
                sg, r_sg = rot_sg.next()
                P.op("act", lambda e, pb=pb, n=n, sg=sg: e.activation(out=sg[:, :n], in_=ps_m[pb][:, :n], func=AF.Silu), reads=[r_psm[pb]], writes=[r_sg])
                P.op("dve", lambda e, pb=pb, n=n, fi=fi, t0=t0, t1=t1, sg=sg: e.tensor_tensor(out=big[:, fi, t0:t1], in0=ps_u[pb][:, :n], in1=sg[:, :n], op=ALU.mult),
                     reads=[r_psu[pb], r_sg], writes=[r_big])
        for cg2 in range(4):
            cs = slice(cg2 * 512, (cg2 + 1) * 512)
            wb = 1
            P.dma("pool", W[wb][:, 0:11, :], w2_d[fq * 1408:(fq + 1) * 1408, cs].rearrange("(kc p) n -> p kc n", p=128), writes=[r_W[wb]])
            for t in range(nt):
                pb = t % 2
                ts_ = slice(t * 128, (t + 1) * 128)
                for kc in range(11):
                    P.op("pe", lambda e, kc=kc, pb=pb, wb=wb, ts_=ts_: e.matmul(ps_tf[pb][:], lhsT=big[:, kc, ts_], rhs=W[wb][:, kc, :],
                                                                          start=(kc == 0), stop=(kc == 10)), reads=[r_big, r_W[wb]], writes=[r_pstf[pb]], inc=(kc == 10))
                accum_h(t, cs, ps_tf[pb], r_pstf[pb], g2_d, fq == 0)
    for t in range(nt):
        emit_ln(P, hh, r_hh, t, ln_d, 2, 3, ln_scr)
        P.dma("sp", out_d[t * 128:(t + 1) * 128, :], hh[:, t, :], reads=[r_hh[t]])
    P.finish()
    return nc


def run_blk(inputs, mod, layer, o, hcat):
    nc = build_blk(layer)
    T = 1152 if layer == 0 else 1024
    nt = T // 128
    m = mod[layer].reshape(5, 6, D)
    if layer == 0:
        wg = np.ascontiguousarray(inputs["ret_w_in"][0][:, 8192:12288]); wo = inputs["ret_w_out"][0]
        nw = np.ones((1, 128), np.float32)
    else:
        wg = np.ascontiguousarray(inputs["gdn_w_in"][0][:, 8192:12288]); wo = inputs["gdn_w_out"][0]
        nw = np.ascontiguousarray(inputs["gdn_norm"][0][None, :])
    lnrows = np.stack([inputs["ln_g"][layer, 0], inputs["ln_b"][layer, 0], inputs["ln_g"][layer, 1], inputs["ln_b"][layer, 1]], 0)
    in_maps = []
    for core in range(8):
        b, s = core // 2, core % 2
        t0 = s * T + (0 if layer == 0 else NCTX)
        rows = [(4 if (t0 + t * 128) < NCTX else b) for t in range(nt)]

        def msc(js, jh):
            a = np.stack([np.stack([m[r, js], m[r, jh]], -1) for r in rows], 1)
            return np.ascontiguousarray(a.reshape(16, 128, nt, 2).transpose(1, 0, 2, 3))
        h = hcat[b, t0:t0 + T]
        in_maps.append({
            "of": np.ascontiguousarray(o[b, 0, t0:t0 + T].reshape(T, 4096)), "ob": np.ascontiguousarray(o[b, 1, t0:t0 + T].reshape(T, 4096)),
            "h": np.ascontiguousarray(h), "hT": np.ascontiguousarray(h.T),
            "msc1": msc(1, 0), "msc2": msc(4, 3),
            "g1": np.ascontiguousarray(np.stack([m[r, 2] for r in rows], 0)), "g2": np.ascontiguousarray(np.stack([m[r, 5] for r in rows], 0)),
            "lnrows": np.ascontiguousarray(lnrows), "wg": wg, "wo": wo, "w1": inputs["ffn_w_in"][layer], "w2": inputs["ffn_w_out"][layer],
            "nw": nw, "identf": np.eye(128, dtype=np.float32)})
    res = run_bass_kernel_spmd(nc, in_maps, core_ids=list(range(8)))
    if layer == 0:
        out = np.zeros((4, NTOT, D), np.float32)
        for core in range(8):
            b, s = core // 2, core % 2
            out[b, s * T:(s + 1) * T] = res.results[core]["out"]
    else:
        out = np.zeros((4, NLAT, D), np.float32)
        for core in range(8):
            b, s = core // 2, core % 2
            out[b, s * T:(s + 1) * T] = res.results[core]["out"]
    return out


def build_gdn(n_heads=16):
    nc = new_nc()
    T = NTOT
    NCH = T // 128
    hT = nc.dram_tensor("hT", [D, T], F32, kind="ExternalInput").ap()
    msc_d = nc.dram_tensor("msc", [128, 16, 4], F32, kind="ExternalInput").ap()
    wqkv = nc.dram_tensor("wqkv", [D, 8192], F32, kind="ExternalInput").ap()
    wba = nc.dram_tensor("wba", [D, 64], F32, kind="ExternalInput").ap()
    convw_d = nc.dram_tensor("convw", [128, 64, 5], F32, kind="ExternalInput").ap()
    alog_d = nc.dram_tensor("alog", [1, 32], F32, kind="ExternalInput").ap()
    dtb_d = nc.dram_tensor("dtb", [1, 32], F32, kind="ExternalInput").ap()
    id_d = nc.dram_tensor("identf", [128, 128], F32, kind="ExternalInput").ap()
    U_d = nc.dram_tensor("U", [128, 128], F32, kind="ExternalInput").ap()
    msl_d = nc.dram_tensor("msl", [128, 128], F32, kind="ExternalInput").ap()
    o_d = nc.dram_tensor("o", [T, 32, 128], F32, kind="ExternalOutput").ap()
    P = Prog(nc)
    uT = P.sbuf([128, 16, T], BF16); r_uT = P.res()
    emit_uT(P, hT, msc_d, uT, r_uT, T, [(0, NCTX, 0), (NCTX, T, 1)])
    identf = P.sbuf([128, 128], F32); r_idf = P.res()
    identb = P.sbuf([128, 128], BF16); r_idb = P.res()
    U = P.sbuf([128, 128], F32); r_U = P.res()
    msl = P.sbuf([128, 128], F32); r_msl = P.res()
    onesf = P.sbuf([128, 128], F32); r_ones = P.res()
    cst = P.sbuf([128, 2], F32); r_cst = P.res()
    convw = P.sbuf([128, 64, 5], F32); r_cw = P.res()
    alog = P.sbuf([128, 32], F32); r_al = P.res()
    dtb = P.sbuf([128, 32], F32); r_dtb = P.res()
    P.dma("sp", identf[:], id_d, writes=[r_idf])
    P.dma("sp", U[:], U_d, writes=[r_U])
    P.dma("sp", msl[:], msl_d, writes=[r_msl])
    P.dma("sp", convw[:], convw_d, writes=[r_cw])
    P.dma("sp", alog[:], alog_d.partition_broadcast(128), writes=[r_al])
    P.dma("sp", dtb[:], dtb_d.partition_broadcast(128), writes=[r_dtb])
    P.op("dve", lambda e: e.tensor_copy(out=identb[:], in_=identf[:]), reads=[r_idf], writes=[r_idb])
    P.op("dve", lambda e: e.memset(onesf[:], 1.0), writes=[r_ones])
    P.op("dve", lambda e: e.memset(cst[:, 0:1], 1e-6), writes=[r_cst])
    P.op("dve", lambda e: e.memset(cst[:, 1:2], 1.0), writes=[r_cst])
    P.op("act", lambda e: e.activation(out=alog[:], in_=alog[:], func=AF.Exp), reads=[r_al], writes=[r_al])
    bank = [P.psum([128, 512], F32) for _ in range(8)]; r_bk = [P.res(excl=True) for _ in range(8)]
    ps_p = [bank[6], bank[7]]; r_pp = [r_bk[6], r_bk[7]]
    bank_bf = bank[3]
    beta = P.sbuf([128, NCH, 32], F32); r_beta = P.res()
    negb = P.sbuf([128, NCH, 32], F32); r_negb = P.res()
    la = P.sbuf([128, NCH, 32], F32); r_la = P.res()
    wba_t = P.sbuf([128, 16, 64], BF16); r_wba = P.res()
    t32 = P.sbuf([128, 32], F32); r_t32 = P.res()
    P.dma("pool", wba_t[:], wba.rearrange("(kc p) n -> p kc n", p=128), writes=[r_wba])
    for c in range(NCH):
        pb = c % 2
        for kc in range(16):
            P.op("pe", lambda e, c=c, kc=kc, pb=pb: e.matmul(ps_p[pb][:, :64], lhsT=uT[:, kc, c * 128:(c + 1) * 128], rhs=wba_t[:, kc, :],
                                                       start=(kc == 0), stop=(kc == 15)), reads=[r_uT, r_wba], writes=[r_pp[pb]], inc=(kc == 15))
        P.op("act", lambda e, c=c, pb=pb: e.activation(out=beta[:, c, :], in_=ps_p[pb][:, 0:32], func=AF.Sigmoid), reads=[r_pp[pb]], writes=[r_beta])
        P.op("dve", lambda e, pb=pb: e.tensor_tensor(out=t32[:], in0=ps_p[pb][:, 32:64], in1=dtb[:], op=ALU.add), reads=[r_pp[pb], r_dtb], writes=[r_t32])
        P.op("act", lambda e: e.activation(out=t32[:], in_=t32[:], func=AF.Exp), reads=[r_t32], writes=[r_t32])
        P.op("act", lambda e: e.activation(out=t32[:], in_=t32[:], func=AF.Ln, bias=cst[:, 1:2], scale=1.0), reads=[r_t32, r_cst], writes=[r_t32])
        P.op("dve", lambda e, c=c: e.scalar_tensor_tensor(out=la[:, c, :], in0=t32[:], scalar=-1.0, in1=alog[:], op0=ALU.mult, op1=ALU.mult),
             reads=[r_t32, r_al], writes=[r_la])
    P.op("dve", lambda e: e.tensor_scalar_mul(out=negb[:], in0=beta[:], scalar1=-1.0), reads=[r_beta], writes=[r_negb])
    LP = 2 + NCTX + 2 + 2 + NLAT + 2
    xp = P.sbuf([128, LP], F32); r_xp = P.res()
    acc = P.sbuf([128, T], F32); r_acc = P.res()
    ys = P.sbuf([128, T], F32); r_ys = P.res()
    sq = P.sbuf([128, 512], F32); r_sq = P.res()
    rn = P.sbuf([128, 512], F32); r_rn = P.res()
    HS = []
    for _ in range(2):
        HS.append((P.sbuf([128, T], BF16), P.res(), P.sbuf([128, T], BF16), P.res(), P.sbuf([128, NCH, 128], BF16), P.res(),
                   [P.sbuf([128, NCH, 128], BF16) for _ in range(2)], [P.res() for _ in range(2)]))
    Wp = [P.sbuf([128, 16, 128], BF16) for _ in range(2)]; r_Wp = [P.res() for _ in range(2)]
    P.op("pool", lambda e: e.memset(xp[:], 0.0), writes=[r_xp])
    ps_trb = P.psum([128, 1024], BF16) if False else None
    class NS:
        pass

    def f32t():
        return P.sbuf([128, 128], F32), P.res()

    def bf16t():
        return P.sbuf([128, 128], BF16), P.res()
    ST = []
    for s_ in range(2):
        S = NS()
        S.LAU, S.r_LAU = f32t(); S.Zabs, S.r_Z = f32t(); S.decA, S.r_decA = f32t(); S.egR, S.r_egR = f32t()
        S.t1, S.r_t1 = f32t(); S.t2, S.r_t2 = f32t()
        S.Xs = [f32t() for _ in range(2)]; S.Ys = [f32t() for _ in range(2)]; S.Ts = [f32t() for _ in range(2)]
        S.Ttb, S.r_Ttb = bf16t()
        S.cols = P.sbuf([128, 8], F32); S.r_cols = P.res()
        S.qdT, S.r_qdT = bf16t(); S.kbg, S.r_kbg = bf16t(); S.kd, S.r_kd = bf16t(); S.nwT, S.r_nwT = bf16t()
        S.u_sb, S.r_u = bf16t(); S.QKT, S.r_QKT = bf16t()
        S.o_sb = [P.sbuf([128, 128], F32) for _ in range(2)]; S.r_osb = [P.res() for _ in range(2)]
        S.m, S.r_m = f32t(); S.m_bf, S.r_mbf = bf16t()
        S.b = bank[3 * s_:3 * s_ + 3]; S.rb = r_bk[3 * s_:3 * s_ + 3]
        ST.append(S)
    groups = [(0, 256), (256, 768), (768, 1280), (1280, 1792), (1792, 2304)]
    segs = [(2, 0, NCTX), (2 + NCTX + 4, NCTX, NLAT)]
    B = ps_p; RB = r_pp
    wi = [0]

    def project_conv_silu(g):
        wb = wi[0] % 2; wi[0] += 1
        P.dma("pool", Wp[wb][:], wqkv[:, g * 128:(g + 1) * 128].rearrange("(kc p) n -> p kc n", p=128), writes=[r_Wp[wb]])
        for gi, (t0, t1) in enumerate(groups):
            n = t1 - t0; pb = gi % 2
            for kc in range(16):
                P.op("pe", lambda e, kc=kc, pb=pb, wb=wb, t0=t0, t1=t1, n=n: e.matmul(ps_p[pb][:, :n], lhsT=Wp[wb][:, kc, :], rhs=uT[:, kc, t0:t1],
                                                                                start=(kc == 0), stop=(kc == 15)), reads=[r_Wp[wb], r_uT], writes=[r_pp[pb]], inc=(kc == 15))
            yield
            off = 2 + t0 if t0 < NCTX else 2 + NCTX + 4 + (t0 - NCTX)
            P.op("act", lambda e, pb=pb, n=n, off=off: e.activation(out=xp[:, off:off + n], in_=ps_p[pb][:, :n], func=AF.Copy), reads=[r_pp[pb]], writes=[r_xp])
            yield
        for (xo, to, L) in segs:
            P.op("dve", lambda e, xo=xo, to=to, L=L, g=g: e.tensor_scalar_mul(out=acc[:, to:to + L], in0=xp[:, xo - 2:xo - 2 + L], scalar1=convw[:, g, 0:1]),
                 reads=[r_xp, r_cw], writes=[r_acc])
            yield
            for j in range(1, 5):
                P.op("dve", lambda e, xo=xo, to=to, L=L, g=g, j=j: e.scalar_tensor_tensor(out=acc[:, to:to + L], in0=xp[:, xo - 2 + j:xo - 2 + j + L], scalar=convw[:, g, j:j + 1],
                                                                                        in1=acc[:, to:to + L], op0=ALU.mult, op1=ALU.add), reads=[r_xp, r_cw, r_acc], writes=[r_acc])
                yield
        P.op("act", lambda e: e.activation(out=ys[:], in_=acc[:], func=AF.Silu), reads=[r_acc], writes=[r_ys])
        yield

    def l2norm_to(dst, r_dst):
        for gi, (t0, t1) in enumerate(groups):
            n = t1 - t0; pb = gi % 2
            P.op("pool", lambda e, t0=t0, t1=t1, n=n: e.tensor_tensor(out=sq[:, :n], in0=ys[:, t0:t1], in1=ys[:, t0:t1], op=ALU.mult), reads=[r_ys], writes=[r_sq])
            P.op("pe", lambda e, pb=pb, n=n: e.matmul(ps_p[pb][:, :n], lhsT=onesf[:], rhs=sq[:, :n], start=True, stop=True), reads=[r_ones, r_sq], writes=[r_pp[pb]])
            P.op("act", lambda e, pb=pb, n=n: e.activation(out=rn[:, :n], in_=ps_p[pb][:, :n], func=AF.Sqrt, bias=cst[:, 0:1], scale=1.0), reads=[r_pp[pb], r_cst], writes=[r_rn])
            P.op("dve", lambda e, n=n: e.reciprocal(out=rn[:, :n], in_=rn[:, :n]), reads=[r_rn], writes=[r_rn])
            P.op("dve", lambda e, t0=t0, t1=t1, n=n: e.tensor_tensor(out=dst[:, t0:t1], in0=ys[:, t0:t1], in1=rn[:, :n], op=ALU.mult), reads=[r_ys, r_rn], writes=[r_dst])
            yield

    def prep(hq):
        qT, r_qT, kT, r_kT, kTok, r_kTok, vb, r_vb = HS[hq % 2]
        yield from project_conv_silu(hq); yield from l2norm_to(qT, r_qT)
        yield from project_conv_silu(16 + hq); yield from l2norm_to(kT, r_kT)
        for c in range(NCH):
            pb = c % 2
            P.op("pe", lambda e, c=c, pb=pb: e.matmul(B[pb][:, :128], lhsT=kT[:, c * 128:(c + 1) * 128], rhs=identb[:], start=True, stop=True), reads=[r_kT, r_idb], writes=[RB[pb]])
            P.op("act", lambda e, c=c, pb=pb: e.activation(out=kTok[:, c, :], in_=B[pb][:, :128], func=AF.Copy), reads=[RB[pb]], writes=[r_kTok])
            yield
        for e_ in range(2):
            hv = 2 * hq + e_
            yield from project_conv_silu(32 + hv)
            for c in range(NCH):
                pb = c % 2
                P.op("pe", lambda e, c=c, pb=pb: e.matmul(B[pb][:, :128], lhsT=ys[:, c * 128:(c + 1) * 128], rhs=identf[:], start=True, stop=True), reads=[r_ys, r_idf], writes=[RB[pb]])
                P.op("act", lambda e, c=c, pb=pb, e_=e_, hv=hv: e.activation(out=vb[e_][:, c, :], in_=B[pb][:, :128], func=AF.Identity, scale=beta[:, c, hv:hv + 1]),
                     reads=[RB[pb], r_beta], writes=[r_vb[e_]])
                yield

    for _ in prep(0):
        pass
    for hq in range(n_heads):
        gp = prep(hq + 1) if hq + 1 < n_heads else iter(())
        def unit(e_, c, S):
            hv = 2 * hq + e_
            cs = slice(c * 128, (c + 1) * 128)
            ob = c % 2
            b, rb, cols, r_cols = S.b, S.rb, S.cols, S.r_cols
            R_ = b[0][:, 0:128]; G_ = b[1][:, 128:256]; Y2_ = b[1][:, 256:384]; M_ = b[1][:, 384:512]
            TT_ = b[2][:, 0:128]; UU_ = b[2][:, 128:256]; X2_ = b[0][:, 128:256]; W_ = b[0][:, 256:384]
            qT, r_qT, kT, r_kT, kTok, r_kTok, vb, r_vb = HS[hq % 2]
            P.op("dve", lambda e: e.tensor_scalar_mul(out=S.LAU[:], in0=U[:], scalar1=la[:, c, hv:hv + 1]), reads=[r_U, r_la], writes=[S.r_LAU]); yield
            P.op("pe", lambda e: e.matmul(R_, lhsT=onesf[:], rhs=S.LAU[:], start=True, stop=True), reads=[r_ones, S.r_LAU], writes=[rb[0]]); yield
            P.op("pe", lambda e: e.matmul(b[1][:, 0:128], lhsT=S.LAU[:], rhs=onesf[:], start=True, stop=True), reads=[r_ones, S.r_LAU], writes=[rb[1]]); yield
            P.op("act", lambda e: e.activation(out=cols[:, 0:1], in_=b[1][:, 0:1], func=AF.Copy), reads=[rb[1]], writes=[r_cols]); yield
            P.op("act", lambda e: e.activation(out=S.Zabs[:], in_=R_, func=AF.Abs, bias=cols[:, 0:1], scale=-1.0), reads=[rb[0], r_cols], writes=[S.r_Z]); yield
            P.op("act", lambda e: e.activation(out=S.decA[:], in_=S.Zabs[:], func=AF.Exp, scale=-1.0), reads=[S.r_Z], writes=[S.r_decA]); yield
            P.op("dve", lambda e: e.tensor_scalar(out=cols[:, 3:4], in0=b[0][:, 127:128], scalar1=cols[:, 0:1], scalar2=None, op0=ALU.subtract),
                 reads=[rb[0], r_cols], writes=[r_cols]); yield
            P.op("act", lambda e: e.activation(out=cols[:, 1:2], in_=cols[:, 0:1], func=AF.Exp), reads=[r_cols], writes=[r_cols]); yield
            P.op("act", lambda e: e.activation(out=cols[:, 2:3], in_=b[0][:, 127:128], func=AF.Exp), reads=[rb[0], r_cols], writes=[r_cols]); yield
            P.op("act", lambda e: e.activation(out=cols[:, 3:4], in_=cols[:, 3:4], func=AF.Exp), reads=[r_cols], writes=[r_cols]); yield
            P.op("act", lambda e: e.activation(out=S.egR[:], in_=R_, func=AF.Exp), reads=[rb[0]], writes=[S.r_egR]); yield
            P.op("dve", lambda e: e.tensor_tensor(out=S.qdT[:], in0=qT[:, cs], in1=S.egR[:], op=ALU.mult), reads=[r_qT, S.r_egR], writes=[S.r_qdT]); yield
            P.op("dve", lambda e: e.tensor_tensor(out=cols[:, 4:5], in0=cols[:, 1:2], in1=beta[:, c, hv:hv + 1], op=ALU.mult), reads=[r_cols, r_beta], writes=[r_cols]); yield
            P.op("pool", lambda e: e.tensor_tensor(out=S.t1[:], in0=S.decA[:], in1=msl[:], op=ALU.mult), reads=[S.r_decA, r_msl], writes=[S.r_t1]); yield
            P.op("pool", lambda e: e.tensor_tensor(out=S.t2[:], in0=S.decA[:], in1=U[:], op=ALU.mult), reads=[S.r_decA, r_U], writes=[S.r_t2]); yield
            X0, rX0 = S.Xs[0]; Y0, rY0 = S.Ys[0]; T0, rT0 = S.Ts[0]
            P.op("pe", lambda e: e.matmul(G_, lhsT=kT[:, cs], rhs=kT[:, cs], start=True, stop=True), reads=[r_kT], writes=[rb[1]]); yield
            P.op("dve", lambda e: e.scalar_tensor_tensor(out=X0[:], in0=G_, scalar=negb[:, c, hv:hv + 1], in1=S.t1[:], op0=ALU.mult, op1=ALU.mult),
                 reads=[rb[1], r_negb, S.r_t1], writes=[rX0]); yield
            P.op("pe", lambda e: e.matmul(TT_, lhsT=X0[:], rhs=identf[:], start=True, stop=True), reads=[rX0, r_idf], writes=[rb[2]]); yield
            P.op("act", lambda e: e.activation(out=Y0[:], in_=TT_, func=AF.Copy), reads=[rb[2]], writes=[rY0]); yield
            P.op("dve", lambda e: e.tensor_tensor(out=T0[:], in0=TT_, in1=identf[:], op=ALU.add), reads=[rb[2], r_idf], writes=[rT0]); yield
            cur = 0
            for lvl in range(1, 7):
                Xc, rXc = S.Xs[cur]; Yc, rYc = S.Ys[cur]; Tc, rTc = S.Ts[cur]
                Xn, rXn = S.Xs[1 - cur]; Yn, rYn = S.Ys[1 - cur]; Tn, rTn = S.Ts[1 - cur]
                last = (lvl == 6)
                P.op("pe", lambda e, Xc=Xc, Yc=Yc: e.matmul(X2_, lhsT=Yc[:], rhs=Xc[:], start=True, stop=True), reads=[rXc, rYc], writes=[rb[0]]); yield
                if not last:
                    P.op("pe", lambda e, Xc=Xc, Yc=Yc: e.matmul(Y2_, lhsT=Xc[:], rhs=Yc[:], start=True, stop=True), reads=[rXc, rYc], writes=[rb[1]]); yield
                P.op("act", lambda e, Xn=Xn: e.activation(out=Xn[:], in_=X2_, func=AF.Copy), reads=[rb[0]], writes=[rXn]); yield
                if not last:
                    P.op("dve", lambda e, Yn=Yn: e.tensor_copy(out=Yn[:], in_=Y2_), reads=[rb[1]], writes=[rYn]); yield
                P.op("pe", lambda e, Xn=Xn, Tc=Tc: e.matmul(TT_, lhsT=Xn[:], rhs=Tc[:], start=True, stop=False), reads=[rXn, rTc], writes=[rb[2]], inc=False)
                P.op("pe", lambda e, Tc=Tc: e.matmul(TT_, lhsT=identf[:], rhs=Tc[:], start=False, stop=True), reads=[r_idf, rTc], writes=[rb[2]]); yield
                if not last:
                    P.op("dve", lambda e, Tn=Tn: e.tensor_copy(out=Tn[:], in_=TT_), reads=[rb[2]], writes=[rTn]); yield
                else:
                    P.op("dve", lambda e: e.tensor_copy(out=S.Ttb[:], in_=TT_), reads=[rb[2]], writes=[S.r_Ttb]); yield
                cur = 1 - cur
            P.op("act", lambda e: e.activation(out=S.kbg[:], in_=kTok[:, c, :], func=AF.Identity, scale=cols[:, 4:5]), reads=[r_kTok, r_cols], writes=[S.r_kbg]); yield
            P.op("act", lambda e: e.activation(out=S.kd[:], in_=kTok[:, c, :], func=AF.Identity, scale=cols[:, 3:4]), reads=[r_kTok, r_cols], writes=[S.r_kd]); yield
            P.op("pe", lambda e: e.matmul(W_, lhsT=S.kbg[:], rhs=S.Ttb[:], start=True, stop=True), reads=[S.r_kbg, S.r_Ttb], writes=[rb[0]]); yield
            P.op("act", lambda e: e.activation(out=S.nwT[:], in_=W_, func=AF.Identity, scale=-1.0), reads=[rb[0]], writes=[S.r_nwT]); yield
            if c > 0:
                P.op("pe", lambda e: e.matmul(UU_, lhsT=S.Ttb[:], rhs=vb[e_][:, c, :], start=True, stop=False), reads=[S.r_Ttb, r_vb[e_]], writes=[rb[2]], inc=False)
                P.op("pe", lambda e: e.matmul(UU_, lhsT=S.nwT[:], rhs=S.m_bf[:], start=False, stop=True), reads=[S.r_nwT, S.r_mbf], writes=[rb[2]]); yield
            else:
                P.op("pe", lambda e: e.matmul(UU_, lhsT=S.Ttb[:], rhs=vb[e_][:, c, :], start=True, stop=True), reads=[S.r_Ttb, r_vb[e_]], writes=[rb[2]]); yield
            P.op("act", lambda e: e.activation(out=S.u_sb[:], in_=UU_, func=AF.Copy), reads=[rb[2]], writes=[S.r_u]); yield
            P.op("pe", lambda e: e.matmul(G_, lhsT=kT[:, cs], rhs=qT[:, cs], start=True, stop=True), reads=[r_kT, r_qT], writes=[rb[1]]); yield
            P.op("dve", lambda e: e.tensor_tensor(out=S.QKT[:], in0=G_, in1=S.t2[:], op=ALU.mult), reads=[rb[1], S.r_t2], writes=[S.r_QKT]); yield
            if c > 0:
                P.op("pe", lambda e: e.matmul(R_, lhsT=S.qdT[:], rhs=S.m_bf[:], start=True, stop=False), reads=[S.r_qdT, S.r_mbf], writes=[rb[0]], inc=False)
            P.op("pe", lambda e: e.matmul(R_, lhsT=S.QKT[:], rhs=S.u_sb[:], start=(c == 0), stop=True), reads=[S.r_QKT, S.r_u], writes=[rb[0]]); yield
            P.op("act", lambda e: e.activation(out=S.o_sb[ob][:], in_=R_, func=AF.Identity, scale=128.0 ** -0.5), reads=[rb[0]], writes=[S.r_osb[ob]]); yield
            P.dma("sp", o_d[c * 128:(c + 1) * 128, hv, :], S.o_sb[ob][:], reads=[S.r_osb[ob]]); yield
            if c < NCH - 1:
                P.op("pe", lambda e: e.matmul(M_, lhsT=S.kd[:], rhs=S.u_sb[:], start=True, stop=True), reads=[S.r_kd, S.r_u], writes=[rb[1]]); yield
                if c == 0:
                    P.op("dve", lambda e: e.tensor_copy(out=S.m[:], in_=M_), reads=[rb[1]], writes=[S.r_m]); yield
                else:
                    P.op("dve", lambda e: e.scalar_tensor_tensor(out=S.m[:], in0=S.m[:], scalar=cols[:, 2:3], in1=M_, op0=ALU.mult, op1=ALU.add),
                         reads=[S.r_m, r_cols, rb[1]], writes=[S.r_m]); yield
                P.op("pool", lambda e: e.tensor_copy(out=S.m_bf[:], in_=S.m[:]), reads=[S.r_m], writes=[S.r_mbf]); yield

        for c in range(NCH):
            g0 = unit(0, c, ST[0]); g1 = unit(1, c, ST[1])
            d0 = d1 = False
            while not (d0 and d1):
                if not d0:
                    try:
                        next(g0)
                    except StopIteration:
                        d0 = True
                if not d1:
                    try:
                        next(g1)
                    except StopIteration:
                        d1 = True
                next(gp, None)
        for _ in gp:
            pass
    P.finish()
    return nc


def run_gdn(inputs, mod, hcat, n_heads=16):
    nc = build_gdn(n_heads)
    w = inputs["gdn_w_in"][0]
    wqkv = np.ascontiguousarray(w[:, 0:8192])
    kk = np.arange(128)[:, None]; ii = np.arange(128)[None, :]
    U = (kk <= ii).astype(np.float32)
    msl = (kk > ii).astype(np.float32)
    in_maps = []
    for core in range(8):
        b, d = core // 2, core % 2
        h = hcat[b]
        if d:
            h = _seg_flip_np(h)
        cw = inputs["gdn_conv"][0]
        if d:
            cw = cw[::-1]
        convw = np.ascontiguousarray(cw.T.reshape(64, 128, 5).transpose(1, 0, 2))
        wba = np.ascontiguousarray(np.concatenate([w[:, 12288 + d * 32:12288 + (d + 1) * 32], w[:, 12352 + d * 32:12352 + (d + 1) * 32]], 1))
        in_maps.append({
            "hT": np.ascontiguousarray(h.T), "msc": _msc(mod, 1, b, 1, 0), "wqkv": wqkv, "wba": wba, "convw": convw,
            "alog": np.ascontiguousarray(inputs["gdn_a_log"][0, d][None, :]), "dtb": np.ascontiguousarray(inputs["gdn_dt_bias"][0, d][None, :]),
            "identf": np.eye(128, dtype=np.float32), "U": U, "msl": msl})
    res = run_bass_kernel_spmd(nc, in_maps, core_ids=list(range(8)))
    out = np.zeros((4, 2, NTOT, 32, 128), np.float32)
    for core in range(8):
        b, d = core // 2, core % 2
        o = res.results[core]["o"]
        out[b, d] = _seg_flip_np(o) if d else o
    return out


def kernel(**inputs):
    inputs = {k: np.asarray(v, dtype=np.float32) for k, v in inputs.items()}
    mod = run_ada(inputs)
    hcat = np.concatenate([inputs["ctx"], inputs["x"]], axis=1)
    o = run_ret(inputs, mod, hcat)
    h1 = run_blk(inputs, mod, 0, o.reshape(4, 2, NTOT, 4096), hcat)
    o2 = run_gdn(inputs, mod, h1)
    out = run_blk(inputs, mod, 1, o2.reshape(4, 2, NTOT, 4096), h1)
    return out.astype(np.float32)
```
